# Optimizing a Trainium2 kernel written in Bass

```python
import math
import jax, jax.numpy as jnp
from jax import lax
import numpy as np

D_MODEL = 1024
BATCH = 32
SEQ = 256
DEPTH = 2
DEC_BATCH = 8
DEC_SEQ = 2048
PAST_LEN = 512

GRID_W = 64
EPS = 1e-6
QBLK = 128
ROPE_THETA = 10000.0

D_RNN = 512
RNN_BLOCKS = 8
RNN_BLK = D_RNN // RNN_BLOCKS
CONV_W = 4
CONV_PAD_L = 2
CONV_PAD_R = CONV_W - 1 - CONV_PAD_L
RG_C = 8.0

MLA_H = 8
MLA_NOPE = 64
MLA_ROPE = 32
MLA_V = 64
Q_RANK = 384
KV_RANK = 256
MLA_SCALE = (MLA_NOPE + MLA_ROPE) ** -0.5

DIFF_H = 4
DIFF_DK = 64
DIFF_DV = 2 * DIFF_DK
DIFF_SCALE = DIFF_DK ** -0.5

N_BRANCH = 3

IN_SPLITS = (D_RNN, D_RNN,
             Q_RANK, KV_RANK, MLA_ROPE, MLA_H * MLA_V,
             DIFF_H * 2 * DIFF_DK, DIFF_H * 2 * DIFF_DK,
             DIFF_H * DIFF_DV, DIFF_H * DIFF_DV,
             N_BRANCH * D_MODEL)
IN_COLS = sum(IN_SPLITS)

kernel_name = 'hybrid_diffusion_rglru_mla_diffattn_step'


def _rmsnorm(x, g):
    xf = x.astype(jnp.float32)
    y = xf * lax.rsqrt(jnp.mean(xf * xf, axis=-1, keepdims=True) + EPS)
    return (y * g.astype(jnp.float32)).astype(x.dtype)


def _split_cols(z):
    offs = np.cumsum(IN_SPLITS)[:-1].tolist()
    return jnp.split(z, offs, axis=-1)


def _rope_1d(x, pos):
    half = x.shape[-1] // 2
    inv = ROPE_THETA ** (-jnp.arange(half, dtype=jnp.float32) / half)
    ang = pos.astype(jnp.float32)[:, None] * inv[None, :]
    cos = jnp.cos(ang)[:, None, :]
    sin = jnp.sin(ang)[:, None, :]
    xf = x.astype(jnp.float32)
    x1, x2 = xf[..., :half], xf[..., half:]
    return jnp.concatenate([x1 * cos - x2 * sin, x1 * sin + x2 * cos], axis=-1).astype(x.dtype)


def _rope_2d(x, pos_row, pos_col):
    d = x.shape[-1] // 2
    return jnp.concatenate([_rope_1d(x[..., :d], pos_row), _rope_1d(x[..., d:], pos_col)], axis=-1)


def _dwconv(x, w, b):
    s = x.shape[1]
    xp = jnp.pad(x, ((0, 0), (CONV_PAD_L, CONV_PAD_R), (0, 0)))
    y = xp[:, 0:s] * w[0]
    for k in range(1, CONV_W):
        y = y + xp[:, k:k + s] * w[k]
    return y + b


def _lin_scan(a, b, h0, reverse):
    idx = -1 if reverse else 0
    b = b.at[:, idx].add(a[:, idx] * h0)

    def comb(e1, e2):
        a1, b1 = e1
        a2, b2 = e2
        return a1 * a2, a2 * b1 + b2

    _, h = lax.associative_scan(comb, (a, b), reverse=reverse, axis=1)
    return h


def _rglru_dir(xc, w_a, b_a, w_x, b_x, lam, h0, reverse):
    bn, s, _ = xc.shape
    xb = xc.reshape(bn, s, RNN_BLOCKS, RNN_BLK)
    r = jax.nn.sigmoid((jnp.einsum('bsnk,nkj->bsnj', xb, w_a.astype(jnp.float32)).reshape(bn, s, D_RNN)
                        + b_a.astype(jnp.float32)))
    i = jax.nn.sigmoid((jnp.einsum('bsnk,nkj->bsnj', xb, w_x.astype(jnp.float32)).reshape(bn, s, D_RNN)
                        + b_x.astype(jnp.float32)))
    log_a = -RG_C * r * jax.nn.softplus(-lam.astype(jnp.float32))
    a = jnp.exp(log_a)
    b = jnp.sqrt(-jnp.expm1(2.0 * log_a)) * (i * xc)
    return _lin_scan(a, b, h0, reverse)


def _sweep_queries(fn, qs):
    sq = qs[0].shape[1]
    blk = min(QBLK, sq)
    nb = sq // blk
    qb = tuple(jnp.moveaxis(q.reshape(q.shape[0], nb, blk, *q.shape[2:]), 1, 0) for q in qs)
    out = lax.map(lambda t: fn(*t), qb)
    out = jnp.moveaxis(out, 0, 1)
    return out.reshape(out.shape[0], sq, *out.shape[3:])


def _mla_up(ckv_n, w_ukv):
    bn, s, _ = ckv_n.shape
    kv = (ckv_n @ w_ukv).reshape(bn, s, MLA_H, MLA_NOPE + MLA_V)
    return kv[..., :MLA_NOPE], kv[..., MLA_NOPE:]


def _mla_attend(qn, qr, kn, kr, v):
    def blockfn(qn_b, qr_b):
        s = jnp.einsum('bqhd,bkhd->bhqk', qn_b, kn) + jnp.einsum('bqhr,bkr->bhqk', qr_b, kr)
        p = jax.nn.softmax(s.astype(jnp.float32) * MLA_SCALE, axis=-1).astype(v.dtype)
        return jnp.einsum('bhqk,bkhd->bqhd', p, v)
    return _sweep_queries(blockfn, (qn, qr))


def _diff_attend(q, k, v, lam):
    def blockfn(q_b):
        s = jnp.einsum('bqhcd,bkhcd->bchqk', q_b, k).astype(jnp.float32) * DIFF_SCALE
        p = jax.nn.softmax(s, axis=-1)
        pd = p[:, 0] - lam * p[:, 1]
        return jnp.einsum('bhqk,bkhd->bqhd', pd.astype(v.dtype), v)
    return _sweep_queries(blockfn, (q,))


def _layer(l, x, cond, lw, pos, ctx):
    (w_mod, b_mod, g_pre, g_post, w_in, conv_w, conv_b, w_rg_a, b_rg_a, w_rg_x, b_rg_x, rg_lam,
     q_norm, w_uq, kv_norm, w_ukv, lam_q1, lam_k1, lam_q2, lam_k2, diff_norm,
     w_br_rnn, w_br_mla, w_br_diff, w_out) = lw
    bn, s, _ = x.shape
    mod = jax.nn.silu(cond) @ w_mod + b_mod
    if mod.ndim == 2:
        mod = mod[:, None, :]
    shift, scale, gate = jnp.split(mod, 3, axis=-1)
    h = _rmsnorm(x, g_pre) * (1 + scale) + shift
    z = h @ w_in
    rx, rg, cq, ckv, kr, mg, dq, dk, dv, dg, mgate = _split_cols(z)

    xc = _dwconv(rx, conv_w, conv_b).astype(jnp.float32)
    if ctx is None:
        h0f = jnp.zeros((bn, D_RNN), jnp.float32)
        h0b = h0f
    else:
        h0f = ctx[4][:, 0].astype(jnp.float32)
        h0b = ctx[4][:, 1].astype(jnp.float32)
    hf = _rglru_dir(xc, w_rg_a[0], b_rg_a[0], w_rg_x[0], b_rg_x[0], rg_lam[0], h0f, False)
    hb = _rglru_dir(xc, w_rg_a[1], b_rg_a[1], w_rg_x[1], b_rg_x[1], rg_lam[1], h0b, True)
    y_rnn = (hf + hb).astype(x.dtype) * jax.nn.silu(rg)

    q = (_rmsnorm(cq, q_norm) @ w_uq).reshape(bn, s, MLA_H, MLA_NOPE + MLA_ROPE)
    qn, qr = q[..., :MLA_NOPE], q[..., MLA_NOPE:]
    ckv_n = _rmsnorm(ckv, kv_norm)
    kn, vm = _mla_up(ckv_n, w_ukv)
    kr_r = kr
    if pos is not None:
        qr = _rope_2d(qr, pos[0], pos[1])
        kr_r = _rope_2d(kr[:, :, None, :], pos[0], pos[1])[:, :, 0]
    if ctx is not None:
        kn_c, vm_c = _mla_up(ctx[0], w_ukv)
        kn_all = jnp.concatenate([kn_c, kn], axis=1)
        kr_all = jnp.concatenate([ctx[1], kr_r], axis=1)
        vm_all = jnp.concatenate([vm_c, vm], axis=1)
    else:
        kn_all, kr_all, vm_all = kn, kr_r, vm
    y_mla = _mla_attend(qn, qr, kn_all, kr_all, vm_all).reshape(bn, s, MLA_H * MLA_V) * jax.nn.silu(mg)

    qd = dq.reshape(bn, s, DIFF_H * 2, DIFF_DK)
    kd = dk.reshape(bn, s, DIFF_H * 2, DIFF_DK)
    if pos is not None:
        qd = _rope_2d(qd, pos[0], pos[1])
        kd = _rope_2d(kd, pos[0], pos[1])
    qd = qd.reshape(bn, s, DIFF_H, 2, DIFF_DK)
    kd_flat = kd.reshape(bn, s, DIFF_H, 2 * DIFF_DK)
    vd = dv.reshape(bn, s, DIFF_H, DIFF_DV)
    if ctx is not None:
        kd_all = jnp.concatenate([ctx[2], kd_flat], axis=1)
        vd_all = jnp.concatenate([ctx[3], vd], axis=1)
    else:
        kd_all, vd_all = kd_flat, vd
    kd_all = kd_all.reshape(bn, kd_all.shape[1], DIFF_H, 2, DIFF_DK)
    lam_init = 0.8 - 0.6 * math.exp(-0.3 * l)
    lam = (jnp.exp(jnp.sum(lam_q1.astype(jnp.float32) * lam_k1.astype(jnp.float32)))
           - jnp.exp(jnp.sum(lam_q2.astype(jnp.float32) * lam_k2.astype(jnp.float32))) + lam_init)
    od = _diff_attend(qd, kd_all, vd_all, lam)
    od = _rmsnorm(od, diff_norm) * (1.0 - lam_init)
    y_diff = od.reshape(bn, s, DIFF_H * DIFF_DV) * jax.nn.silu(dg)

    g = jax.nn.sigmoid(mgate).reshape(bn, s, N_BRANCH, D_MODEL)
    merged = (g[:, :, 0] * (y_rnn @ w_br_rnn) + g[:, :, 1] * (y_mla @ w_br_mla)
              + g[:, :, 2] * (y_diff @ w_br_diff))
    x_new = x + gate * _rmsnorm(merged @ w_out, g_post)
    if ctx is None:
        st = jnp.stack([hf[:, -1], hb[:, 0]], axis=1).astype(x.dtype)
        return x_new, (ckv_n, kr, kd_flat, vd, st)
    return x_new, None


def setup_inputs(seed: int = 0) -> dict:
    key = jax.random.key(seed)
    ks = iter(jax.random.split(key, 40))

    def nrm(shape, s):
        return jax.random.normal(next(ks), shape, jnp.float32) * s

    def gain(shape):
        return 1.0 + nrm(shape, 0.02)

    u = jax.random.uniform(next(ks), (DEPTH, 2, D_RNN), jnp.float32, minval=0.9, maxval=0.999)
    sa = u ** (1.0 / RG_C)
    rg_lam = jnp.log(sa) - jnp.log1p(-sa)
    return {
        'x_prompt': nrm((BATCH, SEQ, D_MODEL), 1.0),
        'x_sample': nrm((DEC_BATCH, DEC_SEQ, D_MODEL), 1.0),
        'cache_mla_ckv': nrm((DEC_BATCH, DEPTH, PAST_LEN, KV_RANK), 1.0),
        'cache_mla_krope': nrm((DEC_BATCH, DEPTH, PAST_LEN, MLA_ROPE), 1.0),
        'cache_diff_k': nrm((DEC_BATCH, DEPTH, PAST_LEN, DIFF_H, 2 * DIFF_DK), 1.0),
        'cache_diff_v': nrm((DEC_BATCH, DEPTH, PAST_LEN, DIFF_H, DIFF_DV), 1.0),
        'state_rnn': nrm((DEC_BATCH, DEPTH, 2, D_RNN), 0.5),
        'c': nrm((DEC_BATCH, D_MODEL), 1.0),
        'c_ctx': nrm((D_MODEL,), 1.0),
        'w_mod': nrm((DEPTH, D_MODEL, 3 * D_MODEL), 0.5 * D_MODEL ** -0.5),
        'b_mod': nrm((DEPTH, 3 * D_MODEL), 0.01),
        'g_pre': gain((DEPTH, D_MODEL)),
        'g_post': gain((DEPTH, D_MODEL)),
        'w_in': nrm((DEPTH, D_MODEL, IN_COLS), D_MODEL ** -0.5),
        'conv_w': nrm((DEPTH, CONV_W, D_RNN), CONV_W ** -0.5),
        'conv_b': nrm((DEPTH, D_RNN), 0.01),
        'w_rg_a': nrm((DEPTH, 2, RNN_BLOCKS, RNN_BLK, RNN_BLK), RNN_BLK ** -0.5),
        'b_rg_a': nrm((DEPTH, 2, D_RNN), 0.01),
        'w_rg_x': nrm((DEPTH, 2, RNN_BLOCKS, RNN_BLK, RNN_BLK), RNN_BLK ** -0.5),
        'b_rg_x': nrm((DEPTH, 2, D_RNN), 0.01),
        'rg_lam': rg_lam,
        'q_norm': gain((DEPTH, Q_RANK)),
        'w_uq': nrm((DEPTH, Q_RANK, MLA_H * (MLA_NOPE + MLA_ROPE)), Q_RANK ** -0.5),
        'kv_norm': gain((DEPTH, KV_RANK)),
        'w_ukv': nrm((DEPTH, KV_RANK, MLA_H * (MLA_NOPE + MLA_V)), KV_RANK ** -0.5),
        'lam_q1': nrm((DEPTH, DIFF_DK), 0.1),
        'lam_k1': nrm((DEPTH, DIFF_DK), 0.1),
        'lam_q2': nrm((DEPTH, DIFF_DK), 0.1),
        'lam_k2': nrm((DEPTH, DIFF_DK), 0.1),
        'diff_norm': gain((DEPTH, DIFF_DV)),
        'w_br_rnn': nrm((DEPTH, D_RNN, D_MODEL), D_RNN ** -0.5),
        'w_br_mla': nrm((DEPTH, MLA_H * MLA_V, D_MODEL), (MLA_H * MLA_V) ** -0.5),
        'w_br_diff': nrm((DEPTH, DIFF_H * DIFF_DV, D_MODEL), (DIFF_H * DIFF_DV) ** -0.5),
        'w_out': nrm((DEPTH, D_MODEL, D_MODEL), D_MODEL ** -0.5),
    }


def reference(x_prompt, x_sample, cache_mla_ckv, cache_mla_krope, cache_diff_k, cache_diff_v, state_rnn,
              c, c_ctx, w_mod, b_mod, g_pre, g_post, w_in, conv_w, conv_b, w_rg_a, b_rg_a, w_rg_x, b_rg_x,
              rg_lam, q_norm, w_uq, kv_norm, w_ukv, lam_q1, lam_k1, lam_q2, lam_k2, diff_norm,
              w_br_rnn, w_br_mla, w_br_diff, w_out):
    weights = (w_mod, b_mod, g_pre, g_post, w_in, conv_w, conv_b, w_rg_a, b_rg_a, w_rg_x, b_rg_x, rg_lam,
               q_norm, w_uq, kv_norm, w_ukv, lam_q1, lam_k1, lam_q2, lam_k2, diff_norm,
               w_br_rnn, w_br_mla, w_br_diff, w_out)

    xp = x_prompt
    ckv_l, kr_l, dk_l, dv_l, st_l = [], [], [], [], []
    for l in range(DEPTH):
        lw = [w[l] for w in weights]
        xp, (ckv_n, kr, kd, vd, st) = _layer(l, xp, c_ctx, lw, None, None)
        ckv_l.append(ckv_n)
        kr_l.append(kr)
        dk_l.append(kd)
        dv_l.append(vd)
        st_l.append(st)
    y_prompt = xp

    n_lat = x_sample.shape[1]
    rows = n_lat // GRID_W
    pos_row = jnp.repeat(jnp.arange(rows, dtype=jnp.int32), GRID_W)
    pos_col = jnp.tile(jnp.arange(GRID_W, dtype=jnp.int32), rows)
    xs = x_sample
    for l in range(DEPTH):
        lw = [w[l] for w in weights]
        ctx = (cache_mla_ckv[:, l], cache_mla_krope[:, l], cache_diff_k[:, l], cache_diff_v[:, l], state_rnn[:, l])
        xs, _ = _layer(l, xs, c, lw, (pos_row, pos_col), ctx)
    y_sample = xs

    new_mla_ckv = jnp.stack(ckv_l, axis=1)
    new_mla_krope = jnp.stack(kr_l, axis=1)
    new_diff_k = jnp.stack(dk_l, axis=1)
    new_diff_v = jnp.stack(dv_l, axis=1)
    new_state_rnn = jnp.stack(st_l, axis=1)
    return (y_prompt, y_sample, new_mla_ckv, new_mla_krope, new_diff_k, new_diff_v, new_state_rnn)
```

```python
import math
from contextlib import ExitStack
import numpy as np
import concourse.bass as bass
import concourse.mybir as mybir
from concourse.bass_utils import run_bass_kernel_spmd

F32 = mybir.dt.float32
BF16 = mybir.dt.bfloat16
AF = mybir.ActivationFunctionType
ALU = mybir.AluOpType
AX = mybir.AxisListType

D = 1024
DEPTH = 2
TS = 2048
NPB = 4
SP = 256
TP = NPB * SP
PAST = 512
EPS = 1e-6
IN_COLS = 7328
C_RX, C_RG, C_CQ, C_CKV, C_KR, C_MG, C_DQ, C_DK, C_DV, C_DG, C_MGATE = (
    0, 512, 1024, 1408, 1664, 1696, 2208, 2720, 3232, 3744, 4256)
MLA_SCALE = 96 ** -0.5
DIFF_SCALE = 64 ** -0.5
THETA = 10000.0
STRICT = True


class Buf:
    __slots__ = ("name", "w", "r", "local", "excl")

    def __init__(self, name, local=False, excl=False):
        self.name = name
        self.w = None
        self.r = {}
        self.local = local
        self.excl = excl


class Prog:
    def __init__(self, nc):
        self.nc = nc
        self.eng = {"pe": nc.tensor, "act": nc.scalar, "dve": nc.vector, "pool": nc.gpsimd, "sp": nc.sync}
        self.sem = {}
        self.cnt = {e: 0 for e in self.eng}
        self.seen = {e: {} for e in self.eng}
        for e in self.eng:
            self.sem["e:" + e] = nc.alloc_semaphore("sem_" + e)
        self.dq = {}
        for q, k in (("sp", 24), ("pool", 24), ("act", 6)):
            keys = []
            for j in range(k):
                key = "d:%s%d" % (q, j)
                self.sem[key] = nc.alloc_semaphore("dsem_%s%d" % (q, j))
                keys.append(key)
            self.dq[q] = {"i": 0, "keys": keys}
        self.phase_tok = None
        self.n_wait = 0
        self.log = {e: [] for e in self.eng}

    def _wait(self, e, deps):
        for k, v in deps.items():
            if self.seen[e].get(k, 0) >= v:
                continue
            self.eng[e].wait_ge(self.sem[k], v)
            self.seen[e][k] = v
            self.n_wait += 1
            self.log[e].append(("wait", k, v))

    def _deps(self, e, reads, writes, is_dma):
        deps = {}
        own = "e:" + e

        def add(tok, raw=False):
            if tok is None:
                return
            k, v = tok
            if (not is_dma) and k == own and (e == "pe" or not (raw or STRICT)):
                return
            if deps.get(k, 0) < v:
                deps[k] = v

        loc = False
        for b in reads:
            add(b.w, raw=True)
            loc = loc or b.local
        for b in writes:
            add(b.w)
            for k, v in b.r.items():
                add((k, v))
            loc = loc or b.local
        if loc:
            add(self.phase_tok)
        return deps

    def _record(self, tok, reads, writes):
        k, v = tok
        for b in reads:
            if b.r.get(k, 0) < v:
                b.r[k] = v
        for b in writes:
            b.w = tok
            b.r = {}

    def op(self, e, fn, reads=(), writes=()):
        ex = [b for b in reads if b.excl]
        if ex:
            reads = [b for b in reads if not b.excl]
            writes = list(writes) + ex
        self._wait(e, self._deps(e, reads, writes, False))
        ins = fn(self.eng[e])
        self.cnt[e] += 1
        ins.then_inc(self.sem["e:" + e], 1)
        self.log[e].append(("inc", "e:" + e, 1))
        self._record(("e:" + e, self.cnt[e]), reads, writes)

    def dma(self, q, out, in_, reads=(), writes=(), slow=False):
        st = self.dq[q]
        i = st["i"]
        kk = len(st["keys"])
        key = st["keys"][i % kk]
        deps = self._deps(q, reads, writes, True)
        if i >= kk:
            deps[key] = max(deps.get(key, 0), 16 * (i // kk))
        self._wait(q, deps)
        if slow:
            ins = self.eng[q].dma_start(out=out, in_=in_, allow_slow_non_contiguous=True)
        else:
            ins = self.eng[q].dma_start(out=out, in_=in_)
        ins.then_inc(self.sem[key], 16)
        self.log[q].append(("inc", key, 16))
        st["i"] = i + 1
        self._record((key, 16 * (i // kk + 1)), reads, writes)

    def _all_tokens(self):
        deps = {}
        for e in self.eng:
            if self.cnt[e] > 0:
                deps["e:" + e] = self.cnt[e]
        for q, st in self.dq.items():
            kk = len(st["keys"])
            for j, key in enumerate(st["keys"]):
                uses = (st["i"] - j + kk - 1) // kk if st["i"] > j else 0
                if uses > 0:
                    deps[key] = 16 * uses
        return deps

    def barrier(self, scratch):
        deps = self._all_tokens()
        self._wait("dve", deps)
        ins = self.eng["dve"].memset(scratch, 0.0)
        self.cnt["dve"] += 1
        ins.then_inc(self.sem["e:dve"], 1)
        self.log["dve"].append(("inc", "e:dve", 1))
        self.phase_tok = ("e:dve", self.cnt["dve"])

    def finish(self):
        deps = self._all_tokens()
        self._wait("sp", deps)


def rope_consts():
    t = np.arange(TS)
    prow = (t // 64).astype(np.float64)
    pcol = (t % 64).astype(np.float64)
    inv = THETA ** (-np.arange(16, dtype=np.float64) / 16.0)
    cos = np.zeros((128, TS), np.float64)
    sin = np.zeros((128, TS), np.float64)
    for p in range(128):
        q = p % 64
        pos = prow if q < 32 else pcol
        f = inv[q % 16]
        cos[p] = np.cos(pos * f)
        sin[p] = np.sin(pos * f)
    R = np.zeros((128, 128), np.float32)
    for i in range(128):
        if i % 32 < 16:
            R[i, i + 16] = -1.0
        else:
            R[i, i - 16] = 1.0
    return cos.astype(np.float32), sin.astype(np.float32), np.ascontiguousarray(R.T), np.eye(128, dtype=np.float32)


def build_program(stop=None):
    nc = bass.Bass("TRN2", target_bir_lowering=False)
    P = Prog(nc)

    def din(name, shape):
        return nc.dram_tensor(name, list(shape), F32, kind="ExternalInput").ap()

    def dout(name, shape):
        return nc.dram_tensor(name, list(shape), F32, kind="ExternalOutput").ap()

    xs = din("xs", [TS, D])
    xp = din("xp", [TP, D])
    ckv_c = din("ckv_c", [DEPTH, PAST, 256])
    kr_c = din("kr_c", [DEPTH, PAST, 32])
    dk_c = din("dk_c", [DEPTH, PAST, 512])
    dv_c = din("dv_c", [DEPTH, PAST, 512])
    st_c = din("st_c", [DEPTH, 2, 512])
    cvec = din("cvec", [2, D])
    w_mod = din("w_mod", [DEPTH, D, 3 * D])
    b_mod = din("b_mod", [DEPTH, 3 * D])
    g_pre = din("g_pre", [DEPTH, D])
    g_post = din("g_post", [DEPTH, D])
    w_in = din("w_in", [DEPTH, D, IN_COLS])
    conv_w = din("conv_w", [DEPTH, 4, 512])
    conv_b = din("conv_b", [DEPTH, 512])
    w_rg_a = din("w_rg_a", [DEPTH, 2, 8, 64, 64])
    b_rg_a = din("b_rg_a", [DEPTH, 2, 512])
    w_rg_x = din("w_rg_x", [DEPTH, 2, 8, 64, 64])
    b_rg_x = din("b_rg_x", [DEPTH, 2, 512])
    rg_lam = din("rg_lam", [DEPTH, 2, 512])
    q_norm = din("q_norm", [DEPTH, 384])
    w_uq = din("w_uq", [DEPTH, 384, 768])
    kv_norm = din("kv_norm", [DEPTH, 256])
    w_ukv = din("w_ukv", [DEPTH, 256, 1024])
    lam_q1 = din("lam_q1", [DEPTH, 64])
    lam_k1 = din("lam_k1", [DEPTH, 64])
    lam_q2 = din("lam_q2", [DEPTH, 64])
    lam_k2 = din("lam_k2", [DEPTH, 64])
    diff_norm = din("diff_norm", [DEPTH, 128])
    w_br = [din("w_br_rnn", [DEPTH, 512, D]), din("w_br_mla", [DEPTH, 512, D]), din("w_br_diff", [DEPTH, 512, D])]
    w_out = din("w_out", [DEPTH, D, D])
    cosd_d = din("cosd", [128, TS])
    sind_d = din("sind", [128, TS])
    rdT_d = din("rdT", [128, 128])
    ident_d = din("ident", [128, 128])

    ys = dout("ys", [TS, D])
    yp = dout("yp", [TP, D])
    o_ckv = dout("o_ckv", [NPB, DEPTH, SP, 256])
    o_kr = dout("o_kr", [NPB, DEPTH, SP, 32])
    o_dk = dout("o_dk", [NPB, DEPTH, SP, 512])
    o_dv = dout("o_dv", [NPB, DEPTH, SP, 512])
    o_st = dout("o_st", [NPB, DEPTH, 2, 512])
    x1s = nc.dram_tensor("x1s", [TS, D], F32).ap()
    x1p = nc.dram_tensor("x1p", [TP, D], F32).ap()

    uid = {"n": 0}

    def un(name):
        uid["n"] += 1
        return "%s_%d" % (name, uid["n"])

    def sb(name, shape, dt=F32):
        return nc.alloc_sbuf_tensor(un("sb_" + name), list(shape), dt)

    ident = sb("ident", [128, 128]); b_ident = Buf("ident")
    rdT = sb("rdT", [128, 128]); b_rdT = Buf("rdT")
    cosd = sb("cosd_sb", [128, TS]); sind = sb("sind_sb", [128, TS]); b_tab = Buf("tab")
    ones_b = sb("ones_b", [128, 128], BF16); ones_f = sb("ones_f", [128, 128]); b_ones = Buf("ones")
    scratch1 = sb("scratch1", [128, 8])
    ccol = sb("ccol", [128, 8, 2]); scf = sb("scf", [128, 8, 2]); scb = sb("scb", [128, 8, 2], BF16)
    screp = sb("screp", [128, 2, 8, 128], BF16); b_sc = Buf("sc")
    hT = sb("hT", [128, 8, TS], BF16)
    b_hT = [Buf("hT%d" % g) for g in range(TS // 512)]
    NW = 3
    wbuf = [sb("wbuf%d" % i, [128, 8, 512], BF16) for i in range(NW)]
    b_wbuf = [Buf("wbuf%d" % i) for i in range(NW)]
    wstate = {"i": 0}
    bm = sb("bm", [128, 24]); gpre = sb("gpre", [128, 8]); modc = sb("modc", [128, 24, 2]); a1 = sb("a1", [128, 8, 2])
    tmp8 = sb("tmp8", [128, 8]); modraw = sb("modraw", [128, 48])
    gg = [sb("gg%d" % u, [128, D]) for u in range(2)]
    cw = sb("cw", [128, 4, 4]); cb = sb("cb", [128, 4]); bga = sb("bga", [128, 2, 4]); bgx = sb("bgx", [128, 2, 4])
    lam = sb("lam", [128, 2, 4]); sl = sb("sl", [128, 2, 4]); sl2 = sb("sl2", [128, 2, 4]); h0c = sb("h0c", [128, 2, 4])
    lt = [sb("lt%d" % i, [128, 2, 4]) for i in range(4)]
    bd = sb("bd", [128, 2, 2, 4, 128], BF16)
    qn = sb("qn", [128, 3]); kvn = sb("kvn", [128, 2]); kvn_bc = sb("kvn_bc", [128, 256])
    dn = sb("dn", [128, 1]); neglam = sb("neglam", [128, 1]); lamw = sb("lamw", [128, 4, 64]); lame = sb("lame", [128, 2])
    b_par = Buf("params")
    pall = nc.alloc_psum_tensor("pall", [128, 8 * 512], F32)
    ps = [pall[:, i * 512:(i + 1) * 512] for i in range(8)]
    b_ps = [Buf("ps%d" % i, excl=True) for i in range(8)]

    wcache = {}

    def cast_load(dst_ap, dst_buf, src_ap, key):
        if key is None:
            P.dma("pool", dst_ap, src_ap, writes=[dst_buf])
            return
        if key in wcache:
            sc, b_sc = wcache[key]
            P.dma("pool", dst_ap, sc, reads=[b_sc], writes=[dst_buf])
            return
        P.dma("pool", dst_ap, src_ap, writes=[dst_buf])
        sc = nc.dram_tensor(un("wsc"), list(dst_ap.shape), BF16).ap()
        b_sc = Buf("wsc")
        P.dma("sp", sc, dst_ap, reads=[dst_buf], writes=[b_sc])
        wcache[key] = (sc, b_sc)

    def load_w(src_ap, ncols, key):
        i = wstate["i"] % NW
        wstate["i"] += 1
        cast_load(wbuf[i][:, :, 0:ncols], b_wbuf[i], src_ap.rearrange("(c p) n -> p c n", p=128), key)
        return wbuf[i], b_wbuf[i]

    dbg = {}

    def dbg_dump(name, tile_ap, shape, reads):
        return

    def ring_slot():
        i = wstate["i"] % NW
        wstate["i"] += 1
        return wbuf[i], b_wbuf[i]

    def mm(ps_ap, pairs, reads, writes, start=True, stop=True):
        def fn(pe):
            n = len(pairs)
            ins = None
            for i, (lh, rh) in enumerate(pairs):
                ins = pe.matmul(ps_ap, lhsT=lh, rhs=rh, start=(start and i == 0), stop=(stop and i == n - 1))
            return ins
        P.op("pe", fn, reads=reads, writes=writes)

    def act(out, in_, func, reads, writes, **kw):
        P.op("act", lambda e: e.activation(out=out, in_=in_, func=func, **kw), reads=reads, writes=writes)

    def dve(fn, reads, writes):
        P.op("dve", fn, reads=reads, writes=writes)

    P.dma("sp", ident[:], ident_d, writes=[b_ident])
    P.dma("sp", rdT[:], rdT_d, writes=[b_rdT])
    P.dma("sp", cosd[:], cosd_d, writes=[b_tab])
    P.dma("sp", sind[:], sind_d, writes=[b_tab])
    dve(lambda e: e.memset(ones_b[:], 1.0), [], [b_ones])
    dve(lambda e: e.memset(ones_f[:], 1.0), [], [b_ones])
    for u in range(2):
        P.dma("sp", ccol[:, :, u], cvec[u].rearrange("(c p) -> p c", p=128), writes=[b_sc], slow=True)
    act(scf[:], ccol[:], AF.Silu, [b_sc], [b_sc])
    act(scb[:], scf[:], AF.Identity, [b_sc], [b_sc])
    for u in range(2):
        for k in range(8):
            act(screp[:, u, k, :], ones_f[:], AF.Identity, [b_sc, b_ones], [b_sc], scale=scf[:, k, u:u + 1])

    units = [
        dict(name="s", u=0, T=TS, nseq=1, Ts=TS, ctx=PAST, rope=True, x=[xs, x1s, ys]),
        dict(name="p", u=1, T=TP, nseq=NPB, Ts=SP, ctx=0, rope=False, x=[xp, x1p, yp]),
    ]
    b_x1 = {("s", t): Buf("x1s%d" % t) for t in range(TS // 128)}
    b_x1.update({("p", t): Buf("x1p%d" % t) for t in range(TP // 128)})

    def layer_params(l):
        lam_init = 0.8 - 0.6 * math.exp(-0.3 * l)
        P.barrier(scratch1[:, 0:1])
        rd = [b_par]
        ldb = []

        def nb():
            ldb.append(Buf("pload", local=True))
            return ldb[-1]
        P.dma("sp", bm[:], b_mod[l].rearrange("(c p) -> p c", p=128), writes=[nb()], slow=True)
        P.dma("sp", gpre[:], g_pre[l].rearrange("(c p) -> p c", p=128), writes=[nb()], slow=True)
        for k in range(4):
            P.dma("sp", cw[:, :, k], conv_w[l, k].rearrange("(c p) -> p c", p=128), writes=[nb()], slow=True)
        P.dma("sp", cb[:], conv_b[l].rearrange("(c p) -> p c", p=128), writes=[nb()], slow=True)
        for d in range(2):
            P.dma("sp", bga[:, d, :], b_rg_a[l, d].rearrange("(c p) -> p c", p=128), writes=[nb()], slow=True)
        for d in range(2):
            P.dma("sp", bgx[:, d, :], b_rg_x[l, d].rearrange("(c p) -> p c", p=128), writes=[nb()], slow=True)
        for d in range(2):
            P.dma("sp", lam[:, d, :], rg_lam[l, d].rearrange("(c p) -> p c", p=128), writes=[nb()], slow=True)
        for d in range(2):
            P.dma("sp", h0c[:, d, :], st_c[l, d].rearrange("(c p) -> p c", p=128), writes=[nb()], slow=True)
        P.dma("sp", qn[:], q_norm[l].rearrange("(c p) -> p c", p=128), writes=[nb()], slow=True)
        P.dma("sp", kvn[:], kv_norm[l].rearrange("(c p) -> p c", p=128), writes=[nb()], slow=True)
        P.dma("sp", kvn_bc[:], kv_norm[l:l + 1, :].partition_broadcast(128), writes=[nb()])
        P.dma("sp", dn[:], diff_norm[l].rearrange("(c p) -> p c", p=128), writes=[nb()], slow=True)
        for i, t in enumerate((lam_q1, lam_k1, lam_q2, lam_k2)):
            P.dma("sp", lamw[:, i, :], t[l:l + 1, :].partition_broadcast(128), writes=[nb()])
        b_bd = Buf("bd0", local=True)
        dve(lambda e: e.memset(bd[:], 0.0), [], [b_bd])
        for gi, wg in enumerate((w_rg_a, w_rg_x)):
            for d in range(2):
                for hh in range(2):
                    src = wg[l, d].rearrange("(c two) k j -> two k c j", two=2)[hh]
                    P.dma("pool", bd[hh * 64:(hh + 1) * 64, d, gi, :, hh * 64:(hh + 1) * 64], src, reads=[b_bd], writes=[nb()])
        dve(lambda e: e.memset(tmp8[:, 0:1], 0.0), ldb + [b_bd], [b_par])
        act(lt[0][:], lam[:], AF.Exp, rd, [b_par], scale=-1.0)
        dve(lambda e: e.tensor_scalar(out=lt[1][:], in0=lt[0][:], scalar1=2.0, scalar2=None, op0=ALU.add), rd, [b_par])
        dve(lambda e: e.reciprocal(out=lt[1][:], in_=lt[1][:]), rd, [b_par])
        dve(lambda e: e.tensor_tensor(out=lt[0][:], in0=lt[0][:], in1=lt[1][:], op=ALU.mult), rd, [b_par])
        dve(lambda e: e.tensor_tensor(out=lt[1][:], in0=lt[0][:], in1=lt[0][:], op=ALU.mult), rd, [b_par])
        dve(lambda e: e.tensor_scalar(out=lt[2][:], in0=lt[1][:], scalar1=1.0 / 11.0, scalar2=1.0 / 9.0, op0=ALU.mult, op1=ALU.add), rd, [b_par])
        for cst in (1.0 / 7.0, 1.0 / 5.0, 1.0 / 3.0, 1.0):
            dve(lambda e: e.tensor_tensor(out=lt[2][:], in0=lt[2][:], in1=lt[1][:], op=ALU.mult), rd, [b_par])
            dve(lambda e, cst=cst: e.tensor_scalar(out=lt[2][:], in0=lt[2][:], scalar1=cst, scalar2=None, op0=ALU.add), rd, [b_par])
        dve(lambda e: e.tensor_tensor(out=lt[2][:], in0=lt[2][:], in1=lt[0][:], op=ALU.mult), rd, [b_par])
        dve(lambda e: e.tensor_scalar(out=sl[:], in0=lt[2][:], scalar1=-16.0, scalar2=None, op0=ALU.mult), rd, [b_par])
        dve(lambda e: e.tensor_scalar(out=sl2[:], in0=lt[2][:], scalar1=-32.0, scalar2=None, op0=ALU.mult), rd, [b_par])
        dve(lambda e: e.tensor_tensor(out=lamw[:, 0, :], in0=lamw[:, 0, :], in1=lamw[:, 1, :], op=ALU.mult), rd, [b_par])
        dve(lambda e: e.tensor_tensor(out=lamw[:, 2, :], in0=lamw[:, 2, :], in1=lamw[:, 3, :], op=ALU.mult), rd, [b_par])
        dve(lambda e: e.reduce_sum(out=lame[:, 0:1], in_=lamw[:, 0, :], axis=AX.X), rd, [b_par])
        dve(lambda e: e.reduce_sum(out=lame[:, 1:2], in_=lamw[:, 2, :], axis=AX.X), rd, [b_par])
        act(lame[:], lame[:], AF.Exp, rd, [b_par])
        dve(lambda e: e.tensor_tensor(out=neglam[:], in0=lame[:, 1:2], in1=lame[:, 0:1], op=ALU.subtract), rd, [b_par])
        dve(lambda e: e.tensor_scalar(out=neglam[:], in0=neglam[:], scalar1=-lam_init, scalar2=None, op0=ALU.add), rd, [b_par])
        dve(lambda e: e.tensor_scalar(out=dn[:], in0=dn[:], scalar1=1.0 - lam_init, scalar2=None, op0=ALU.mult), rd, [b_par])

        with ExitStack() as es:
            bmg = es.enter_context(nc.sbuf_tensor(un("bmg"), [128, D], F32)); gpb = es.enter_context(nc.sbuf_tensor(un("gpb"), [128, D], F32))
            b_l = Buf("modl", local=True)
            P.dma("sp", bmg[:], b_mod[l:l + 1, 2 * D:3 * D].partition_broadcast(128), writes=[b_l])
            P.dma("sp", gpb[:], g_post[l:l + 1, :].partition_broadcast(128), writes=[b_l])
            psm = ps[0][:].rearrange("p (c u) -> p c u", u=2)
            for pc in range(6):
                wt, bw = load_w(w_mod[l][:, pc * 512:(pc + 1) * 512], 512, None)
                for cc in range(4):
                    col = pc * 4 + cc
                    mm(psm[:, col, :], [(wt[:, k, cc * 128:(cc + 1) * 128], scb[:, k, :]) for k in range(8)],
                       [bw, b_sc], [b_ps[0]])
                if pc >= 4:
                    half = pc - 4
                    for u in range(2):
                        bank = 1 + u * 2 + half
                        mm(ps[bank][:], [(screp[:, u, k, :], wt[:, k, :]) for k in range(8)], [bw, b_sc], [b_ps[bank]])
            dve(lambda e: e.tensor_copy(out=modraw[:], in_=ps[0][:, 0:48]), [b_ps[0]], [b_par])
            dbg_dump("modraw", modraw[:], [128, 48], [b_par])
            mrv = modraw[:].rearrange("p (c u) -> p c u", u=2)
            for u in range(2):
                dve(lambda e, u=u: e.tensor_tensor(out=modc[:, :, u], in0=mrv[:, :, u], in1=bm[:], op=ALU.add), [b_par], [b_par])
                dve(lambda e, u=u: e.tensor_scalar(out=tmp8[:], in0=modc[:, 8:16, u], scalar1=1.0, scalar2=None, op0=ALU.add), rd, [b_par])
                dve(lambda e, u=u: e.tensor_tensor(out=a1[:, :, u], in0=tmp8[:], in1=gpre[:], op=ALU.mult), rd, [b_par])
                for half in range(2):
                    bank = 1 + u * 2 + half
                    hs = slice(half * 512, (half + 1) * 512)
                    dve(lambda e, u=u, bank=bank, hs=hs: e.tensor_tensor(out=gg[u][:, hs], in0=ps[bank][:], in1=bmg[:, hs], op=ALU.add), [b_ps[bank], b_l], [b_par])
                    dve(lambda e, u=u, hs=hs: e.tensor_tensor(out=gg[u][:, hs], in0=gg[u][:, hs], in1=gpb[:, hs], op=ALU.mult), [b_l, b_par], [b_par])
            P.barrier(scratch1[:, 0:1])
        if l == 0:
            dbg_dump("modc", modc[:], [128, 24, 2], [b_par])
            dbg_dump("a1", a1[:], [128, 8, 2], [b_par])
            dbg_dump("gg1", gg[1][:], [128, D], [b_par])
            dbg_dump("scf", scf[:], [128, 8, 2], [b_sc])
            dbg_dump("sl", sl[:], [128, 2, 4], [b_par])
            dbg_dump("neglam", neglam[:], [128, 1], [b_par])

    def phase_h(l, U):
        u = U["u"]
        xsrc = U["x"][l]
        with ExitStack() as es:
            xt = [es.enter_context(nc.sbuf_tensor(un("xt%d" % i), [128, 4, D], F32)) for i in range(2)]
            b_xt = [Buf("xt%d" % i, local=True) for i in range(2)]
            junk = es.enter_context(nc.sbuf_tensor(un("junk"), [128, D], F32)); b_junk = Buf("junk", local=True)
            ssq = [es.enter_context(nc.sbuf_tensor(un("ssq%d" % i), [128, 4], F32)) for i in range(2)]
            b_ss = [Buf("ssq%d" % i, local=True) for i in range(2)]
            for g in range(U["T"] // 512):
                i = g % 2
                rdx = [b_x1[(U["name"], 4 * g + j)] for j in range(4)] if l > 0 else []
                P.dma("sp", xt[i][:], xsrc[g * 512:(g + 1) * 512, :].rearrange("(j p) d -> p j d", p=128), reads=rdx, writes=[b_xt[i]])
                for j in range(4):
                    act(junk[:], xt[i][:, j, :], AF.Square, [b_xt[i]], [b_junk, b_ss[i]], accum_out=ssq[i][:, j:j + 1])
                act(ssq[i][:], ssq[i][:], AF.Sqrt, [b_ss[i]], [b_ss[i]], scale=1.0 / D, bias=EPS)
                dve(lambda e, i=i: e.reciprocal(out=ssq[i][:], in_=ssq[i][:]), [b_ss[i]], [b_ss[i]])
                for j in range(4):
                    dve(lambda e, i=i, j=j: e.tensor_scalar(out=xt[i][:, j, :], in0=xt[i][:, j, :], scalar1=ssq[i][:, j:j + 1], scalar2=None, op0=ALU.mult),
                        [b_ss[i], b_xt[i]], [b_xt[i]])
                for c in range(8):
                    bank = c % 4

                    def tr(pe, i=i, c=c, bank=bank):
                        ins = None
                        for j in range(4):
                            ins = pe.transpose(ps[bank][:, j * 128:(j + 1) * 128], xt[i][:, j, c * 128:(c + 1) * 128], ident[:])
                        return ins
                    P.op("pe", tr, reads=[b_xt[i], b_ident], writes=[b_ps[bank]])
                    dst = hT[:, c, g * 512:(g + 1) * 512]
                    if c % 2 == 0:
                        act(dst, ps[bank][:], AF.Identity, [b_ps[bank], b_par], [b_hT[g]], scale=a1[:, c, u:u + 1], bias=modc[:, c, u:u + 1])
                    else:
                        dve(lambda e, dst=dst, bank=bank, c=c: e.tensor_scalar(out=dst, in0=ps[bank][:], scalar1=a1[:, c, u:u + 1], scalar2=modc[:, c, u:u + 1], op0=ALU.mult, op1=ALU.add),
                            [b_ps[bank], b_par], [b_hT[g]])
            P.barrier(scratch1[:, 0:1])
            if l == 0 and U["name"] == "p":
                dbg_dump("hT", hT[:, :, 0:1024], [128, 8, 1024], hTr if False else b_hT[:2])

    def phase_rnn(l, U, yr, b_yr):
        T, nseq, Ts = U["T"], U["nseq"], U["Ts"]
        G = T // 512
        hTr = b_hT[:G]
        with ExitStack() as es:
            def loc(name, shape, dt=F32):
                return es.enter_context(nc.sbuf_tensor(un(name), list(shape), dt)), Buf(name, local=True)
            rx, b_rx = loc("rx", [128, T]); xc, b_xc = loc("xc", [128, T]); xcb, b_xcb = loc("xcb", [128, T], BF16)
            A, b_A = loc("rA", [128, T]); I, b_I = loc("rI", [128, T]); S, b_S = loc("rS", [128, T])
            A1, b_A1 = loc("rA1", [128, T]); I1, b_I1 = loc("rI1", [128, T])
            srg0, b_srg0 = loc("srg0", [128, 512]); srg1, b_srg1 = loc("srg1", [128, 512]); stc, b_stc0 = loc("stc", [128, 4, NPB, 2])
            b_stcs = [b_stc0] + [Buf("stc%d" % i, local=True) for i in range(1, 4)]
            wrx, bwrx = load_w(w_in[l][:, C_RX:C_RX + 512], 512, ("in", l, C_RX))
            wrg, bwrg = load_w(w_in[l][:, C_RG:C_RG + 512], 512, ("in", l, C_RG))
            for c in range(4):
                cs = slice(c * 128, (c + 1) * 128)
                for g in range(G):
                    gs = slice(g * 512, (g + 1) * 512)
                    bank = g % 4
                    mm(ps[bank][:], [(wrx[:, k, cs], hT[:, k, gs]) for k in range(8)], [bwrx, hTr[g]], [b_ps[bank]])
                    act(rx[:, gs], ps[bank][:], AF.Identity, [b_ps[bank]], [b_rx])
                dve(lambda e, c=c: e.tensor_scalar(out=xc[:], in0=rx[:], scalar1=cw[:, c, 2:3], scalar2=cb[:, c:c + 1], op0=ALU.mult, op1=ALU.add), [b_rx, b_par], [b_xc])
                for s in range(nseq):
                    t0 = s * Ts
                    for k, off in ((0, -2), (1, -1), (3, 1)):
                        if off < 0:
                            o_sl = slice(t0 - off, t0 + Ts); i_sl = slice(t0, t0 + Ts + off)
                        else:
                            o_sl = slice(t0, t0 + Ts - off); i_sl = slice(t0 + off, t0 + Ts)
                        dve(lambda e, c=c, k=k, o_sl=o_sl, i_sl=i_sl: e.scalar_tensor_tensor(out=xc[:, o_sl], in0=rx[:, i_sl], scalar=cw[:, c, k:k + 1], in1=xc[:, o_sl], op0=ALU.mult, op1=ALU.add),
                            [b_rx, b_par, b_xc], [b_xc])
                act(xcb[:], xc[:], AF.Identity, [b_xc], [b_xcb])
                Ad = [(A, b_A), (A1, b_A1)]
                Id = [(I, b_I), (I1, b_I1)]
                Td = [(rx, b_rx), (S, b_S)]
                for d in range(2):
                    A_, b_A_ = Ad[d]; I_, b_I_ = Id[d]; T_, b_T_ = Td[d]
                    for gi, (dst, b_dst, bias_t) in enumerate(((A_, b_A_, bga), (I_, b_I_, bgx))):
                        for g in range(G):
                            gs = slice(g * 512, (g + 1) * 512)
                            bank = 4 + (g % 4)
                            mm(ps[bank][:], [(bd[:, d, gi, c, :], xcb[:, gs])], [b_par, b_xcb], [b_ps[bank]])
                            act(dst[:, gs], ps[bank][:], AF.Sigmoid, [b_ps[bank], b_par], [b_dst], bias=bias_t[:, d, c:c + 1])
                    act(T_[:], A_[:], AF.Exp, [b_A_, b_par], [b_T_], scale=sl2[:, d, c:c + 1])
                    act(A_[:], A_[:], AF.Exp, [b_A_, b_par], [b_A_], scale=sl[:, d, c:c + 1])
                    act(T_[:], T_[:], AF.Sqrt, [b_T_], [b_T_], scale=-1.0, bias=1.0)
                for d in range(2):
                    A_, b_A_ = Ad[d]; I_, b_I_ = Id[d]; T_, b_T_ = Td[d]
                    dve(lambda e, I_=I_, T_=T_: e.tensor_tensor(out=I_[:], in0=I_[:], in1=T_[:], op=ALU.mult), [b_I_, b_T_], [b_I_])
                    dve(lambda e, I_=I_: e.tensor_tensor(out=I_[:], in0=I_[:], in1=xc[:], op=ALU.mult), [b_I_, b_xc], [b_I_])
                    for s in range(nseq):
                        ss_ = slice(s * Ts, (s + 1) * Ts)
                        init = h0c[:, d, c:c + 1] if U["ctx"] else 0.0
                        if d == 0:
                            dve(lambda e, ss_=ss_, init=init, T_=T_, A_=A_, I_=I_: e.tensor_tensor_scan(out=T_[:, ss_], data0=A_[:, ss_], data1=I_[:, ss_], initial=init, op0=ALU.mult, op1=ALU.add),
                                [b_A_, b_I_, b_par], [b_T_])
                        else:
                            rs_ = slice((s + 1) * Ts - 1, s * Ts - 1 if s > 0 else None, -1)
                            dve(lambda e, rs_=rs_, init=init, T_=T_, A_=A_, I_=I_: e.tensor_tensor_scan(out=T_[:, rs_], data0=A_[:, rs_], data1=I_[:, rs_], initial=init, op0=ALU.mult, op1=ALU.add),
                                [b_A_, b_I_, b_par], [b_T_])
                if not U["ctx"]:
                    for s in range(nseq):
                        dve(lambda e, s=s, c=c: e.tensor_copy(out=stc[:, c, s, 0:1], in_=rx[:, (s + 1) * Ts - 1:(s + 1) * Ts]), [b_rx], [b_stcs[c]])
                        dve(lambda e, s=s, c=c: e.tensor_copy(out=stc[:, c, s, 1:2], in_=S[:, s * Ts:s * Ts + 1]), [b_S], [b_stcs[c]])
                if not U["ctx"]:
                    for s in range(nseq):
                        P.dma("sp", o_st[s, l, :, c * 128:(c + 1) * 128].rearrange("d p -> p d"), stc[:, c, s, :], reads=[b_stcs[c]], slow=True)
                dve(lambda e: e.tensor_tensor(out=rx[:], in0=rx[:], in1=S[:], op=ALU.add), [b_rx, b_S], [b_rx])
                for g in range(G):
                    gs = slice(g * 512, (g + 1) * 512)
                    bank = g % 4
                    mm(ps[bank][:], [(wrg[:, k, cs], hT[:, k, gs]) for k in range(8)], [bwrg, hTr[g]], [b_ps[bank]])
                    srg, b_srg = (srg0, b_srg0) if g % 2 == 0 else (srg1, b_srg1)
                    act(srg[:], ps[bank][:], AF.Silu, [b_ps[bank]], [b_srg])
                    dve(lambda e, gs=gs, c=c, srg=srg: e.tensor_tensor(out=yr[:, c, gs], in0=rx[:, gs], in1=srg[:], op=ALU.mult), [b_rx, b_srg], [b_yr])
            P.barrier(scratch1[:, 0:1])

    def rope_evac(dst, psrc, bank_r, g, rows, reads, writes, xr, b_xr, t1, b_t1):
        gs = slice(g * 512, (g + 1) * 512)
        n = rows.stop - rows.start
        act(xr[rows, :], psrc, AF.Identity, reads, [b_xr])
        mm(ps[bank_r][rows, :], [(rdT[rows, rows], xr[rows, :])], [b_xr, b_rdT], [b_ps[bank_r]])
        dve(lambda e: e.tensor_tensor(out=t1[rows, :], in0=ps[bank_r][rows, :], in1=sind[rows, gs], op=ALU.mult), [b_ps[bank_r], b_tab], [b_t1])
        dve(lambda e: e.tensor_tensor(out=xr[rows, :], in0=xr[rows, :], in1=cosd[rows, gs], op=ALU.mult), [b_xr, b_tab], [b_xr])
        dve(lambda e: e.tensor_tensor(out=dst, in0=xr[rows, :], in1=t1[rows, :], op=ALU.add), [b_xr, b_t1], writes)

    def phase_mla(l, U, ym, b_ym):
        T, nseq, Ts, ctx = U["T"], U["nseq"], U["Ts"], U["ctx"]
        G = T // 512
        Tk = Ts + ctx
        KT = Tk // 128
        TkA = nseq * Tk
        hTr = b_hT[:G]
        with ExitStack() as es:
            def loc(name, shape, dt=F32):
                return es.enter_context(nc.sbuf_tensor(un(name), list(shape), dt)), Buf(name, local=True)
            cqn, b_cqn = loc("cqn", [128, 3, T], BF16)
            ckvn, b_ckvn = loc("ckvn", [128, 2, TkA], BF16)
            Kh = []; b_Kh = []
            for i in range(2):
                t, b = loc("Kh%d" % i, [128, TkA], BF16); Kh.append(t); b_Kh.append(b)
            Qh = []; b_Qh = []
            for i in range(2):
                t, b = loc("Qh%d" % i, [128, T], BF16); Qh.append(t); b_Qh.append(b)
            Vt = []; b_Vt = []
            for i in range(2):
                t, b = loc("Vt%d" % i, [128, nseq * KT, 2, 128], BF16); Vt.append(t); b_Vt.append(b)
            PT2a, b_PT2a = loc("PT2a", [128, 1024], BF16); PT2b, b_PT2b = loc("PT2b", [128, 1024], BF16)
            xr, b_xr = loc("xr", [128, 512]); t1, b_t1 = loc("t1", [128, 512])
            xr2, b_xr2 = loc("xr2", [128, 512]); t12, b_t12 = loc("t12", [128, 512])
            sq, b_sq = loc("sq", [128, 3, 512], BF16)
            rs, b_rs = loc("rs", [128, 512])
            rc, b_rc = loc("rc", [128, 512]); ot, b_ot = loc("ot", [128, 512])
            wq2, b_wq2 = loc("wq2", [128, 3, 8, 128], BF16)
            wkv2, b_wkv2 = loc("wkv2", [128, 2, 8, 128], BF16)
            wkr2, b_wkr2 = loc("wkr2", [128, 8, 64], BF16)
            ost, b_ost = loc("ost", [128, 288]); ssv, b_ssv = loc("ssv", [128, 2])
            es2 = es.enter_context(ExitStack())

            def loc2(name, shape, dt=F32):
                return es2.enter_context(nc.sbuf_tensor(un(name), list(shape), dt)), Buf(name, local=True)
            wq_slot, b_wq_raw = ring_slot()
            wq_raw = wq_slot[:].rearrange("p c n -> p (c n)")[:, 0:2304].rearrange("p (c n) -> p c n", n=768)
            wkv_slot, b_wkv_raw = ring_slot()
            wkv_raw = wkv_slot[:].rearrange("p c n -> p (c n)")[:, 0:2048].rearrange("p (c n) -> p c n", n=1024)
            cst, b_cst = loc2("cst", [128, 4, 256]); krst, b_krst = loc2("krst", [128, 4, 64])
            krT = Kh

            cast_load(wq_raw, b_wq_raw, w_uq[l].rearrange("(c p) n -> p c n", p=128), ("uq", l))
            cast_load(wkv_raw, b_wkv_raw, w_ukv[l].rearrange("(c p) n -> p c n", p=128), ("ukv", l))
            wq_v = wq_raw.rearrange("p c (h e) -> p c h e", e=96)
            dve(lambda e: e.memset(wq2[:], 0.0), [], [b_wq2])
            for c in range(3):
                dve(lambda e, c=c: e.tensor_copy(out=wq2[:, c, :, 0:64:2], in_=wq_v[:, c, :, 64:96]), [b_wq_raw], [b_wq2])
                dve(lambda e, c=c: e.tensor_copy(out=wq2[:, c, :, 64:128], in_=wq_v[:, c, :, 0:64]), [b_wq_raw], [b_wq2])
            wkv_v = wkv_raw.rearrange("p c (h e) -> p c h e", e=128)
            for c in range(2):
                dve(lambda e, c=c: e.tensor_copy(out=wkv2[:, c, :, 0:64], in_=wkv_v[:, c, :, 64:128]), [b_wkv_raw], [b_wkv2])
                dve(lambda e, c=c: e.tensor_copy(out=wkv2[:, c, :, 64:128], in_=wkv_v[:, c, :, 0:64]), [b_wkv_raw], [b_wkv2])
            wcq, bwcq = load_w(w_in[l][:, C_CQ:C_CQ + 384], 384, ("in", l, C_CQ))
            wck, bwck = load_w(w_in[l][:, C_CKV:C_CKV + 288], 288, ("in", l, C_CKV))
            dve(lambda e: e.memset(wkr2[:], 0.0), [], [b_wkr2])
            dve(lambda e: e.tensor_copy(out=wkr2[:, :, 0:64:2], in_=wck[:, :, 256:288]), [bwck], [b_wkr2])
            wmg, bwmg = load_w(w_in[l][:, C_MG:C_MG + 512], 512, ("in", l, C_MG))
            for i in range(2):
                dve(lambda e, i=i: e.memset(Vt[i][:], 1.0), [], [b_Vt[i]])

            def koff(s, t):
                return s * Tk + t
            if ctx:
                P.dma("sp", cst[:], ckv_c[l].rearrange("(j p) d -> p j d", p=128), writes=[b_cst])
                dve(lambda e: e.memset(krst[:], 0.0), [], [b_krst])
                krraw, b_krraw = loc2("krraw", [128, 4, 32])
                P.dma("sp", krraw[:], kr_c[l].rearrange("(j p) d -> p j d", p=128), writes=[b_krraw])
                dve(lambda e: e.tensor_copy(out=krst[:, :, 0:64:2], in_=krraw[:]), [b_krraw, b_krst], [b_krst])
                for j2 in range(2):
                    def tr(pe, j2=j2):
                        ins = None
                        for j in range(4):
                            ins = pe.transpose(ps[j2][:, j * 128:(j + 1) * 128], cst[:, j, j2 * 128:(j2 + 1) * 128], ident[:])
                        return ins
                    P.op("pe", tr, reads=[b_cst, b_ident], writes=[b_ps[j2]])
                    act(ckvn[:, j2, 0:512], ps[j2][:], AF.Identity, [b_ps[j2]], [b_ckvn])

                def tr2(pe):
                    ins = None
                    for j in range(4):
                        ins = pe.transpose(ps[2][0:64, j * 128:(j + 1) * 128], krst[:, j, :], ident[:])
                    return ins
                P.op("pe", tr2, reads=[b_krst, b_ident], writes=[b_ps[2]])
                for i in range(2):
                    act(Kh[i][0:64, 0:512], ps[2][0:64, :], AF.Identity, [b_ps[2]], [b_Kh[i]])

            for g in range(G):
                gs = slice(g * 512, (g + 1) * 512)
                for j in range(3):
                    mm(ps[j][:], [(wcq[:, k, j * 128:(j + 1) * 128], hT[:, k, gs]) for k in range(8)], [bwcq, hTr[g]], [b_ps[j]])
                    act(sq[:, j, :], ps[j][:], AF.Square, [b_ps[j]], [b_sq])
                mm(ps[3][:], [(ones_b[:], sq[:, j, :]) for j in range(3)], [b_ones, b_sq], [b_ps[3]])
                act(rs[:], ps[3][:], AF.Ln, [b_ps[3]], [b_rs], scale=1.0 / 384.0, bias=EPS)
                act(rs[:], rs[:], AF.Exp, [b_rs], [b_rs], scale=-0.5)
                for j in range(3):
                    dve(lambda e, j=j, gs=gs: e.scalar_tensor_tensor(out=cqn[:, j, gs], in0=ps[j][:], scalar=qn[:, j:j + 1], in1=rs[:], op0=ALU.mult, op1=ALU.mult),
                        [b_ps[j], b_par, b_rs], [b_cqn])
                for j in range(2):
                    mm(ps[4 + j][:], [(wck[:, k, j * 128:(j + 1) * 128], hT[:, k, gs]) for k in range(8)], [bwck, hTr[g]], [b_ps[4 + j]])
                    act(sq[:, j, :], ps[4 + j][:], AF.Square, [b_ps[4 + j]], [b_sq])
                mm(ps[6][:], [(ones_b[:], sq[:, j, :]) for j in range(2)], [b_ones, b_sq], [b_ps[6]])
                act(rs[:], ps[6][:], AF.Ln, [b_ps[6]], [b_rs], scale=1.0 / 256.0, bias=EPS)
                act(rs[:], rs[:], AF.Exp, [b_rs], [b_rs], scale=-0.5)
                if nseq == 1:
                    kdst = [(slice(ctx + g * 512, ctx + (g + 1) * 512), slice(0, 512))]
                else:
                    kdst = [(slice((2 * g + h2) * Tk, (2 * g + h2 + 1) * Tk), slice(h2 * 256, (h2 + 1) * 256)) for h2 in range(2)]
                for j in range(2):
                    for (kd_, sd_) in kdst:
                        dve(lambda e, j=j, kd_=kd_, sd_=sd_: e.scalar_tensor_tensor(out=ckvn[:, j, kd_], in0=ps[4 + j][:, sd_], scalar=kvn[:, j:j + 1], in1=rs[:, sd_], op0=ALU.mult, op1=ALU.mult),
                            [b_ps[4 + j], b_par, b_rs], [b_ckvn])
                mm(ps[7][0:64, :], [(wkr2[:, k, :], hT[:, k, gs]) for k in range(8)], [b_wkr2, hTr[g]], [b_ps[7]])
                for (kd_, sd_) in kdst:
                    if U["rope"]:
                        rope_evac(Kh[0][0:64, kd_], ps[7][0:64, :], 3, g, slice(0, 64), [b_ps[7]], [b_Kh[0]], xr, b_xr, t1, b_t1)
                        act(Kh[1][0:64, kd_], Kh[0][0:64, kd_], AF.Identity, [b_Kh[0]], [b_Kh[1]])
                    else:
                        for i in range(2):
                            act(Kh[i][0:64, kd_], ps[7][0:64, sd_], AF.Identity, [b_ps[7]], [b_Kh[i]])
                for j in range(4):
                    bank = j % 3
                    mm(ps[bank][:], [(wmg[:, k, j * 128:(j + 1) * 128], hT[:, k, gs]) for k in range(8)], [bwmg, hTr[g]], [b_ps[bank]])
                    act(ym[:, j, gs], ps[bank][:], AF.Silu, [b_ps[bank]], [b_ym])

            if not ctx:
                for tt in range(T // 128):
                    ts_ = slice(tt * 128, (tt + 1) * 128)
                    bank = 4 + tt % 2
                    s, r0 = divmod(tt * 128, Ts)
                    mm(ps[bank][:, 0:288], [(hT[:, k, ts_], wck[:, k, 0:288]) for k in range(8)], [bwck, hTr[tt // 4]], [b_ps[bank]])
                    act(ost[:, 0:256], ps[bank][:, 0:256], AF.Square, [b_ps[bank]], [b_ost, b_ssv], accum_out=ssv[:, 0:1])
                    act(ost[:, 256:288], ps[bank][:, 256:288], AF.Identity, [b_ps[bank]], [b_ost])
                    act(ssv[:, 0:1], ssv[:, 0:1], AF.Sqrt, [b_ssv], [b_ssv], scale=1.0 / 256.0, bias=EPS)
                    dve(lambda e: e.reciprocal(out=ssv[:, 0:1], in_=ssv[:, 0:1]), [b_ssv], [b_ssv])
                    dve(lambda e, bank=bank: e.scalar_tensor_tensor(out=ost[:, 0:256], in0=ps[bank][:, 0:256], scalar=ssv[:, 0:1], in1=kvn_bc[:], op0=ALU.mult, op1=ALU.mult),
                        [b_ps[bank], b_ssv, b_par], [b_ost])
                    P.dma("sp", o_ckv[s, l, r0:r0 + 128, :], ost[:, 0:256], reads=[b_ost])
                    P.dma("sp", o_kr[s, l, r0:r0 + 128, :], ost[:, 256:288], reads=[b_ost])

            NQ = 512 if nseq == 1 else Ts
            nq_groups = T // NQ
            PT = [PT2a, PT2b]
            b_PT = [b_PT2a, b_PT2b]

            def prep(h):
                hi = h % 2
                if hi == 0:
                    vi = (h // 2) % 2
                    for kt in range(nseq * KT):
                        bank = 6 + kt % 2
                        ks = slice(kt * 128, (kt + 1) * 128)
                        pv = ps[bank][:, 0:128].rearrange("p (a e) -> p a e", e=64)
                        mm(pv, [(ckvn[:, r, ks], wkv2[:, r, h:h + 2, 0:64]) for r in range(2)], [b_ckvn, b_wkv2], [b_ps[bank]])
                        dve(lambda e, vi=vi, kt=kt, pv=pv: e.tensor_copy(out=Vt[vi][:, kt, 0, 0:64], in_=pv[:, 0, :]), [b_ps[bank]], [b_Vt[vi]])
                        dve(lambda e, vi=vi, kt=kt, pv=pv: e.tensor_copy(out=Vt[vi][:, kt, 1, 64:128], in_=pv[:, 1, :]), [b_ps[bank]], [b_Vt[vi]])
                        yield
                for kg in range(TkA // 512):
                    ks = slice(kg * 512, (kg + 1) * 512)
                    bank = 6 + kg % 2
                    mm(ps[bank][:], [(wkv2[:, r, h, :], ckvn[:, r, ks]) for r in range(2)], [b_wkv2, b_ckvn], [b_ps[bank]])
                    dve(lambda e, ks=ks, bank=bank, hi=hi: e.tensor_copy(out=Kh[hi][64:128, ks], in_=ps[bank][64:128, :]), [b_ps[bank]], [b_Kh[hi]])
                    yield
                for g in range(G):
                    gs = slice(g * 512, (g + 1) * 512)
                    bank = 6 + g % 2
                    mm(ps[bank][:], [(wq2[:, c, h, :], cqn[:, c, gs]) for c in range(3)], [b_wq2, b_cqn], [b_ps[bank]])
                    dve(lambda e, gs=gs, bank=bank, hi=hi: e.tensor_copy(out=Qh[hi][64:128, gs], in_=ps[bank][64:128, :]), [b_ps[bank]], [b_Qh[hi]])
                    if U["rope"]:
                        rows = slice(0, 64)
                        xq, b_xq = (xr, b_xr) if g % 2 == 0 else (xr2, b_xr2)
                        tq, b_tq = (t1, b_t1) if g % 2 == 0 else (t12, b_t12)
                        dve(lambda e, xq=xq, bank=bank: e.tensor_copy(out=xq[rows, :], in_=ps[bank][0:64, :]), [b_ps[bank]], [b_xq])
                        yield
                        rb = 7 - g % 2
                        mm(ps[rb][rows, :], [(rdT[rows, rows], xq[rows, :])], [b_xq, b_rdT], [b_ps[rb]])
                        yield
                        dve(lambda e, tq=tq, rb=rb, gs=gs: e.tensor_tensor(out=tq[rows, :], in0=ps[rb][rows, :], in1=sind[rows, gs], op=ALU.mult), [b_ps[rb], b_tab], [b_tq])
                        dve(lambda e, xq=xq, gs=gs: e.tensor_tensor(out=xq[rows, :], in0=xq[rows, :], in1=cosd[rows, gs], op=ALU.mult), [b_xq, b_tab], [b_xq])
                        dve(lambda e, xq=xq, tq=tq, gs=gs, hi=hi: e.tensor_tensor(out=Qh[hi][0:64, gs], in0=xq[rows, :], in1=tq[rows, :], op=ALU.add), [b_xq, b_tq], [b_Qh[hi]])
                    else:
                        dve(lambda e, gs=gs, bank=bank, hi=hi: e.tensor_copy(out=Qh[hi][0:64, gs], in_=ps[bank][0:64, :]), [b_ps[bank]], [b_Qh[hi]])
                    yield

            def n_prep_units(h):
                return (nseq * KT if h % 2 == 0 else 0) + TkA // 512 + G * (3 if U["rope"] else 1)

            def attention(h, gen=None, n_units=0):
                hi = h % 2
                vt = Vt[(h // 2) % 2]
                b_vt = b_Vt[(h // 2) % 2]
                steps = []
                for qg in range(nq_groups):
                    if nseq == 1:
                        for kp in range(KT // 2):
                            steps.append((qg, 0, (2 * kp, 2 * kp + 1), kp == 0, kp == KT // 2 - 1))
                    else:
                        steps.append((qg, qg, (0, 1), True, True))

                def qk(i):
                    qg, s, kts, first, last = steps[i]
                    pair = (i % 2) * 2
                    qs = slice(qg * NQ, (qg + 1) * NQ)
                    for t, kt in enumerate(kts):
                        kk_ = s * KT + kt
                        ks = slice(kk_ * 128, (kk_ + 1) * 128)
                        if NQ == 512:
                            mm(ps[pair + t][:], [(Kh[hi][:, ks], Qh[hi][:, qs])], [b_Kh[hi], b_Qh[hi]], [b_ps[pair + t]])
                        else:
                            mm(ps[pair][:, t * NQ:(t + 1) * NQ], [(Kh[hi][:, ks], Qh[hi][:, qs])], [b_Kh[hi], b_Qh[hi]], [b_ps[pair]])

                def expv(i):
                    qg, s, kts, first, last = steps[i]
                    pair = (i % 2) * 2
                    pt = PT[i % 2]
                    b_pt = b_PT[i % 2]
                    ob = 4 + qg % 2
                    qs = slice(qg * NQ, (qg + 1) * NQ)
                    if NQ == 512:
                        act(pt[:, 0:1024], pall[:, pair * 512:(pair + 2) * 512], AF.Exp, [b_ps[pair], b_ps[pair + 1]], [b_pt], scale=MLA_SCALE)
                    else:
                        act(pt[:, 0:512], ps[pair][:], AF.Exp, [b_ps[pair]], [b_pt], scale=MLA_SCALE)
                    for t, kt in enumerate(kts):
                        kk_ = s * KT + kt
                        mm(ps[ob][:, 0:NQ], [(vt[:, kk_, hi, :], pt[:, t * NQ:(t + 1) * NQ])], [b_vt, b_pt], [b_ps[ob]],
                           start=(first and t == 0), stop=(last and t == len(kts) - 1))
                    if last:
                        dr = slice(0, 64) if hi == 0 else slice(64, 128)
                        sr = slice(64, 128) if hi == 0 else slice(0, 64)
                        act(rc[dr, 0:NQ], ps[ob][sr, 0:NQ], AF.Ln, [b_ps[ob]], [b_rc])
                        act(rc[dr, 0:NQ], rc[dr, 0:NQ], AF.Exp, [b_rc], [b_rc], scale=-1.0)
                        dve(lambda e: e.tensor_tensor(out=ot[dr, 0:NQ], in0=ps[ob][dr, 0:NQ], in1=rc[dr, 0:NQ], op=ALU.mult), [b_ps[ob], b_rc], [b_ot])
                        dve(lambda e: e.tensor_tensor(out=ym[dr, h // 2, qs], in0=ot[dr, 0:NQ], in1=ym[dr, h // 2, qs], op=ALU.mult), [b_ot, b_ym], [b_ym])

                per_step = -(-n_units // max(1, len(steps) - 1)) if gen is not None else 0
                qk(0)
                for i in range(len(steps)):
                    if i + 1 < len(steps):
                        qk(i + 1)
                    if gen is not None:
                        for _ in range(per_step):
                            next(gen, None)
                    expv(i)
                if gen is not None:
                    for _ in gen:
                        pass

            for _ in prep(0):
                pass
            for h in range(8):
                if h + 1 < 8:
                    attention(h, prep(h + 1), n_prep_units(h + 1))
                else:
                    attention(h)
            P.barrier(scratch1[:, 0:1])

    def phase_diff(l, U, yd, b_yd):
        T, nseq, Ts, ctx = U["T"], U["nseq"], U["Ts"], U["ctx"]
        G = T // 512
        Tk = Ts + ctx
        KT = Tk // 128
        TkA = nseq * Tk
        hTr = b_hT[:G]
        with ExitStack() as es:
            def loc(name, shape, dt=F32):
                return es.enter_context(nc.sbuf_tensor(un(name), list(shape), dt)), Buf(name, local=True)
            qd, b_qd = loc("qd", [128, 4, T], BF16)
            kd, b_kd = loc("kd", [128, 4, TkA], BF16)
            vd, b_vd = loc("vd", [128, nseq * KT, 512], BF16)
            PT2a, b_PT2a = loc("dPT2a", [128, 1024], BF16); PT2b, b_PT2b = loc("dPT2b", [128, 1024], BF16)
            xr, b_xr = loc("dxr", [128, 512]); t1, b_t1 = loc("dt1", [128, 512])
            r1, b_r1 = loc("r1", [128, 512]); r2, b_r2 = loc("r2", [128, 512])
            od, b_od = loc("od", [128, 512]); ou, b_ou = loc("ou", [128, 512])
            sqd, b_sqd = loc("sqd", [128, 512], BF16)
            ost = []; b_ost = []
            for i in range(2):
                t, b = loc("dost%d" % i, [128, 512]); ost.append(t); b_ost.append(b)

            def koff_list(g):
                if nseq == 1:
                    return [(slice(ctx + g * 512, ctx + (g + 1) * 512), slice(0, 512))]
                return [(slice((2 * g + h2) * Tk, (2 * g + h2 + 1) * Tk), slice(h2 * 256, (h2 + 1) * 256)) for h2 in range(2)]

            if ctx:
                cst = [xr, t1, r1, r2]
                b_cst = [b_xr, b_t1, b_r1, b_r2]
                for j in range(4):
                    P.dma("sp", cst[j][:], dk_c[l][j * 128:(j + 1) * 128, :], writes=[b_cst[j]])
                for j2 in range(4):
                    def tr(pe, j2=j2):
                        ins = None
                        for j in range(4):
                            ins = pe.transpose(ps[j2][:, j * 128:(j + 1) * 128], cst[j][:, j2 * 128:(j2 + 1) * 128], ident[:])
                        return ins
                    P.op("pe", tr, reads=b_cst + [b_ident], writes=[b_ps[j2]])
                    act(kd[:, j2, 0:512], ps[j2][:], AF.Identity, [b_ps[j2]], [b_kd])
                P.dma("pool", vd[:, 0:4, :], dv_c[l].rearrange("(j p) d -> p j d", p=128), writes=[b_vd])

            for (col0, dst, b_dst, is_k) in ((C_DQ, qd, b_qd, False), (C_DK, kd, b_kd, True)):
                wt, bw = load_w(w_in[l][:, col0:col0 + 512], 512, ("in", l, col0))
                for g in range(G):
                    gs = slice(g * 512, (g + 1) * 512)
                    for j in range(4):
                        bank = j % 3
                        mm(ps[bank][:], [(wt[:, k, j * 128:(j + 1) * 128], hT[:, k, gs]) for k in range(8)], [bw, hTr[g]], [b_ps[bank]])
                        dsts = koff_list(g) if is_k else [(gs, slice(0, 512))]
                        if U["rope"]:
                            if j % 2 == 0:
                                rope_evac(dst[:, j, dsts[0][0]], ps[bank][:], 3 + (j % 2), g, slice(0, 128), [b_ps[bank]], [b_dst], xr, b_xr, t1, b_t1)
                            else:
                                rope_evac(dst[:, j, dsts[0][0]], ps[bank][:], 3 + (j % 2), g, slice(0, 128), [b_ps[bank]], [b_dst], r1, b_r1, r2, b_r2)
                        else:
                            for (kd_, sd_) in dsts:
                                act(dst[:, j, kd_], ps[bank][:, sd_], AF.Identity, [b_ps[bank]], [b_dst])
                if is_k and not ctx:
                    for tt in range(T // 128):
                        ts_ = slice(tt * 128, (tt + 1) * 128)
                        bank = 5 + tt % 2
                        s, r0 = divmod(tt * 128, Ts)
                        mm(ps[bank][:], [(hT[:, k, ts_], wt[:, k, :]) for k in range(8)], [bw, hTr[tt // 4]], [b_ps[bank]])
                        act(ost[tt % 2][:], ps[bank][:], AF.Identity, [b_ps[bank]], [b_ost[tt % 2]])
                        P.dma("sp", o_dk[s, l, r0:r0 + 128, :], ost[tt % 2][:], reads=[b_ost[tt % 2]])
            wt, bw = load_w(w_in[l][:, C_DV:C_DV + 512], 512, ("in", l, C_DV))
            for tt in range(T // 128):
                ts_ = slice(tt * 128, (tt + 1) * 128)
                bank = 5 + tt % 2
                s, r0 = divmod(tt * 128, Ts)
                kt = s * KT + (ctx + r0) // 128
                mm(ps[bank][:], [(hT[:, k, ts_], wt[:, k, :]) for k in range(8)], [bw, hTr[tt // 4]], [b_ps[bank]])
                if ctx:
                    dve(lambda e, kt=kt, bank=bank: e.tensor_copy(out=vd[:, kt, :], in_=ps[bank][:]), [b_ps[bank]], [b_vd])
                else:
                    act(ost[tt % 2][:], ps[bank][:], AF.Identity, [b_ps[bank]], [b_ost[tt % 2]])
                    dve(lambda e, kt=kt, tt=tt: e.tensor_copy(out=vd[:, kt, :], in_=ost[tt % 2][:]), [b_ost[tt % 2]], [b_vd])
                    P.dma("sp", o_dv[s, l, r0:r0 + 128, :], ost[tt % 2][:], reads=[b_ost[tt % 2]])
            wt, bw = load_w(w_in[l][:, C_DG:C_DG + 512], 512, ("in", l, C_DG))
            for g in range(G):
                gs = slice(g * 512, (g + 1) * 512)
                for j in range(4):
                    bank = j % 3
                    mm(ps[bank][:], [(wt[:, k, j * 128:(j + 1) * 128], hT[:, k, gs]) for k in range(8)], [bw, hTr[g]], [b_ps[bank]])
                    act(yd[:, j, gs], ps[bank][:], AF.Silu, [b_ps[bank]], [b_yd])

            NQ = 512 if nseq == 1 else Ts
            PT = [PT2a, PT2b]
            b_PT = [b_PT2a, b_PT2b]
            for j in range(4):
                steps = []
                for qg in range(T // NQ):
                    s = 0 if nseq == 1 else qg
                    for kt in range(KT):
                        steps.append((qg, s, kt))

                def qk(i, j=j, steps=steps):
                    qg, s, kt = steps[i]
                    pair = (i % 2) * 2
                    qs = slice(qg * NQ, (qg + 1) * NQ)
                    kk_ = s * KT + kt
                    ks = slice(kk_ * 128, (kk_ + 1) * 128)
                    for cp in range(2):
                        rows = slice(cp * 64, (cp + 1) * 64)
                        if NQ == 512:
                            mm(ps[pair + cp][:], [(kd[rows, j, ks], qd[rows, j, qs])], [b_kd, b_qd], [b_ps[pair + cp]])
                        else:
                            mm(ps[pair + cp][:, 0:NQ], [(kd[rows, j, ks], qd[rows, j, qs])], [b_kd, b_qd], [b_ps[pair + cp]])

                def expv(i, j=j, steps=steps):
                    qg, s, kt = steps[i]
                    pair = (i % 2) * 2
                    pt = PT[i % 2]
                    b_pt = b_PT[i % 2]
                    qs = slice(qg * NQ, (qg + 1) * NQ)
                    kk_ = s * KT + kt
                    if NQ == 512:
                        act(pt[:, 0:1024], pall[:, pair * 512:(pair + 2) * 512], AF.Exp, [b_ps[pair], b_ps[pair + 1]], [b_pt], scale=DIFF_SCALE)
                    else:
                        act(pt[:, 0:2 * NQ].rearrange("p (b c) -> p b c", c=NQ),
                            pall[:, pair * 512:(pair + 2) * 512].rearrange("p (b c) -> p b c", c=512)[:, :, 0:NQ],
                            AF.Exp, [b_ps[pair], b_ps[pair + 1]], [b_pt], scale=DIFF_SCALE)
                    for cp in range(2):
                        mm(ps[4 + cp][:, 0:NQ], [(vd[:, kk_, j * 128:(j + 1) * 128], pt[:, cp * NQ:(cp + 1) * NQ])], [b_vd, b_pt], [b_ps[4 + cp]],
                           start=(kt == 0), stop=(kt == KT - 1))
                        mm(ps[6 + cp][:, 0:NQ], [(ones_b[:], pt[:, cp * NQ:(cp + 1) * NQ])], [b_ones, b_pt], [b_ps[6 + cp]],
                           start=(kt == 0), stop=(kt == KT - 1))
                    if kt == KT - 1:
                        act(r1[:, 0:NQ], ps[6][:, 0:NQ], AF.Ln, [b_ps[6]], [b_r1])
                        act(r2[:, 0:NQ], ps[7][:, 0:NQ], AF.Ln, [b_ps[7]], [b_r2])
                        act(r1[:, 0:NQ], r1[:, 0:NQ], AF.Exp, [b_r1], [b_r1], scale=-1.0)
                        act(r2[:, 0:NQ], r2[:, 0:NQ], AF.Exp, [b_r2], [b_r2], scale=-1.0)
                        dve(lambda e: e.tensor_tensor(out=ou[:, 0:NQ], in0=ps[4][:, 0:NQ], in1=r1[:, 0:NQ], op=ALU.mult), [b_ps[4], b_r1], [b_ou])
                        dve(lambda e: e.scalar_tensor_tensor(out=od[:, 0:NQ], in0=ps[5][:, 0:NQ], scalar=neglam[:, 0:1], in1=r2[:, 0:NQ], op0=ALU.mult, op1=ALU.mult),
                            [b_ps[5], b_r2, b_par], [b_od])
                        dve(lambda e: e.tensor_tensor(out=od[:, 0:NQ], in0=od[:, 0:NQ], in1=ou[:, 0:NQ], op=ALU.add), [b_od, b_ou], [b_od])
                        dve(lambda e: e.tensor_tensor(out=sqd[:, 0:NQ], in0=od[:, 0:NQ], in1=od[:, 0:NQ], op=ALU.mult), [b_od], [b_sqd])

                        def part2(nb, j=j, qs=qs):
                            mm(ps[nb][:, 0:NQ], [(ones_b[:], sqd[:, 0:NQ])], [b_ones, b_sqd], [b_ps[nb]])
                            act(ou[:, 0:NQ], ps[nb][:, 0:NQ], AF.Ln, [b_ps[nb]], [b_ou], scale=1.0 / 128.0, bias=EPS)
                            act(ou[:, 0:NQ], ou[:, 0:NQ], AF.Exp, [b_ou], [b_ou], scale=-0.5)
                            dve(lambda e: e.scalar_tensor_tensor(out=od[:, 0:NQ], in0=od[:, 0:NQ], scalar=dn[:, 0:1], in1=ou[:, 0:NQ], op0=ALU.mult, op1=ALU.mult),
                                [b_od, b_ou, b_par], [b_od])
                            dve(lambda e: e.tensor_tensor(out=yd[:, j, qs], in0=od[:, 0:NQ], in1=yd[:, j, qs], op=ALU.mult), [b_od, b_yd], [b_yd])
                        pend.append((i + min(2, KT - 1), part2))

                pend = []
                qk(0)
                for i in range(len(steps)):
                    while pend and pend[0][0] <= i:
                        pend.pop(0)[1](((i + 1) % 2) * 2)
                    if i + 1 < len(steps):
                        qk(i + 1)
                    expv(i)
                while pend:
                    pend.pop(0)[1](0)
            P.barrier(scratch1[:, 0:1])

    def phase_merge_out(l, U, ybr, b_ybr):
        T = U["T"]
        G = T // 512
        u = U["u"]
        hTr = b_hT[:G]
        xsrc = U["x"][l]
        xdst = U["x"][l + 1]
        with ExitStack() as es:
            def loc(name, shape, dt=F32):
                return es.enter_context(nc.sbuf_tensor(un(name), list(shape), dt)), Buf(name, local=True)
            mg, b_mg = loc("merged", [128, 8, T], BF16)
            esA = ExitStack()

            def locA(name, shape, dt=F32):
                return esA.enter_context(nc.sbuf_tensor(un(name), list(shape), dt)), Buf(name, local=True)
            loc_save = loc
            loc = locA
            wbr2 = []; b_wbr2 = []
            for i in range(2):
                t, b = loc("wbr2_%d" % i, [128, 3, 4, 256], BF16); wbr2.append(t); b_wbr2.append(b)
            gsb = []; b_gsb = []
            for i in range(4):
                t, b = loc("gsb%d" % i, [128, 512]); gsb.append(t); b_gsb.append(b)
            mt = []; b_mt = []
            for i in range(4):
                t, b = loc("mt%d" % i, [128, 512]); mt.append(t); b_mt.append(b)
            it = 0
            hslot = {"n": 0, "cur": None}

            def half_slot():
                if hslot["n"] % 2 == 0:
                    hslot["cur"] = ring_slot()
                t, b = hslot["cur"]
                v = t[:].rearrange("p c n -> p (c n)")[:, (hslot["n"] % 2) * 2048:(hslot["n"] % 2 + 1) * 2048].rearrange("p (c n) -> p c n", n=256)
                hslot["n"] += 1
                return v, b

            def load_group(f2):
                wi = f2 % 2
                wm = []
                for b in range(3):
                    c0 = C_MGATE + b * D + f2 * 256
                    v, bb = half_slot()
                    cast_load(v, bb, w_in[l][:, c0:c0 + 256].rearrange("(c p) n -> p c n", p=128), ("mg2", l, b, f2))
                    wm.append((v, bb))
                    cast_load(wbr2[wi][:, b, :, :], b_wbr2[wi], w_br[b][l][:, f2 * 256:(f2 + 1) * 256].rearrange("(c p) n -> p c n", p=128), ("br2", l, b, f2))
                return wm

            nxt = load_group(0)
            for f2 in range(4):
                wm2 = nxt
                wi = f2 % 2
                if f2 + 1 < 4:
                    nxt = load_group(f2 + 1)
                for g in range(G):
                    gs = slice(g * 512, (g + 1) * 512)
                    for fi in range(2):
                        f = f2 * 2 + fi
                        fs = slice(fi * 128, (fi + 1) * 128)
                        ma = (f * G + g) % 2
                        for b in range(3):
                            r = it % 4
                            it += 1
                            bA, bB = 2 * r, 2 * r + 1
                            mm(ps[bA][:], [(wm2[b][0][:, k, fs], hT[:, k, gs]) for k in range(8)], [wm2[b][1], hTr[g]], [b_ps[bA]])
                            act(gsb[r][:], ps[bA][:], AF.Sigmoid, [b_ps[bA]], [b_gsb[r]])
                            mm(ps[bB][:], [(wbr2[wi][:, b, k, fs], ybr[b][:, k, gs]) for k in range(4)], [b_wbr2[wi], b_ybr[b]], [b_ps[bB]])
                            if b == 0:
                                dve(lambda e, r=r, bB=bB, ma=ma: e.tensor_tensor(out=mt[ma][:], in0=ps[bB][:], in1=gsb[r][:], op=ALU.mult), [b_ps[bB], b_gsb[r]], [b_mt[ma]])
                            else:
                                dve(lambda e, r=r, bB=bB, ma=ma: e.tensor_tensor(out=mt[2 + ma][:], in0=ps[bB][:], in1=gsb[r][:], op=ALU.mult), [b_ps[bB], b_gsb[r]], [b_mt[2 + ma]])
                                if b == 1:
                                    dve(lambda e, ma=ma: e.tensor_tensor(out=mt[ma][:], in0=mt[ma][:], in1=mt[2 + ma][:], op=ALU.add), [b_mt[ma], b_mt[2 + ma]], [b_mt[ma]])
                                else:
                                    dve(lambda e, ma=ma, f=f, gs=gs: e.tensor_tensor(out=mg[:, f, gs], in0=mt[ma][:], in1=mt[2 + ma][:], op=ALU.add), [b_mt[ma], b_mt[2 + ma]], [b_mg])
            P.barrier(scratch1[:, 0:1])
            esA.close()
            loc = loc_save
            wo = []
            for half in range(2):
                wo.append(load_w(w_out[l][:, half * 512:(half + 1) * 512], 512, ("out", l, half)))
            xt = []; b_xt = []; ot = []; b_ot = []
            for i in range(2):
                t, b = loc("oxt%d" % i, [128, D]); xt.append(t); b_xt.append(b)
                t, b = loc("oot%d" % i, [128, D]); ot.append(t); b_ot.append(b)
            junk, b_junk = loc("ojunk", [128, 512]); ss2, b_ss2 = loc("oss2", [128, 4])
            for tt in range(T // 128):
                i = tt % 2
                ts_ = slice(tt * 128, (tt + 1) * 128)
                rdx = [b_x1[(U["name"], tt)]] if l > 0 else []
                P.dma("sp", xt[i][:], xsrc[ts_, :], reads=rdx, writes=[b_xt[i]])
                for half in range(2):
                    bank = 6 + half
                    mm(ps[bank][:], [(mg[:, k, ts_], wo[half][0][:, k, :]) for k in range(8)], [wo[half][1], b_mg], [b_ps[bank]])
                    act(junk[:], ps[bank][:], AF.Square, [b_ps[bank]], [b_junk, b_ss2], accum_out=ss2[:, half:half + 1])
                dve(lambda e: e.tensor_tensor(out=ss2[:, 2:3], in0=ss2[:, 0:1], in1=ss2[:, 1:2], op=ALU.add), [b_ss2], [b_ss2])
                act(ss2[:, 3:4], ss2[:, 2:3], AF.Sqrt, [b_ss2], [b_ss2], scale=1.0 / D, bias=EPS)
                dve(lambda e: e.reciprocal(out=ss2[:, 3:4], in_=ss2[:, 3:4]), [b_ss2], [b_ss2])
                for half in range(2):
                    bank = 6 + half
                    hs = slice(half * 512, (half + 1) * 512)
                    dve(lambda e, i=i, bank=bank, hs=hs: e.scalar_tensor_tensor(out=ot[i][:, hs], in0=ps[bank][:], scalar=ss2[:, 3:4], in1=gg[u][:, hs], op0=ALU.mult, op1=ALU.mult),
                        [b_ps[bank], b_ss2, b_par], [b_ot[i]])
                dve(lambda e, i=i: e.tensor_tensor(out=ot[i][:], in0=ot[i][:], in1=xt[i][:], op=ALU.add), [b_ot[i], b_xt[i]], [b_ot[i]])
                wr = [b_x1[(U["name"], tt)]] if l == 0 else []
                P.dma("sp", xdst[ts_, :], ot[i][:], reads=[b_ot[i]], writes=wr)
            P.barrier(scratch1[:, 0:1])

    class _Stop(Exception):
        pass

    def chk(l, U, ph):
        if stop is not None and tuple(stop[:3]) == (l, U["name"] if U else None, ph):
            raise _Stop()

    try:
        for l in range(DEPTH):
            layer_params(l)
            chk(l, None, "params")
            for U in units:
                T = U["T"]
                if stop is not None and len(stop) > 3 and U["name"] not in stop[3]:
                    continue
                phase_h(l, U)
                chk(l, U, "h")
                with ExitStack() as es:
                    ym = es.enter_context(nc.sbuf_tensor(un("ym"), [128, 4, T], BF16)); b_ym = Buf("ym", local=True)
                    phase_mla(l, U, ym, b_ym)
                    chk(l, U, "mla")
                    yd = es.enter_context(nc.sbuf_tensor(un("yd"), [128, 4, T], BF16)); b_yd = Buf("yd", local=True)
                    phase_diff(l, U, yd, b_yd)
                    chk(l, U, "diff")
                    yr = es.enter_context(nc.sbuf_tensor(un("yr"), [128, 4, T], BF16)); b_yr = Buf("yr", local=True)
                    phase_rnn(l, U, yr, b_yr)
                    chk(l, U, "rnn")
                    phase_merge_out(l, U, [yr, ym, yd], [b_yr, b_ym, b_yd])
                    chk(l, U, "merge")
    except _Stop:
        pass
    P.finish()
    return nc, P


_CACHE = {}


def make_in_maps(inp, n=8):
    f32 = lambda a: np.ascontiguousarray(np.asarray(a, dtype=np.float32))
    cos, sin, rdT, ident = rope_consts()
    shared = {k: f32(inp[k]) for k in (
        "w_mod", "b_mod", "g_pre", "g_post", "w_in", "conv_w", "conv_b", "w_rg_a", "b_rg_a", "w_rg_x", "b_rg_x", "rg_lam",
        "q_norm", "w_uq", "kv_norm", "w_ukv", "lam_q1", "lam_k1", "lam_q2", "lam_k2", "diff_norm",
        "w_br_rnn", "w_br_mla", "w_br_diff", "w_out")}
    shared.update({"cosd": cos, "sind": sin, "rdT": rdT, "ident": ident})
    x_prompt = f32(inp["x_prompt"]); x_sample = f32(inp["x_sample"])
    c = f32(inp["c"]); c_ctx = f32(inp["c_ctx"])
    in_maps = []
    for i in range(n):
        m = dict(shared)
        m["xs"] = x_sample[i]
        m["xp"] = x_prompt[NPB * i:NPB * (i + 1)].reshape(TP, D)
        m["ckv_c"] = f32(inp["cache_mla_ckv"][i])
        m["kr_c"] = f32(inp["cache_mla_krope"][i])
        m["dk_c"] = f32(inp["cache_diff_k"][i]).reshape(DEPTH, PAST, 512)
        m["dv_c"] = f32(inp["cache_diff_v"][i]).reshape(DEPTH, PAST, 512)
        m["st_c"] = f32(inp["state_rnn"][i])
        m["cvec"] = np.stack([c[i], c_ctx], axis=0)
        in_maps.append(m)
    return in_maps


def kernel(**inp):
    n = 8
    if "nc" not in _CACHE:
        _CACHE["nc"] = build_program()[0]
    nc = _CACHE["nc"]
    in_maps = make_in_maps(inp, n)
    res = run_bass_kernel_spmd(nc, in_maps, core_ids=list(range(n)))
    R = res.results
    y_prompt = np.concatenate([R[i]["yp"].reshape(NPB, SP, D) for i in range(n)], axis=0)
    y_sample = np.stack([R[i]["ys"] for i in range(n)], axis=0)
    new_ckv = np.concatenate([R[i]["o_ckv"] for i in range(n)], axis=0)
    new_kr = np.concatenate([R[i]["o_kr"] for i in range(n)], axis=0)
    new_dk = np.concatenate([R[i]["o_dk"].reshape(NPB, DEPTH, SP, 4, 128) for i in range(n)], axis=0)
    new_dv = np.concatenate([R[i]["o_dv"].reshape(NPB, DEPTH, SP, 4, 128) for i in range(n)], axis=0)
    new_st = np.concatenate([R[i]["o_st"] for i in range(n)], axis=0)
    return (y_prompt.astype(np.float32), y_sample.astype(np.float32), new_ckv.astype(np.float32), new_kr.astype(np.float32),
            new_dk.astype(np.float32), new_dv.astype(np.float32), new_st.astype(np.float32))
```

```python
import math
from contextlib import ExitStack
import numpy as np
import concourse.bass as bass
import concourse.mybir as mybir
from concourse.bass_utils import run_bass_kernel_spmd

F32 = mybir.dt.float32
BF16 = mybir.dt.bfloat16
AF = mybir.ActivationFunctionType
ALU = mybir.AluOpType
AX = mybir.AxisListType

D = 1024
DEPTH = 2
TS = 2048
NPB = 4
SP = 256
TP = NPB * SP
PAST = 512
EPS = 1e-6
IN_COLS = 7328
C_RX, C_RG, C_CQ, C_CKV, C_KR, C_MG, C_DQ, C_DK, C_DV, C_DG, C_MGATE = (
    0, 512, 1024, 1408, 1664, 1696, 2208, 2720, 3232, 3744, 4256)
MLA_SCALE = 96 ** -0.5
DIFF_SCALE = 64 ** -0.5
THETA = 10000.0
STRICT = True


class Buf:
    __slots__ = ("name", "w", "r", "local", "excl")

    def __init__(self, name, local=False, excl=False):
        self.name = name
        self.w = None
        self.r = {}
        self.local = local
        self.excl = excl


class Prog:
    def __init__(self, nc):
        self.nc = nc
        self.eng = {"pe": nc.tensor, "act": nc.scalar, "dve": nc.vector, "pool": nc.gpsimd, "sp": nc.sync}
        self.sem = {}
        self.cnt = {e: 0 for e in self.eng}
        self.seen = {e: {} for e in self.eng}
        for e in self.eng:
            self.sem["e:" + e] = nc.alloc_semaphore("sem_" + e)
        self.dq = {}
        for q, k in (("sp", 24), ("pool", 24), ("act", 6)):
            keys = []
            for j in range(k):
                key = "d:%s%d" % (q, j)
                self.sem[key] = nc.alloc_semaphore("dsem_%s%d" % (q, j))
                keys.append(key)
            self.dq[q] = {"i": 0, "keys": keys}
        self.phase_tok = None
        self.n_wait = 0
        self.log = {e: [] for e in self.eng}

    def _wait(self, e, deps):
        for k, v in deps.items():
            if self.seen[e].get(k, 0) >= v:
                continue
            self.eng[e].wait_ge(self.sem[k], v)
            self.seen[e][k] = v
            self.n_wait += 1
            self.log[e].append(("wait", k, v))

    def _deps(self, e, reads, writes, is_dma, disjoint=False):
        deps = {}
        own = "e:" + e

        def add(tok, raw=False):
            if tok is None:
                return
            k, v = tok
            if (not is_dma) and k == own and (e == "pe" or not (raw or STRICT)):
                return
            if deps.get(k, 0) < v:
                deps[k] = v

        loc = False
        for b in reads:
            add(b.w, raw=True)
            loc = loc or b.local
        for b in writes:
            if not (disjoint and b.w is not None and b.w[0] == own):
                add(b.w)
            for k, v in b.r.items():
                add((k, v))
            loc = loc or b.local
        if loc:
            add(self.phase_tok)
        return deps

    def _record(self, tok, reads, writes):
        k, v = tok
        for b in reads:
            if b.r.get(k, 0) < v:
                b.r[k] = v
        for b in writes:
            b.w = tok
            b.r = {}

    def op(self, e, fn, reads=(), writes=(), disjoint=False):
        ex = [b for b in reads if b.excl]
        if ex:
            reads = [b for b in reads if not b.excl]
            writes = list(writes) + ex
        self._wait(e, self._deps(e, reads, writes, False, disjoint))
        ins = fn(self.eng[e])
        self.cnt[e] += 1
        ins.then_inc(self.sem["e:" + e], 1)
        self.log[e].append(("inc", "e:" + e, 1))
        self._record(("e:" + e, self.cnt[e]), reads, writes)

    def dma(self, q, out, in_, reads=(), writes=(), slow=False):
        st = self.dq[q]
        i = st["i"]
        kk = len(st["keys"])
        key = st["keys"][i % kk]
        deps = self._deps(q, reads, writes, True)
        if i >= kk:
            deps[key] = max(deps.get(key, 0), 16 * (i // kk))
        self._wait(q, deps)
        if slow:
            ins = self.eng[q].dma_start(out=out, in_=in_, allow_slow_non_contiguous=True)
        else:
            ins = self.eng[q].dma_start(out=out, in_=in_)
        ins.then_inc(self.sem[key], 16)
        self.log[q].append(("inc", key, 16))
        st["i"] = i + 1
        self._record((key, 16 * (i // kk + 1)), reads, writes)

    def _all_tokens(self):
        deps = {}
        for e in self.eng:
            if self.cnt[e] > 0:
                deps["e:" + e] = self.cnt[e]
        for q, st in self.dq.items():
            kk = len(st["keys"])
            for j, key in enumerate(st["keys"]):
                uses = (st["i"] - j + kk - 1) // kk if st["i"] > j else 0
                if uses > 0:
                    deps[key] = 16 * uses
        return deps

    def barrier(self, scratch):
        deps = self._all_tokens()
        self._wait("dve", deps)
        ins = self.eng["dve"].memset(scratch, 0.0)
        self.cnt["dve"] += 1
        ins.then_inc(self.sem["e:dve"], 1)
        self.log["dve"].append(("inc", "e:dve", 1))
        self.phase_tok = ("e:dve", self.cnt["dve"])

    def finish(self):
        deps = self._all_tokens()
        self._wait("sp", deps)


def rope_consts():
    t = np.arange(TS)
    prow = (t // 64).astype(np.float64)
    pcol = (t % 64).astype(np.float64)
    inv = THETA ** (-np.arange(16, dtype=np.float64) / 16.0)
    cos = np.zeros((128, TS), np.float64)
    sin = np.zeros((128, TS), np.float64)
    for p in range(128):
        q = p % 64
        pos = prow if q < 32 else pcol
        f = inv[q % 16]
        cos[p] = np.cos(pos * f)
        sin[p] = np.sin(pos * f)
    R = np.zeros((128, 128), np.float32)
    for i in range(128):
        if i % 32 < 16:
            R[i, i + 16] = -1.0
        else:
            R[i, i - 16] = 1.0
    return cos.astype(np.float32), sin.astype(np.float32), np.ascontiguousarray(R.T), np.eye(128, dtype=np.float32)


def build_program(stop=None):
    nc = bass.Bass("TRN2", target_bir_lowering=False)
    P = Prog(nc)

    def din(name, shape):
        return nc.dram_tensor(name, list(shape), F32, kind="ExternalInput").ap()

    def dout(name, shape):
        return nc.dram_tensor(name, list(shape), F32, kind="ExternalOutput").ap()

    xs = din("xs", [TS, D])
    xp = din("xp", [TP, D])
    ckv_c = din("ckv_c", [DEPTH, PAST, 256])
    kr_c = din("kr_c", [DEPTH, PAST, 32])
    dk_c = din("dk_c", [DEPTH, PAST, 512])
    dv_c = din("dv_c", [DEPTH, PAST, 512])
    st_c = din("st_c", [DEPTH, 2, 512])
    cvec = din("cvec", [2, D])
    w_mod = din("w_mod", [DEPTH, D, 3 * D])
    b_mod = din("b_mod", [DEPTH, 3 * D])
    g_pre = din("g_pre", [DEPTH, D])
    g_post = din("g_post", [DEPTH, D])
    w_in = din("w_in", [DEPTH, D, IN_COLS])
    conv_w = din("conv_w", [DEPTH, 4, 512])
    conv_b = din("conv_b", [DEPTH, 512])
    w_rg_a = din("w_rg_a", [DEPTH, 2, 8, 64, 64])
    b_rg_a = din("b_rg_a", [DEPTH, 2, 512])
    w_rg_x = din("w_rg_x", [DEPTH, 2, 8, 64, 64])
    b_rg_x = din("b_rg_x", [DEPTH, 2, 512])
    rg_lam = din("rg_lam", [DEPTH, 2, 512])
    q_norm = din("q_norm", [DEPTH, 384])
    w_uq = din("w_uq", [DEPTH, 384, 768])
    kv_norm = din("kv_norm", [DEPTH, 256])
    w_ukv = din("w_ukv", [DEPTH, 256, 1024])
    lam_q1 = din("lam_q1", [DEPTH, 64])
    lam_k1 = din("lam_k1", [DEPTH, 64])
    lam_q2 = din("lam_q2", [DEPTH, 64])
    lam_k2 = din("lam_k2", [DEPTH, 64])
    diff_norm = din("diff_norm", [DEPTH, 128])
    w_br = [din("w_br_rnn", [DEPTH, 512, D]), din("w_br_mla", [DEPTH, 512, D]), din("w_br_diff", [DEPTH, 512, D])]
    w_out = din("w_out", [DEPTH, D, D])
    cosd_d = din("cosd", [128, TS])
    sind_d = din("sind", [128, TS])
    rdT_d = din("rdT", [128, 128])
    ident_d = din("ident", [128, 128])

    ys = dout("ys", [TS, D])
    yp = dout("yp", [TP, D])
    o_ckv = dout("o_ckv", [NPB, DEPTH, SP, 256])
    o_kr = dout("o_kr", [NPB, DEPTH, SP, 32])
    o_dk = dout("o_dk", [NPB, DEPTH, SP, 512])
    o_dv = dout("o_dv", [NPB, DEPTH, SP, 512])
    o_st = dout("o_st", [NPB, DEPTH, 2, 512])
    x1s = nc.dram_tensor("x1s", [TS, D], F32).ap()
    x1p = nc.dram_tensor("x1p", [TP, D], F32).ap()

    uid = {"n": 0}

    def un(name):
        uid["n"] += 1
        return "%s_%d" % (name, uid["n"])

    def sb(name, shape, dt=F32):
        return nc.alloc_sbuf_tensor(un("sb_" + name), list(shape), dt)

    ident = sb("ident", [128, 128]); b_ident = Buf("ident")
    rdT = sb("rdT", [128, 128]); b_rdT = Buf("rdT")
    cosd = sb("cosd_sb", [128, TS]); sind = sb("sind_sb", [128, TS]); b_tab = Buf("tab")
    ones_b = sb("ones_b", [128, 128], BF16); ones_f = sb("ones_f", [128, 128]); b_ones = Buf("ones")
    scratch1 = sb("scratch1", [128, 8])
    ccol = sb("ccol", [128, 8, 2]); scf = sb("scf", [128, 8, 2]); scb = sb("scb", [128, 8, 2], BF16)
    screp = sb("screp", [128, 2, 8, 128], BF16); b_sc = Buf("sc")
    hT = sb("hT", [128, 8, TS], BF16)
    b_hT = [Buf("hT%d" % g) for g in range(TS // 512)]
    NW = 3
    wbuf = [sb("wbuf%d" % i, [128, 8, 512], BF16) for i in range(NW)]
    b_wbuf = [Buf("wbuf%d" % i) for i in range(NW)]
    wstate = {"i": 0}
    bm = sb("bm", [128, 24]); gpre = sb("gpre", [128, 8]); modc = sb("modc", [128, 24, 2]); a1 = sb("a1", [128, 8, 2])
    tmp8 = sb("tmp8", [128, 8]); modraw = sb("modraw", [128, 48])
    gg = [sb("gg%d" % u, [128, D]) for u in range(2)]
    cw = sb("cw", [128, 4, 4]); cb = sb("cb", [128, 4]); bga = sb("bga", [128, 2, 4]); bgx = sb("bgx", [128, 2, 4])
    lam = sb("lam", [128, 2, 4]); sl = sb("sl", [128, 2, 4]); sl2 = sb("sl2", [128, 2, 4]); h0c = sb("h0c", [128, 2, 4])
    lt = [sb("lt%d" % i, [128, 2, 4]) for i in range(4)]
    bd = sb("bd", [128, 2, 2, 4, 128], BF16)
    qn = sb("qn", [128, 3]); kvn = sb("kvn", [128, 2]); kvn_bc = sb("kvn_bc", [128, 256])
    dn = sb("dn", [128, 1]); neglam = sb("neglam", [128, 1]); lamw = sb("lamw", [128, 4, 64]); lame = sb("lame", [128, 2])
    b_par = Buf("params")
    pall = nc.alloc_psum_tensor("pall", [128, 8 * 512], F32)
    ps = [pall[:, i * 512:(i + 1) * 512] for i in range(8)]
    b_ps = [Buf("ps%d" % i, excl=True) for i in range(8)]

    wcache = {}

    def cast_load(dst_ap, dst_buf, src_ap, key):
        if key is None:
            P.dma("pool", dst_ap, src_ap, writes=[dst_buf])
            return
        if key in wcache:
            sc, b_sc = wcache[key]
            P.dma("pool", dst_ap, sc, reads=[b_sc], writes=[dst_buf])
            return
        P.dma("pool", dst_ap, src_ap, writes=[dst_buf])
        sc = nc.dram_tensor(un("wsc"), list(dst_ap.shape), BF16).ap()
        b_sc = Buf("wsc")
        P.dma("sp", sc, dst_ap, reads=[dst_buf], writes=[b_sc])
        wcache[key] = (sc, b_sc)

    def load_w(src_ap, ncols, key):
        i = wstate["i"] % NW
        wstate["i"] += 1
        cast_load(wbuf[i][:, :, 0:ncols], b_wbuf[i], src_ap.rearrange("(c p) n -> p c n", p=128), key)
        return wbuf[i], b_wbuf[i]

    dbg = {}

    def dbg_dump(name, tile_ap, shape, reads):
        return

    def ring_slot():
        i = wstate["i"] % NW
        wstate["i"] += 1
        return wbuf[i], b_wbuf[i]

    def mm(ps_ap, pairs, reads, writes, start=True, stop=True):
        def fn(pe):
            n = len(pairs)
            ins = None
            for i, (lh, rh) in enumerate(pairs):
                ins = pe.matmul(ps_ap, lhsT=lh, rhs=rh, start=(start and i == 0), stop=(stop and i == n - 1))
            return ins
        P.op("pe", fn, reads=reads, writes=writes)

    def act(out, in_, func, reads, writes, dj=False, **kw):
        P.op("act", lambda e: e.activation(out=out, in_=in_, func=func, **kw), reads=reads, writes=writes, disjoint=dj)

    def dve(fn, reads, writes, dj=False):
        P.op("dve", fn, reads=reads, writes=writes, disjoint=dj)

    P.dma("sp", ident[:], ident_d, writes=[b_ident])
    P.dma("sp", rdT[:], rdT_d, writes=[b_rdT])
    P.dma("sp", cosd[:], cosd_d, writes=[b_tab])
    P.dma("sp", sind[:], sind_d, writes=[b_tab])
    dve(lambda e: e.memset(ones_b[:], 1.0), [], [b_ones])
    dve(lambda e: e.memset(ones_f[:], 1.0), [], [b_ones])
    for u in range(2):
        P.dma("sp", ccol[:, :, u], cvec[u].rearrange("(c p) -> p c", p=128), writes=[b_sc], slow=True)
    act(scf[:], ccol[:], AF.Silu, [b_sc], [b_sc])
    act(scb[:], scf[:], AF.Identity, [b_sc], [b_sc])
    for u in range(2):
        for k in range(8):
            act(screp[:, u, k, :], ones_f[:], AF.Identity, [b_sc, b_ones], [b_sc], scale=scf[:, k, u:u + 1])

    units = [
        dict(name="s", u=0, T=TS, nseq=1, Ts=TS, ctx=PAST, rope=True, x=[xs, x1s, ys]),
        dict(name="p", u=1, T=TP, nseq=NPB, Ts=SP, ctx=0, rope=False, x=[xp, x1p, yp]),
    ]
    b_x1 = {("s", t): Buf("x1s%d" % t) for t in range(TS // 128)}
    b_x1.update({("p", t): Buf("x1p%d" % t) for t in range(TP // 128)})

    def layer_params(l):
        lam_init = 0.8 - 0.6 * math.exp(-0.3 * l)
        P.barrier(scratch1[:, 0:1])
        rd = [b_par]
        ldb = []

        def nb():
            ldb.append(Buf("pload", local=True))
            return ldb[-1]
        P.dma("sp", bm[:], b_mod[l].rearrange("(c p) -> p c", p=128), writes=[nb()], slow=True)
        P.dma("sp", gpre[:], g_pre[l].rearrange("(c p) -> p c", p=128), writes=[nb()], slow=True)
        for k in range(4):
            P.dma("sp", cw[:, :, k], conv_w[l, k].rearrange("(c p) -> p c", p=128), writes=[nb()], slow=True)
        P.dma("sp", cb[:], conv_b[l].rearrange("(c p) -> p c", p=128), writes=[nb()], slow=True)
        for d in range(2):
            P.dma("sp", bga[:, d, :], b_rg_a[l, d].rearrange("(c p) -> p c", p=128), writes=[nb()], slow=True)
        for d in range(2):
            P.dma("sp", bgx[:, d, :], b_rg_x[l, d].rearrange("(c p) -> p c", p=128), writes=[nb()], slow=True)
        for d in range(2):
            P.dma("sp", lam[:, d, :], rg_lam[l, d].rearrange("(c p) -> p c", p=128), writes=[nb()], slow=True)
        for d in range(2):
            P.dma("sp", h0c[:, d, :], st_c[l, d].rearrange("(c p) -> p c", p=128), writes=[nb()], slow=True)
        P.dma("sp", qn[:], q_norm[l].rearrange("(c p) -> p c", p=128), writes=[nb()], slow=True)
        P.dma("sp", kvn[:], kv_norm[l].rearrange("(c p) -> p c", p=128), writes=[nb()], slow=True)
        P.dma("sp", kvn_bc[:], kv_norm[l:l + 1, :].partition_broadcast(128), writes=[nb()])
        P.dma("sp", dn[:], diff_norm[l].rearrange("(c p) -> p c", p=128), writes=[nb()], slow=True)
        for i, t in enumerate((lam_q1, lam_k1, lam_q2, lam_k2)):
            P.dma("sp", lamw[:, i, :], t[l:l + 1, :].partition_broadcast(128), writes=[nb()])
        b_bd = Buf("bd0", local=True)
        dve(lambda e: e.memset(bd[:], 0.0), [], [b_bd])
        for gi, wg in enumerate((w_rg_a, w_rg_x)):
            for d in range(2):
                for hh in range(2):
                    src = wg[l, d].rearrange("(c two) k j -> two k c j", two=2)[hh]
                    P.dma("pool", bd[hh * 64:(hh + 1) * 64, d, gi, :, hh * 64:(hh + 1) * 64], src, reads=[b_bd], writes=[nb()])
        dve(lambda e: e.memset(tmp8[:, 0:1], 0.0), ldb + [b_bd], [b_par])
        act(lt[0][:], lam[:], AF.Exp, rd, [b_par], scale=-1.0)
        dve(lambda e: e.tensor_scalar(out=lt[1][:], in0=lt[0][:], scalar1=2.0, scalar2=None, op0=ALU.add), rd, [b_par])
        dve(lambda e: e.reciprocal(out=lt[1][:], in_=lt[1][:]), rd, [b_par])
        dve(lambda e: e.tensor_tensor(out=lt[0][:], in0=lt[0][:], in1=lt[1][:], op=ALU.mult), rd, [b_par])
        dve(lambda e: e.tensor_tensor(out=lt[1][:], in0=lt[0][:], in1=lt[0][:], op=ALU.mult), rd, [b_par])
        dve(lambda e: e.tensor_scalar(out=lt[2][:], in0=lt[1][:], scalar1=1.0 / 11.0, scalar2=1.0 / 9.0, op0=ALU.mult, op1=ALU.add), rd, [b_par])
        for cst in (1.0 / 7.0, 1.0 / 5.0, 1.0 / 3.0, 1.0):
            dve(lambda e: e.tensor_tensor(out=lt[2][:], in0=lt[2][:], in1=lt[1][:], op=ALU.mult), rd, [b_par])
            dve(lambda e, cst=cst: e.tensor_scalar(out=lt[2][:], in0=lt[2][:], scalar1=cst, scalar2=None, op0=ALU.add), rd, [b_par])
        dve(lambda e: e.tensor_tensor(out=lt[2][:], in0=lt[2][:], in1=lt[0][:], op=ALU.mult), rd, [b_par])
        dve(lambda e: e.tensor_scalar(out=sl[:], in0=lt[2][:], scalar1=-16.0, scalar2=None, op0=ALU.mult), rd, [b_par])
        dve(lambda e: e.tensor_scalar(out=sl2[:], in0=lt[2][:], scalar1=-32.0, scalar2=None, op0=ALU.mult), rd, [b_par])
        dve(lambda e: e.tensor_tensor(out=lamw[:, 0, :], in0=lamw[:, 0, :], in1=lamw[:, 1, :], op=ALU.mult), rd, [b_par])
        dve(lambda e: e.tensor_tensor(out=lamw[:, 2, :], in0=lamw[:, 2, :], in1=lamw[:, 3, :], op=ALU.mult), rd, [b_par])
        dve(lambda e: e.reduce_sum(out=lame[:, 0:1], in_=lamw[:, 0, :], axis=AX.X), rd, [b_par])
        dve(lambda e: e.reduce_sum(out=lame[:, 1:2], in_=lamw[:, 2, :], axis=AX.X), rd, [b_par])
        act(lame[:], lame[:], AF.Exp, rd, [b_par])
        dve(lambda e: e.tensor_tensor(out=neglam[:], in0=lame[:, 1:2], in1=lame[:, 0:1], op=ALU.subtract), rd, [b_par])
        dve(lambda e: e.tensor_scalar(out=neglam[:], in0=neglam[:], scalar1=-lam_init, scalar2=None, op0=ALU.add), rd, [b_par])
        dve(lambda e: e.tensor_scalar(out=dn[:], in0=dn[:], scalar1=1.0 - lam_init, scalar2=None, op0=ALU.mult), rd, [b_par])

        with ExitStack() as es:
            bmg = es.enter_context(nc.sbuf_tensor(un("bmg"), [128, D], F32)); gpb = es.enter_context(nc.sbuf_tensor(un("gpb"), [128, D], F32))
            b_l = Buf("modl", local=True)
            P.dma("sp", bmg[:], b_mod[l:l + 1, 2 * D:3 * D].partition_broadcast(128), writes=[b_l])
            P.dma("sp", gpb[:], g_post[l:l + 1, :].partition_broadcast(128), writes=[b_l])
            psm = ps[0][:].rearrange("p (c u) -> p c u", u=2)
            for pc in range(6):
                wt, bw = load_w(w_mod[l][:, pc * 512:(pc + 1) * 512], 512, None)
                for cc in range(4):
                    col = pc * 4 + cc
                    mm(psm[:, col, :], [(wt[:, k, cc * 128:(cc + 1) * 128], scb[:, k, :]) for k in range(8)],
                       [bw, b_sc], [b_ps[0]])
                if pc >= 4:
                    half = pc - 4
                    for u in range(2):
                        bank = 1 + u * 2 + half
                        mm(ps[bank][:], [(screp[:, u, k, :], wt[:, k, :]) for k in range(8)], [bw, b_sc], [b_ps[bank]])
            dve(lambda e: e.tensor_copy(out=modraw[:], in_=ps[0][:, 0:48]), [b_ps[0]], [b_par])
            dbg_dump("modraw", modraw[:], [128, 48], [b_par])
            mrv = modraw[:].rearrange("p (c u) -> p c u", u=2)
            for u in range(2):
                dve(lambda e, u=u: e.tensor_tensor(out=modc[:, :, u], in0=mrv[:, :, u], in1=bm[:], op=ALU.add), [b_par], [b_par])
                dve(lambda e, u=u: e.tensor_scalar(out=tmp8[:], in0=modc[:, 8:16, u], scalar1=1.0, scalar2=None, op0=ALU.add), rd, [b_par])
                dve(lambda e, u=u: e.tensor_tensor(out=a1[:, :, u], in0=tmp8[:], in1=gpre[:], op=ALU.mult), rd, [b_par])
                for half in range(2):
                    bank = 1 + u * 2 + half
                    hs = slice(half * 512, (half + 1) * 512)
                    dve(lambda e, u=u, bank=bank, hs=hs: e.tensor_tensor(out=gg[u][:, hs], in0=ps[bank][:], in1=bmg[:, hs], op=ALU.add), [b_ps[bank], b_l], [b_par])
                    dve(lambda e, u=u, hs=hs: e.tensor_tensor(out=gg[u][:, hs], in0=gg[u][:, hs], in1=gpb[:, hs], op=ALU.mult), [b_l, b_par], [b_par])
            P.barrier(scratch1[:, 0:1])
        if l == 0:
            dbg_dump("modc", modc[:], [128, 24, 2], [b_par])
            dbg_dump("a1", a1[:], [128, 8, 2], [b_par])
            dbg_dump("gg1", gg[1][:], [128, D], [b_par])
            dbg_dump("scf", scf[:], [128, 8, 2], [b_sc])
            dbg_dump("sl", sl[:], [128, 2, 4], [b_par])
            dbg_dump("neglam", neglam[:], [128, 1], [b_par])

    def phase_h(l, U):
        u = U["u"]
        xsrc = U["x"][l]
        with ExitStack() as es:
            xt = [es.enter_context(nc.sbuf_tensor(un("xt%d" % i), [128, 4, D], F32)) for i in range(2)]
            b_xt = [Buf("xt%d" % i, local=True) for i in range(2)]
            junk = es.enter_context(nc.sbuf_tensor(un("junk"), [128, D], F32)); b_junk = Buf("junk", local=True)
            ssq = [es.enter_context(nc.sbuf_tensor(un("ssq%d" % i), [128, 4], F32)) for i in range(2)]
            b_ss = [Buf("ssq%d" % i, local=True) for i in range(2)]
            for g in range(U["T"] // 512):
                i = g % 2
                rdx = [b_x1[(U["name"], 4 * g + j)] for j in range(4)] if l > 0 else []
                P.dma("sp", xt[i][:], xsrc[g * 512:(g + 1) * 512, :].rearrange("(j p) d -> p j d", p=128), reads=rdx, writes=[b_xt[i]])
                for j in range(4):
                    act(junk[:], xt[i][:, j, :], AF.Square, [b_xt[i]], [b_junk, b_ss[i]], accum_out=ssq[i][:, j:j + 1])
                act(ssq[i][:], ssq[i][:], AF.Sqrt, [b_ss[i]], [b_ss[i]], scale=1.0 / D, bias=EPS)
                dve(lambda e, i=i: e.reciprocal(out=ssq[i][:], in_=ssq[i][:]), [b_ss[i]], [b_ss[i]])
                for j in range(4):
                    dve(lambda e, i=i, j=j: e.tensor_scalar(out=xt[i][:, j, :], in0=xt[i][:, j, :], scalar1=ssq[i][:, j:j + 1], scalar2=None, op0=ALU.mult),
                        [b_ss[i], b_xt[i]], [b_xt[i]])
                for c in range(8):
                    bank = c % 4

                    def tr(pe, i=i, c=c, bank=bank):
                        ins = None
                        for j in range(4):
                            ins = pe.transpose(ps[bank][:, j * 128:(j + 1) * 128], xt[i][:, j, c * 128:(c + 1) * 128], ident[:])
                        return ins
                    P.op("pe", tr, reads=[b_xt[i], b_ident], writes=[b_ps[bank]])
                    dst = hT[:, c, g * 512:(g + 1) * 512]
                    if c % 2 == 0:
                        act(dst, ps[bank][:], AF.Identity, [b_ps[bank], b_par], [b_hT[g]], scale=a1[:, c, u:u + 1], bias=modc[:, c, u:u + 1])
                    else:
                        dve(lambda e, dst=dst, bank=bank, c=c: e.tensor_scalar(out=dst, in0=ps[bank][:], scalar1=a1[:, c, u:u + 1], scalar2=modc[:, c, u:u + 1], op0=ALU.mult, op1=ALU.add),
                            [b_ps[bank], b_par], [b_hT[g]], dj=True)
            P.barrier(scratch1[:, 0:1])
            if l == 0 and U["name"] == "p":
                dbg_dump("hT", hT[:, :, 0:1024], [128, 8, 1024], hTr if False else b_hT[:2])

    def phase_rnn(l, U, yr, b_yr):
        T, nseq, Ts = U["T"], U["nseq"], U["Ts"]
        G = T // 512
        hTr = b_hT[:G]
        with ExitStack() as es:
            def loc(name, shape, dt=F32):
                return es.enter_context(nc.sbuf_tensor(un(name), list(shape), dt)), Buf(name, local=True)
            rx, b_rx = loc("rx", [128, T]); xc, b_xc = loc("xc", [128, T]); xcb, b_xcb = loc("xcb", [128, T], BF16)
            A, b_A = loc("rA", [128, T]); I, b_I = loc("rI", [128, T]); S, b_S = loc("rS", [128, T])
            A1, b_A1 = loc("rA1", [128, T]); I1, b_I1 = loc("rI1", [128, T])
            srg0, b_srg0 = loc("srg0", [128, 512]); srg1, b_srg1 = loc("srg1", [128, 512]); stc, b_stc0 = loc("stc", [128, 4, NPB, 2])
            b_stcs = [b_stc0] + [Buf("stc%d" % i, local=True) for i in range(1, 4)]
            wrx, bwrx = load_w(w_in[l][:, C_RX:C_RX + 512], 512, ("in", l, C_RX))
            wrg, bwrg = load_w(w_in[l][:, C_RG:C_RG + 512], 512, ("in", l, C_RG))
            for c in range(4):
                cs = slice(c * 128, (c + 1) * 128)
                for g in range(G):
                    gs = slice(g * 512, (g + 1) * 512)
                    bank = g % 4
                    mm(ps[bank][:], [(wrx[:, k, cs], hT[:, k, gs]) for k in range(8)], [bwrx, hTr[g]], [b_ps[bank]])
                    act(rx[:, gs], ps[bank][:], AF.Identity, [b_ps[bank]], [b_rx])
                dve(lambda e, c=c: e.tensor_scalar(out=xc[:], in0=rx[:], scalar1=cw[:, c, 2:3], scalar2=cb[:, c:c + 1], op0=ALU.mult, op1=ALU.add), [b_rx, b_par], [b_xc])
                for s in range(nseq):
                    t0 = s * Ts
                    for k, off in ((0, -2), (1, -1), (3, 1)):
                        if off < 0:
                            o_sl = slice(t0 - off, t0 + Ts); i_sl = slice(t0, t0 + Ts + off)
                        else:
                            o_sl = slice(t0, t0 + Ts - off); i_sl = slice(t0 + off, t0 + Ts)
                        dve(lambda e, c=c, k=k, o_sl=o_sl, i_sl=i_sl: e.scalar_tensor_tensor(out=xc[:, o_sl], in0=rx[:, i_sl], scalar=cw[:, c, k:k + 1], in1=xc[:, o_sl], op0=ALU.mult, op1=ALU.add),
                            [b_rx, b_par, b_xc], [b_xc])
                act(xcb[:], xc[:], AF.Identity, [b_xc], [b_xcb])
                Ad = [(A, b_A), (A1, b_A1)]
                Id = [(I, b_I), (I1, b_I1)]
                Td = [(rx, b_rx), (S, b_S)]
                for d in range(2):
                    A_, b_A_ = Ad[d]; I_, b_I_ = Id[d]; T_, b_T_ = Td[d]
                    for gi, (dst, b_dst, bias_t) in enumerate(((A_, b_A_, bga), (I_, b_I_, bgx))):
                        for g in range(G):
                            gs = slice(g * 512, (g + 1) * 512)
                            bank = 4 + (g % 4)
                            mm(ps[bank][:], [(bd[:, d, gi, c, :], xcb[:, gs])], [b_par, b_xcb], [b_ps[bank]])
                            act(dst[:, gs], ps[bank][:], AF.Sigmoid, [b_ps[bank], b_par], [b_dst], bias=bias_t[:, d, c:c + 1])
                    act(T_[:], A_[:], AF.Exp, [b_A_, b_par], [b_T_], scale=sl2[:, d, c:c + 1])
                    act(A_[:], A_[:], AF.Exp, [b_A_, b_par], [b_A_], scale=sl[:, d, c:c + 1])
                    act(T_[:], T_[:], AF.Sqrt, [b_T_], [b_T_], scale=-1.0, bias=1.0)
                for d in range(2):
                    A_, b_A_ = Ad[d]; I_, b_I_ = Id[d]; T_, b_T_ = Td[d]
                    dve(lambda e, I_=I_, T_=T_: e.tensor_tensor(out=I_[:], in0=I_[:], in1=T_[:], op=ALU.mult), [b_I_, b_T_], [b_I_])
                    dve(lambda e, I_=I_: e.tensor_tensor(out=I_[:], in0=I_[:], in1=xc[:], op=ALU.mult), [b_I_, b_xc], [b_I_])
                    for s in range(nseq):
                        ss_ = slice(s * Ts, (s + 1) * Ts)
                        init = h0c[:, d, c:c + 1] if U["ctx"] else 0.0
                        if d == 0:
                            dve(lambda e, ss_=ss_, init=init, T_=T_, A_=A_, I_=I_: e.tensor_tensor_scan(out=T_[:, ss_], data0=A_[:, ss_], data1=I_[:, ss_], initial=init, op0=ALU.mult, op1=ALU.add),
                                [b_A_, b_I_, b_par], [b_T_])
                        else:
                            rs_ = slice((s + 1) * Ts - 1, s * Ts - 1 if s > 0 else None, -1)
                            dve(lambda e, rs_=rs_, init=init, T_=T_, A_=A_, I_=I_: e.tensor_tensor_scan(out=T_[:, rs_], data0=A_[:, rs_], data1=I_[:, rs_], initial=init, op0=ALU.mult, op1=ALU.add),
                                [b_A_, b_I_, b_par], [b_T_])
                if not U["ctx"]:
                    for s in range(nseq):
                        dve(lambda e, s=s, c=c: e.tensor_copy(out=stc[:, c, s, 0:1], in_=rx[:, (s + 1) * Ts - 1:(s + 1) * Ts]), [b_rx], [b_stcs[c]])
                        dve(lambda e, s=s, c=c: e.tensor_copy(out=stc[:, c, s, 1:2], in_=S[:, s * Ts:s * Ts + 1]), [b_S], [b_stcs[c]])
                if not U["ctx"]:
                    for s in range(nseq):
                        P.dma("sp", o_st[s, l, :, c * 128:(c + 1) * 128].rearrange("d p -> p d"), stc[:, c, s, :], reads=[b_stcs[c]], slow=True)
                dve(lambda e: e.tensor_tensor(out=rx[:], in0=rx[:], in1=S[:], op=ALU.add), [b_rx, b_S], [b_rx])
                for g in range(G):
                    gs = slice(g * 512, (g + 1) * 512)
                    bank = g % 4
                    mm(ps[bank][:], [(wrg[:, k, cs], hT[:, k, gs]) for k in range(8)], [bwrg, hTr[g]], [b_ps[bank]])
                    srg, b_srg = (srg0, b_srg0) if g % 2 == 0 else (srg1, b_srg1)
                    act(srg[:], ps[bank][:], AF.Silu, [b_ps[bank]], [b_srg])
                    dve(lambda e, gs=gs, c=c, srg=srg: e.tensor_tensor(out=yr[:, c, gs], in0=rx[:, gs], in1=srg[:], op=ALU.mult), [b_rx, b_srg], [b_yr])
            P.barrier(scratch1[:, 0:1])

    def rope_evac(dst, psrc, bank_r, g, rows, reads, writes, xr, b_xr, t1, b_t1):
        gs = slice(g * 512, (g + 1) * 512)
        n = rows.stop - rows.start
        act(xr[rows, :], psrc, AF.Identity, reads, [b_xr])
        mm(ps[bank_r][rows, :], [(rdT[rows, rows], xr[rows, :])], [b_xr, b_rdT], [b_ps[bank_r]])
        dve(lambda e: e.tensor_tensor(out=t1[rows, :], in0=ps[bank_r][rows, :], in1=sind[rows, gs], op=ALU.mult), [b_ps[bank_r], b_tab], [b_t1])
        dve(lambda e: e.tensor_tensor(out=xr[rows, :], in0=xr[rows, :], in1=cosd[rows, gs], op=ALU.mult), [b_xr, b_tab], [b_xr])
        dve(lambda e: e.tensor_tensor(out=dst, in0=xr[rows, :], in1=t1[rows, :], op=ALU.add), [b_xr, b_t1], writes)

    def phase_mla(l, U, ym, b_ym):
        T, nseq, Ts, ctx = U["T"], U["nseq"], U["Ts"], U["ctx"]
        G = T // 512
        Tk = Ts + ctx
        KT = Tk // 128
        TkA = nseq * Tk
        hTr = b_hT[:G]
        with ExitStack() as es:
            def loc(name, shape, dt=F32):
                return es.enter_context(nc.sbuf_tensor(un(name), list(shape), dt)), Buf(name, local=True)
            cqn, b_cqn = loc("cqn", [128, 3, T], BF16)
            ckvn, b_ckvn = loc("ckvn", [128, 2, TkA], BF16)
            Kh = []; b_Kh = []
            for i in range(2):
                t, b = loc("Kh%d" % i, [128, TkA], BF16); Kh.append(t); b_Kh.append(b)
            Qh = []; b_Qh = []
            for i in range(2):
                t, b = loc("Qh%d" % i, [128, T], BF16); Qh.append(t); b_Qh.append(b)
            Vt = []; b_Vt = []
            for i in range(2):
                t, b = loc("Vt%d" % i, [128, nseq * KT, 2, 128], BF16); Vt.append(t); b_Vt.append(b)
            PT2a, b_PT2a = loc("PT2a", [128, 1024], BF16); PT2b, b_PT2b = loc("PT2b", [128, 1024], BF16)
            xr, b_xr = loc("xr", [128, 512]); t1, b_t1 = loc("t1", [128, 512])
            xr2, b_xr2 = loc("xr2", [128, 512]); t12, b_t12 = loc("t12", [128, 512])
            sq, b_sq = loc("sq", [128, 3, 512], BF16)
            rs, b_rs = loc("rs", [128, 512])
            rc, b_rc = loc("rc", [128, 512]); ot, b_ot = loc("ot", [128, 512])
            wq2, b_wq2 = loc("wq2", [128, 3, 8, 128], BF16)
            wkv2, b_wkv2 = loc("wkv2", [128, 2, 8, 128], BF16)
            wkr2, b_wkr2 = loc("wkr2", [128, 8, 64], BF16)
            ost, b_ost = loc("ost", [128, 288]); ssv, b_ssv = loc("ssv", [128, 2])
            es2 = es.enter_context(ExitStack())

            def loc2(name, shape, dt=F32):
                return es2.enter_context(nc.sbuf_tensor(un(name), list(shape), dt)), Buf(name, local=True)
            wq_slot, b_wq_raw = ring_slot()
            wq_raw = wq_slot[:].rearrange("p c n -> p (c n)")[:, 0:2304].rearrange("p (c n) -> p c n", n=768)
            wkv_slot, b_wkv_raw = ring_slot()
            wkv_raw = wkv_slot[:].rearrange("p c n -> p (c n)")[:, 0:2048].rearrange("p (c n) -> p c n", n=1024)
            cst, b_cst = loc2("cst", [128, 4, 256]); krst, b_krst = loc2("krst", [128, 4, 64])
            krT = Kh

            cast_load(wq_raw, b_wq_raw, w_uq[l].rearrange("(c p) n -> p c n", p=128), ("uq", l))
            cast_load(wkv_raw, b_wkv_raw, w_ukv[l].rearrange("(c p) n -> p c n", p=128), ("ukv", l))
            wq_v = wq_raw.rearrange("p c (h e) -> p c h e", e=96)
            dve(lambda e: e.memset(wq2[:], 0.0), [], [b_wq2])
            for c in range(3):
                dve(lambda e, c=c: e.tensor_copy(out=wq2[:, c, :, 0:64:2], in_=wq_v[:, c, :, 64:96]), [b_wq_raw], [b_wq2])
                dve(lambda e, c=c: e.tensor_copy(out=wq2[:, c, :, 64:128], in_=wq_v[:, c, :, 0:64]), [b_wq_raw], [b_wq2])
            wkv_v = wkv_raw.rearrange("p c (h e) -> p c h e", e=128)
            for c in range(2):
                dve(lambda e, c=c: e.tensor_copy(out=wkv2[:, c, :, 0:64], in_=wkv_v[:, c, :, 64:128]), [b_wkv_raw], [b_wkv2])
                dve(lambda e, c=c: e.tensor_copy(out=wkv2[:, c, :, 64:128], in_=wkv_v[:, c, :, 0:64]), [b_wkv_raw], [b_wkv2])
            wcq, bwcq = load_w(w_in[l][:, C_CQ:C_CQ + 384], 384, ("in", l, C_CQ))
            wck, bwck = load_w(w_in[l][:, C_CKV:C_CKV + 288], 288, ("in", l, C_CKV))
            dve(lambda e: e.memset(wkr2[:], 0.0), [], [b_wkr2])
            dve(lambda e: e.tensor_copy(out=wkr2[:, :, 0:64:2], in_=wck[:, :, 256:288]), [bwck], [b_wkr2])
            wmg, bwmg = load_w(w_in[l][:, C_MG:C_MG + 512], 512, ("in", l, C_MG))
            for i in range(2):
                dve(lambda e, i=i: e.memset(Vt[i][:], 1.0), [], [b_Vt[i]])

            def koff(s, t):
                return s * Tk + t
            if ctx:
                P.dma("sp", cst[:], ckv_c[l].rearrange("(j p) d -> p j d", p=128), writes=[b_cst])
                dve(lambda e: e.memset(krst[:], 0.0), [], [b_krst])
                krraw, b_krraw = loc2("krraw", [128, 4, 32])
                P.dma("sp", krraw[:], kr_c[l].rearrange("(j p) d -> p j d", p=128), writes=[b_krraw])
                dve(lambda e: e.tensor_copy(out=krst[:, :, 0:64:2], in_=krraw[:]), [b_krraw, b_krst], [b_krst])
                for j2 in range(2):
                    def tr(pe, j2=j2):
                        ins = None
                        for j in range(4):
                            ins = pe.transpose(ps[j2][:, j * 128:(j + 1) * 128], cst[:, j, j2 * 128:(j2 + 1) * 128], ident[:])
                        return ins
                    P.op("pe", tr, reads=[b_cst, b_ident], writes=[b_ps[j2]])
                    act(ckvn[:, j2, 0:512], ps[j2][:], AF.Identity, [b_ps[j2]], [b_ckvn])

                def tr2(pe):
                    ins = None
                    for j in range(4):
                        ins = pe.transpose(ps[2][0:64, j * 128:(j + 1) * 128], krst[:, j, :], ident[:])
                    return ins
                P.op("pe", tr2, reads=[b_krst, b_ident], writes=[b_ps[2]])
                for i in range(2):
                    act(Kh[i][0:64, 0:512], ps[2][0:64, :], AF.Identity, [b_ps[2]], [b_Kh[i]])

            for g in range(G):
                gs = slice(g * 512, (g + 1) * 512)
                for j in range(3):
                    mm(ps[j][:], [(wcq[:, k, j * 128:(j + 1) * 128], hT[:, k, gs]) for k in range(8)], [bwcq, hTr[g]], [b_ps[j]])
                    act(sq[:, j, :], ps[j][:], AF.Square, [b_ps[j]], [b_sq])
                mm(ps[3][:], [(ones_b[:], sq[:, j, :]) for j in range(3)], [b_ones, b_sq], [b_ps[3]])
                act(rs[:], ps[3][:], AF.Ln, [b_ps[3]], [b_rs], scale=1.0 / 384.0, bias=EPS)
                act(rs[:], rs[:], AF.Exp, [b_rs], [b_rs], scale=-0.5)
                for j in range(3):
                    dve(lambda e, j=j, gs=gs: e.scalar_tensor_tensor(out=cqn[:, j, gs], in0=ps[j][:], scalar=qn[:, j:j + 1], in1=rs[:], op0=ALU.mult, op1=ALU.mult),
                        [b_ps[j], b_par, b_rs], [b_cqn])
                for j in range(2):
                    mm(ps[4 + j][:], [(wck[:, k, j * 128:(j + 1) * 128], hT[:, k, gs]) for k in range(8)], [bwck, hTr[g]], [b_ps[4 + j]])
                    act(sq[:, j, :], ps[4 + j][:], AF.Square, [b_ps[4 + j]], [b_sq])
                mm(ps[6][:], [(ones_b[:], sq[:, j, :]) for j in range(2)], [b_ones, b_sq], [b_ps[6]])
                act(rs[:], ps[6][:], AF.Ln, [b_ps[6]], [b_rs], scale=1.0 / 256.0, bias=EPS)
                act(rs[:], rs[:], AF.Exp, [b_rs], [b_rs], scale=-0.5)
                if nseq == 1:
                    kdst = [(slice(ctx + g * 512, ctx + (g + 1) * 512), slice(0, 512))]
                else:
                    kdst = [(slice((2 * g + h2) * Tk, (2 * g + h2 + 1) * Tk), slice(h2 * 256, (h2 + 1) * 256)) for h2 in range(2)]
                for j in range(2):
                    for (kd_, sd_) in kdst:
                        dve(lambda e, j=j, kd_=kd_, sd_=sd_: e.scalar_tensor_tensor(out=ckvn[:, j, kd_], in0=ps[4 + j][:, sd_], scalar=kvn[:, j:j + 1], in1=rs[:, sd_], op0=ALU.mult, op1=ALU.mult),
                            [b_ps[4 + j], b_par, b_rs], [b_ckvn])
                mm(ps[7][0:64, :], [(wkr2[:, k, :], hT[:, k, gs]) for k in range(8)], [b_wkr2, hTr[g]], [b_ps[7]])
                for (kd_, sd_) in kdst:
                    if U["rope"]:
                        rope_evac(Kh[0][0:64, kd_], ps[7][0:64, :], 3, g, slice(0, 64), [b_ps[7]], [b_Kh[0]], xr, b_xr, t1, b_t1)
                        act(Kh[1][0:64, kd_], Kh[0][0:64, kd_], AF.Identity, [b_Kh[0]], [b_Kh[1]])
                    else:
                        for i in range(2):
                            act(Kh[i][0:64, kd_], ps[7][0:64, sd_], AF.Identity, [b_ps[7]], [b_Kh[i]])
                for j in range(4):
                    bank = j % 3
                    mm(ps[bank][:], [(wmg[:, k, j * 128:(j + 1) * 128], hT[:, k, gs]) for k in range(8)], [bwmg, hTr[g]], [b_ps[bank]])
                    act(ym[:, j, gs], ps[bank][:], AF.Silu, [b_ps[bank]], [b_ym], dj=True)

            if not ctx:
                for tt in range(T // 128):
                    ts_ = slice(tt * 128, (tt + 1) * 128)
                    bank = 4 + tt % 2
                    s, r0 = divmod(tt * 128, Ts)
                    mm(ps[bank][:, 0:288], [(hT[:, k, ts_], wck[:, k, 0:288]) for k in range(8)], [bwck, hTr[tt // 4]], [b_ps[bank]])
                    act(ost[:, 0:256], ps[bank][:, 0:256], AF.Square, [b_ps[bank]], [b_ost, b_ssv], accum_out=ssv[:, 0:1])
                    act(ost[:, 256:288], ps[bank][:, 256:288], AF.Identity, [b_ps[bank]], [b_ost])
                    act(ssv[:, 0:1], ssv[:, 0:1], AF.Sqrt, [b_ssv], [b_ssv], scale=1.0 / 256.0, bias=EPS)
                    dve(lambda e: e.reciprocal(out=ssv[:, 0:1], in_=ssv[:, 0:1]), [b_ssv], [b_ssv])
                    dve(lambda e, bank=bank: e.scalar_tensor_tensor(out=ost[:, 0:256], in0=ps[bank][:, 0:256], scalar=ssv[:, 0:1], in1=kvn_bc[:], op0=ALU.mult, op1=ALU.mult),
                        [b_ps[bank], b_ssv, b_par], [b_ost])
                    P.dma("sp", o_ckv[s, l, r0:r0 + 128, :], ost[:, 0:256], reads=[b_ost])
                    P.dma("sp", o_kr[s, l, r0:r0 + 128, :], ost[:, 256:288], reads=[b_ost])

            NQ = 512 if nseq == 1 else Ts
            nq_groups = T // NQ
            PT = [PT2a, PT2b]
            b_PT = [b_PT2a, b_PT2b]

            def prep(h):
                hi = h % 2
                if hi == 0:
                    vi = (h // 2) % 2
                    for kt in range(nseq * KT):
                        bank = 6 + kt % 2
                        ks = slice(kt * 128, (kt + 1) * 128)
                        pv = ps[bank][:, 0:128].rearrange("p (a e) -> p a e", e=64)
                        mm(pv, [(ckvn[:, r, ks], wkv2[:, r, h:h + 2, 0:64]) for r in range(2)], [b_ckvn, b_wkv2], [b_ps[bank]])
                        dve(lambda e, vi=vi, kt=kt, pv=pv: e.tensor_copy(out=Vt[vi][:, kt, 0, 0:64], in_=pv[:, 0, :]), [b_ps[bank]], [b_Vt[vi]], dj=True)
                        dve(lambda e, vi=vi, kt=kt, pv=pv: e.tensor_copy(out=Vt[vi][:, kt, 1, 64:128], in_=pv[:, 1, :]), [b_ps[bank]], [b_Vt[vi]], dj=True)
                        yield
                for kg in range(TkA // 512):
                    ks = slice(kg * 512, (kg + 1) * 512)
                    bank = 6 + kg % 2
                    mm(ps[bank][:], [(wkv2[:, r, h, :], ckvn[:, r, ks]) for r in range(2)], [b_wkv2, b_ckvn], [b_ps[bank]])
                    dve(lambda e, ks=ks, bank=bank, hi=hi: e.tensor_copy(out=Kh[hi][64:128, ks], in_=ps[bank][64:128, :]), [b_ps[bank]], [b_Kh[hi]], dj=True)
                    yield
                for g in range(G):
                    gs = slice(g * 512, (g + 1) * 512)
                    bank = 6 + g % 2
                    mm(ps[bank][:], [(wq2[:, c, h, :], cqn[:, c, gs]) for c in range(3)], [b_wq2, b_cqn], [b_ps[bank]])
                    dve(lambda e, gs=gs, bank=bank, hi=hi: e.tensor_copy(out=Qh[hi][64:128, gs], in_=ps[bank][64:128, :]), [b_ps[bank]], [b_Qh[hi]], dj=True)
                    if U["rope"]:
                        rows = slice(0, 64)
                        xq, b_xq = (xr, b_xr) if g % 2 == 0 else (xr2, b_xr2)
                        tq, b_tq = (t1, b_t1) if g % 2 == 0 else (t12, b_t12)
                        dve(lambda e, xq=xq, bank=bank: e.tensor_copy(out=xq[rows, :], in_=ps[bank][0:64, :]), [b_ps[bank]], [b_xq])
                        yield
                        rb = 7 - g % 2
                        mm(ps[rb][rows, :], [(rdT[rows, rows], xq[rows, :])], [b_xq, b_rdT], [b_ps[rb]])
                        yield
                        dve(lambda e, tq=tq, rb=rb, gs=gs: e.tensor_tensor(out=tq[rows, :], in0=ps[rb][rows, :], in1=sind[rows, gs], op=ALU.mult), [b_ps[rb], b_tab], [b_tq])
                        dve(lambda e, xq=xq, gs=gs: e.tensor_tensor(out=xq[rows, :], in0=xq[rows, :], in1=cosd[rows, gs], op=ALU.mult), [b_xq, b_tab], [b_xq])
                        dve(lambda e, xq=xq, tq=tq, gs=gs, hi=hi: e.tensor_tensor(out=Qh[hi][0:64, gs], in0=xq[rows, :], in1=tq[rows, :], op=ALU.add), [b_xq, b_tq], [b_Qh[hi]])
                    else:
                        dve(lambda e, gs=gs, bank=bank, hi=hi: e.tensor_copy(out=Qh[hi][0:64, gs], in_=ps[bank][0:64, :]), [b_ps[bank]], [b_Qh[hi]], dj=True)
                    yield

            def n_prep_units(h):
                return (nseq * KT if h % 2 == 0 else 0) + TkA // 512 + G * (3 if U["rope"] else 1)

            def attention(h, gen=None, n_units=0):
                hi = h % 2
                vt = Vt[(h // 2) % 2]
                b_vt = b_Vt[(h // 2) % 2]
                steps = []
                for qg in range(nq_groups):
                    if nseq == 1:
                        for kp in range(KT // 2):
                            steps.append((qg, 0, (2 * kp, 2 * kp + 1), kp == 0, kp == KT // 2 - 1))
                    else:
                        steps.append((qg, qg, (0, 1), True, True))

                def qk(i):
                    qg, s, kts, first, last = steps[i]
                    pair = (i % 2) * 2
                    qs = slice(qg * NQ, (qg + 1) * NQ)
                    for t, kt in enumerate(kts):
                        kk_ = s * KT + kt
                        ks = slice(kk_ * 128, (kk_ + 1) * 128)
                        if NQ == 512:
                            mm(ps[pair + t][:], [(Kh[hi][:, ks], Qh[hi][:, qs])], [b_Kh[hi], b_Qh[hi]], [b_ps[pair + t]])
                        else:
                            mm(ps[pair][:, t * NQ:(t + 1) * NQ], [(Kh[hi][:, ks], Qh[hi][:, qs])], [b_Kh[hi], b_Qh[hi]], [b_ps[pair]])

                def expv(i):
                    qg, s, kts, first, last = steps[i]
                    pair = (i % 2) * 2
                    pt = PT[i % 2]
                    b_pt = b_PT[i % 2]
                    ob = 4 + qg % 2
                    qs = slice(qg * NQ, (qg + 1) * NQ)
                    if NQ == 512:
                        act(pt[:, 0:1024], pall[:, pair * 512:(pair + 2) * 512], AF.Exp, [b_ps[pair], b_ps[pair + 1]], [b_pt], scale=MLA_SCALE)
                    else:
                        act(pt[:, 0:512], ps[pair][:], AF.Exp, [b_ps[pair]], [b_pt], scale=MLA_SCALE)
                    for t, kt in enumerate(kts):
                        kk_ = s * KT + kt
                        mm(ps[ob][:, 0:NQ], [(vt[:, kk_, hi, :], pt[:, t * NQ:(t + 1) * NQ])], [b_vt, b_pt], [b_ps[ob]],
                           start=(first and t == 0), stop=(last and t == len(kts) - 1))
                    if last:
                        dr = slice(0, 64) if hi == 0 else slice(64, 128)
                        sr = slice(64, 128) if hi == 0 else slice(0, 64)
                        act(rc[dr, 0:NQ], ps[ob][sr, 0:NQ], AF.Ln, [b_ps[ob]], [b_rc])
                        act(rc[dr, 0:NQ], rc[dr, 0:NQ], AF.Exp, [b_rc], [b_rc], scale=-1.0)
                        dve(lambda e: e.tensor_tensor(out=ot[dr, 0:NQ], in0=ps[ob][dr, 0:NQ], in1=rc[dr, 0:NQ], op=ALU.mult), [b_ps[ob], b_rc], [b_ot])
                        dve(lambda e: e.tensor_tensor(out=ym[dr, h // 2, qs], in0=ot[dr, 0:NQ], in1=ym[dr, h // 2, qs], op=ALU.mult), [b_ot, b_ym], [b_ym])

                per_step = -(-n_units // max(1, len(steps) - 1)) if gen is not None else 0
                qk(0)
                for i in range(len(steps)):
                    if i + 1 < len(steps):
                        qk(i + 1)
                    if gen is not None:
                        for _ in range(per_step):
                            next(gen, None)
                    expv(i)
                if gen is not None:
                    for _ in gen:
                        pass

            for _ in prep(0):
                pass
            for h in range(8):
                if h + 1 < 8:
                    attention(h, prep(h + 1), n_prep_units(h + 1))
                else:
                    attention(h)
            P.barrier(scratch1[:, 0:1])

    def phase_diff(l, U, yd, b_yd):
        T, nseq, Ts, ctx = U["T"], U["nseq"], U["Ts"], U["ctx"]
        G = T // 512
        Tk = Ts + ctx
        KT = Tk // 128
        TkA = nseq * Tk
        hTr = b_hT[:G]
        with ExitStack() as es:
            def loc(name, shape, dt=F32):
                return es.enter_context(nc.sbuf_tensor(un(name), list(shape), dt)), Buf(name, local=True)
            qd, b_qd = loc("qd", [128, 4, T], BF16)
            kd, b_kd = loc("kd", [128, 4, TkA], BF16)
            vd, b_vd = loc("vd", [128, nseq * KT, 512], BF16)
            PT2a, b_PT2a = loc("dPT2a", [128, 1024], BF16); PT2b, b_PT2b = loc("dPT2b", [128, 1024], BF16)
            xr, b_xr = loc("dxr", [128, 512]); t1, b_t1 = loc("dt1", [128, 512])
            r1, b_r1 = loc("r1", [128, 512]); r2, b_r2 = loc("r2", [128, 512])
            od, b_od = loc("od", [128, 512]); ou, b_ou = loc("ou", [128, 512])
            sqd, b_sqd = loc("sqd", [128, 512], BF16)
            ost = []; b_ost = []
            for i in range(2):
                t, b = loc("dost%d" % i, [128, 512]); ost.append(t); b_ost.append(b)

            def koff_list(g):
                if nseq == 1:
                    return [(slice(ctx + g * 512, ctx + (g + 1) * 512), slice(0, 512))]
                return [(slice((2 * g + h2) * Tk, (2 * g + h2 + 1) * Tk), slice(h2 * 256, (h2 + 1) * 256)) for h2 in range(2)]

            if ctx:
                cst = [xr, t1, r1, r2]
                b_cst = [b_xr, b_t1, b_r1, b_r2]
                for j in range(4):
                    P.dma("sp", cst[j][:], dk_c[l][j * 128:(j + 1) * 128, :], writes=[b_cst[j]])
                for j2 in range(4):
                    def tr(pe, j2=j2):
                        ins = None
                        for j in range(4):
                            ins = pe.transpose(ps[j2][:, j * 128:(j + 1) * 128], cst[j][:, j2 * 128:(j2 + 1) * 128], ident[:])
                        return ins
                    P.op("pe", tr, reads=b_cst + [b_ident], writes=[b_ps[j2]])
                    act(kd[:, j2, 0:512], ps[j2][:], AF.Identity, [b_ps[j2]], [b_kd])
                P.dma("pool", vd[:, 0:4, :], dv_c[l].rearrange("(j p) d -> p j d", p=128), writes=[b_vd])

            for (col0, dst, b_dst, is_k) in ((C_DQ, qd, b_qd, False), (C_DK, kd, b_kd, True)):
                wt, bw = load_w(w_in[l][:, col0:col0 + 512], 512, ("in", l, col0))
                for g in range(G):
                    gs = slice(g * 512, (g + 1) * 512)
                    for j in range(4):
                        bank = j % 3
                        mm(ps[bank][:], [(wt[:, k, j * 128:(j + 1) * 128], hT[:, k, gs]) for k in range(8)], [bw, hTr[g]], [b_ps[bank]])
                        dsts = koff_list(g) if is_k else [(gs, slice(0, 512))]
                        if U["rope"]:
                            if j % 2 == 0:
                                rope_evac(dst[:, j, dsts[0][0]], ps[bank][:], 3 + (j % 2), g, slice(0, 128), [b_ps[bank]], [b_dst], xr, b_xr, t1, b_t1)
                            else:
                                rope_evac(dst[:, j, dsts[0][0]], ps[bank][:], 3 + (j % 2), g, slice(0, 128), [b_ps[bank]], [b_dst], r1, b_r1, r2, b_r2)
                        else:
                            for (kd_, sd_) in dsts:
                                act(dst[:, j, kd_], ps[bank][:, sd_], AF.Identity, [b_ps[bank]], [b_dst])
                if is_k and not ctx:
                    for tt in range(T // 128):
                        ts_ = slice(tt * 128, (tt + 1) * 128)
                        bank = 5 + tt % 2
                        s, r0 = divmod(tt * 128, Ts)
                        mm(ps[bank][:], [(hT[:, k, ts_], wt[:, k, :]) for k in range(8)], [bw, hTr[tt // 4]], [b_ps[bank]])
                        act(ost[tt % 2][:], ps[bank][:], AF.Identity, [b_ps[bank]], [b_ost[tt % 2]])
                        P.dma("sp", o_dk[s, l, r0:r0 + 128, :], ost[tt % 2][:], reads=[b_ost[tt % 2]])
            wt, bw = load_w(w_in[l][:, C_DV:C_DV + 512], 512, ("in", l, C_DV))
            for tt in range(T // 128):
                ts_ = slice(tt * 128, (tt + 1) * 128)
                bank = 5 + tt % 2
                s, r0 = divmod(tt * 128, Ts)
                kt = s * KT + (ctx + r0) // 128
                mm(ps[bank][:], [(hT[:, k, ts_], wt[:, k, :]) for k in range(8)], [bw, hTr[tt // 4]], [b_ps[bank]])
                if ctx:
                    dve(lambda e, kt=kt, bank=bank: e.tensor_copy(out=vd[:, kt, :], in_=ps[bank][:]), [b_ps[bank]], [b_vd], dj=True)
                else:
                    act(ost[tt % 2][:], ps[bank][:], AF.Identity, [b_ps[bank]], [b_ost[tt % 2]])
                    dve(lambda e, kt=kt, tt=tt: e.tensor_copy(out=vd[:, kt, :], in_=ost[tt % 2][:]), [b_ost[tt % 2]], [b_vd])
                    P.dma("sp", o_dv[s, l, r0:r0 + 128, :], ost[tt % 2][:], reads=[b_ost[tt % 2]])
            wt, bw = load_w(w_in[l][:, C_DG:C_DG + 512], 512, ("in", l, C_DG))
            for g in range(G):
                gs = slice(g * 512, (g + 1) * 512)
                for j in range(4):
                    bank = j % 3
                    mm(ps[bank][:], [(wt[:, k, j * 128:(j + 1) * 128], hT[:, k, gs]) for k in range(8)], [bw, hTr[g]], [b_ps[bank]])
                    act(yd[:, j, gs], ps[bank][:], AF.Silu, [b_ps[bank]], [b_yd], dj=True)

            NQ = 512 if nseq == 1 else Ts
            PT = [PT2a, PT2b]
            b_PT = [b_PT2a, b_PT2b]
            for j in range(4):
                steps = []
                for qg in range(T // NQ):
                    s = 0 if nseq == 1 else qg
                    for kt in range(KT):
                        steps.append((qg, s, kt))

                def qk(i, j=j, steps=steps):
                    qg, s, kt = steps[i]
                    pair = (i % 2) * 2
                    qs = slice(qg * NQ, (qg + 1) * NQ)
                    kk_ = s * KT + kt
                    ks = slice(kk_ * 128, (kk_ + 1) * 128)
                    for cp in range(2):
                        rows = slice(cp * 64, (cp + 1) * 64)
                        if NQ == 512:
                            mm(ps[pair + cp][:], [(kd[rows, j, ks], qd[rows, j, qs])], [b_kd, b_qd], [b_ps[pair + cp]])
                        else:
                            mm(ps[pair + cp][:, 0:NQ], [(kd[rows, j, ks], qd[rows, j, qs])], [b_kd, b_qd], [b_ps[pair + cp]])

                def expv(i, j=j, steps=steps):
                    qg, s, kt = steps[i]
                    pair = (i % 2) * 2
                    pt = PT[i % 2]
                    b_pt = b_PT[i % 2]
                    qs = slice(qg * NQ, (qg + 1) * NQ)
                    kk_ = s * KT + kt
                    if NQ == 512:
                        act(pt[:, 0:1024], pall[:, pair * 512:(pair + 2) * 512], AF.Exp, [b_ps[pair], b_ps[pair + 1]], [b_pt], scale=DIFF_SCALE)
                    else:
                        act(pt[:, 0:2 * NQ].rearrange("p (b c) -> p b c", c=NQ),
                            pall[:, pair * 512:(pair + 2) * 512].rearrange("p (b c) -> p b c", c=512)[:, :, 0:NQ],
                            AF.Exp, [b_ps[pair], b_ps[pair + 1]], [b_pt], scale=DIFF_SCALE)
                    for cp in range(2):
                        mm(ps[4 + cp][:, 0:NQ], [(vd[:, kk_, j * 128:(j + 1) * 128], pt[:, cp * NQ:(cp + 1) * NQ])], [b_vd, b_pt], [b_ps[4 + cp]],
                           start=(kt == 0), stop=(kt == KT - 1))
                        mm(ps[6 + cp][:, 0:NQ], [(ones_b[:], pt[:, cp * NQ:(cp + 1) * NQ])], [b_ones, b_pt], [b_ps[6 + cp]],
                           start=(kt == 0), stop=(kt == KT - 1))
                    if kt == KT - 1:
                        act(r1[:, 0:NQ], ps[6][:, 0:NQ], AF.Ln, [b_ps[6]], [b_r1])
                        act(r2[:, 0:NQ], ps[7][:, 0:NQ], AF.Ln, [b_ps[7]], [b_r2])
                        act(r1[:, 0:NQ], r1[:, 0:NQ], AF.Exp, [b_r1], [b_r1], scale=-1.0)
                        act(r2[:, 0:NQ], r2[:, 0:NQ], AF.Exp, [b_r2], [b_r2], scale=-1.0)
                        dve(lambda e: e.tensor_tensor(out=ou[:, 0:NQ], in0=ps[4][:, 0:NQ], in1=r1[:, 0:NQ], op=ALU.mult), [b_ps[4], b_r1], [b_ou])
                        dve(lambda e: e.scalar_tensor_tensor(out=od[:, 0:NQ], in0=ps[5][:, 0:NQ], scalar=neglam[:, 0:1], in1=r2[:, 0:NQ], op0=ALU.mult, op1=ALU.mult),
                            [b_ps[5], b_r2, b_par], [b_od])
                        dve(lambda e: e.tensor_tensor(out=od[:, 0:NQ], in0=od[:, 0:NQ], in1=ou[:, 0:NQ], op=ALU.add), [b_od, b_ou], [b_od])
                        dve(lambda e: e.tensor_tensor(out=sqd[:, 0:NQ], in0=od[:, 0:NQ], in1=od[:, 0:NQ], op=ALU.mult), [b_od], [b_sqd])

                        def part2(nb, j=j, qs=qs):
                            mm(ps[nb][:, 0:NQ], [(ones_b[:], sqd[:, 0:NQ])], [b_ones, b_sqd], [b_ps[nb]])
                            act(ou[:, 0:NQ], ps[nb][:, 0:NQ], AF.Ln, [b_ps[nb]], [b_ou], scale=1.0 / 128.0, bias=EPS)
                            act(ou[:, 0:NQ], ou[:, 0:NQ], AF.Exp, [b_ou], [b_ou], scale=-0.5)
                            dve(lambda e: e.scalar_tensor_tensor(out=od[:, 0:NQ], in0=od[:, 0:NQ], scalar=dn[:, 0:1], in1=ou[:, 0:NQ], op0=ALU.mult, op1=ALU.mult),
                                [b_od, b_ou, b_par], [b_od])
                            dve(lambda e: e.tensor_tensor(out=yd[:, j, qs], in0=od[:, 0:NQ], in1=yd[:, j, qs], op=ALU.mult), [b_od, b_yd], [b_yd])
                        pend.append((i + min(2, KT - 1), part2))

                pend = []
                qk(0)
                for i in range(len(steps)):
                    while pend and pend[0][0] <= i:
                        pend.pop(0)[1](((i + 1) % 2) * 2)
                    if i + 1 < len(steps):
                        qk(i + 1)
                    expv(i)
                while pend:
                    pend.pop(0)[1](0)
            P.barrier(scratch1[:, 0:1])

    def phase_merge_out(l, U, ybr, b_ybr):
        T = U["T"]
        G = T // 512
        u = U["u"]
        hTr = b_hT[:G]
        xsrc = U["x"][l]
        xdst = U["x"][l + 1]
        with ExitStack() as es:
            def loc(name, shape, dt=F32):
                return es.enter_context(nc.sbuf_tensor(un(name), list(shape), dt)), Buf(name, local=True)
            mg, b_mg = loc("merged", [128, 8, T], BF16)
            esA = ExitStack()

            def locA(name, shape, dt=F32):
                return esA.enter_context(nc.sbuf_tensor(un(name), list(shape), dt)), Buf(name, local=True)
            loc_save = loc
            loc = locA
            wbr2 = []; b_wbr2 = []
            for i in range(2):
                t, b = loc("wbr2_%d" % i, [128, 3, 4, 256], BF16); wbr2.append(t); b_wbr2.append(b)
            gsb = []; b_gsb = []
            for i in range(4):
                t, b = loc("gsb%d" % i, [128, 512]); gsb.append(t); b_gsb.append(b)
            mt = []; b_mt = []
            for i in range(4):
                t, b = loc("mt%d" % i, [128, 512]); mt.append(t); b_mt.append(b)
            it = 0
            hslot = {"n": 0, "cur": None}

            def half_slot():
                if hslot["n"] % 2 == 0:
                    hslot["cur"] = ring_slot()
                t, b = hslot["cur"]
                v = t[:].rearrange("p c n -> p (c n)")[:, (hslot["n"] % 2) * 2048:(hslot["n"] % 2 + 1) * 2048].rearrange("p (c n) -> p c n", n=256)
                hslot["n"] += 1
                return v, b

            def load_group(f2):
                wi = f2 % 2
                wm = []
                for b in range(3):
                    c0 = C_MGATE + b * D + f2 * 256
                    v, bb = half_slot()
                    cast_load(v, bb, w_in[l][:, c0:c0 + 256].rearrange("(c p) n -> p c n", p=128), ("mg2", l, b, f2))
                    wm.append((v, bb))
                    cast_load(wbr2[wi][:, b, :, :], b_wbr2[wi], w_br[b][l][:, f2 * 256:(f2 + 1) * 256].rearrange("(c p) n -> p c n", p=128), ("br2", l, b, f2))
                return wm

            nxt = load_group(0)
            for f2 in range(4):
                wm2 = nxt
                wi = f2 % 2
                if f2 + 1 < 4:
                    nxt = load_group(f2 + 1)
                for g in range(G):
                    gs = slice(g * 512, (g + 1) * 512)
                    for fi in range(2):
                        f = f2 * 2 + fi
                        fs = slice(fi * 128, (fi + 1) * 128)
                        ma = (f * G + g) % 2
                        for b in range(3):
                            r = it % 4
                            it += 1
                            bA, bB = 2 * r, 2 * r + 1
                            mm(ps[bA][:], [(wm2[b][0][:, k, fs], hT[:, k, gs]) for k in range(8)], [wm2[b][1], hTr[g]], [b_ps[bA]])
                            act(gsb[r][:], ps[bA][:], AF.Sigmoid, [b_ps[bA]], [b_gsb[r]])
                            mm(ps[bB][:], [(wbr2[wi][:, b, k, fs], ybr[b][:, k, gs]) for k in range(4)], [b_wbr2[wi], b_ybr[b]], [b_ps[bB]])
                            if b == 0:
                                dve(lambda e, r=r, bB=bB, ma=ma: e.tensor_tensor(out=mt[ma][:], in0=ps[bB][:], in1=gsb[r][:], op=ALU.mult), [b_ps[bB], b_gsb[r]], [b_mt[ma]])
                            else:
                                dve(lambda e, r=r, bB=bB, ma=ma: e.tensor_tensor(out=mt[2 + ma][:], in0=ps[bB][:], in1=gsb[r][:], op=ALU.mult), [b_ps[bB], b_gsb[r]], [b_mt[2 + ma]])
                                if b == 1:
                                    dve(lambda e, ma=ma: e.tensor_tensor(out=mt[ma][:], in0=mt[ma][:], in1=mt[2 + ma][:], op=ALU.add), [b_mt[ma], b_mt[2 + ma]], [b_mt[ma]])
                                else:
                                    dve(lambda e, ma=ma, f=f, gs=gs: e.tensor_tensor(out=mg[:, f, gs], in0=mt[ma][:], in1=mt[2 + ma][:], op=ALU.add), [b_mt[ma], b_mt[2 + ma]], [b_mg])
            P.barrier(scratch1[:, 0:1])
            esA.close()
            loc = loc_save
            wo = []
            for half in range(2):
                wo.append(load_w(w_out[l][:, half * 512:(half + 1) * 512], 512, ("out", l, half)))
            xt = []; b_xt = []; ot = []; b_ot = []
            for i in range(2):
                t, b = loc("oxt%d" % i, [128, D]); xt.append(t); b_xt.append(b)
                t, b = loc("oot%d" % i, [128, D]); ot.append(t); b_ot.append(b)
            junk, b_junk = loc("ojunk", [128, 512]); ss2, b_ss2 = loc("oss2", [128, 4])
            for tt in range(T // 128):
                i = tt % 2
                ts_ = slice(tt * 128, (tt + 1) * 128)
                rdx = [b_x1[(U["name"], tt)]] if l > 0 else []
                P.dma("sp", xt[i][:], xsrc[ts_, :], reads=rdx, writes=[b_xt[i]])
                for half in range(2):
                    bank = 6 + half
                    mm(ps[bank][:], [(mg[:, k, ts_], wo[half][0][:, k, :]) for k in range(8)], [wo[half][1], b_mg], [b_ps[bank]])
                    act(junk[:], ps[bank][:], AF.Square, [b_ps[bank]], [b_junk, b_ss2], accum_out=ss2[:, half:half + 1])
                dve(lambda e: e.tensor_tensor(out=ss2[:, 2:3], in0=ss2[:, 0:1], in1=ss2[:, 1:2], op=ALU.add), [b_ss2], [b_ss2])
                act(ss2[:, 3:4], ss2[:, 2:3], AF.Sqrt, [b_ss2], [b_ss2], scale=1.0 / D, bias=EPS)
                dve(lambda e: e.reciprocal(out=ss2[:, 3:4], in_=ss2[:, 3:4]), [b_ss2], [b_ss2])
                for half in range(2):
                    bank = 6 + half
                    hs = slice(half * 512, (half + 1) * 512)
                    dve(lambda e, i=i, bank=bank, hs=hs: e.scalar_tensor_tensor(out=ot[i][:, hs], in0=ps[bank][:], scalar=ss2[:, 3:4], in1=gg[u][:, hs], op0=ALU.mult, op1=ALU.mult),
                        [b_ps[bank], b_ss2, b_par], [b_ot[i]])
                dve(lambda e, i=i: e.tensor_tensor(out=ot[i][:], in0=ot[i][:], in1=xt[i][:], op=ALU.add), [b_ot[i], b_xt[i]], [b_ot[i]])
                wr = [b_x1[(U["name"], tt)]] if l == 0 else []
                P.dma("sp", xdst[ts_, :], ot[i][:], reads=[b_ot[i]], writes=wr)
            P.barrier(scratch1[:, 0:1])

    class _Stop(Exception):
        pass

    def chk(l, U, ph):
        if stop is not None and tuple(stop[:3]) == (l, U["name"] if U else None, ph):
            raise _Stop()

    try:
        for l in range(DEPTH):
            layer_params(l)
            chk(l, None, "params")
            for U in units:
                T = U["T"]
                if stop is not None and len(stop) > 3 and U["name"] not in stop[3]:
                    continue
                phase_h(l, U)
                chk(l, U, "h")
                with ExitStack() as es:
                    ym = es.enter_context(nc.sbuf_tensor(un("ym"), [128, 4, T], BF16)); b_ym = Buf("ym", local=True)
                    phase_mla(l, U, ym, b_ym)
                    chk(l, U, "mla")
                    yd = es.enter_context(nc.sbuf_tensor(un("yd"), [128, 4, T], BF16)); b_yd = Buf("yd", local=True)
                    phase_diff(l, U, yd, b_yd)
                    chk(l, U, "diff")
                    yr = es.enter_context(nc.sbuf_tensor(un("yr"), [128, 4, T], BF16)); b_yr = Buf("yr", local=True)
                    phase_rnn(l, U, yr, b_yr)
                    chk(l, U, "rnn")
                    phase_merge_out(l, U, [yr, ym, yd], [b_yr, b_ym, b_yd])
                    chk(l, U, "merge")
    except _Stop:
        pass
    P.finish()
    return nc, P


_CACHE = {}


def make_in_maps(inp, n=8):
    f32 = lambda a: np.ascontiguousarray(np.asarray(a, dtype=np.float32))
    cos, sin, rdT, ident = rope_consts()
    shared = {k: f32(inp[k]) for k in (
        "w_mod", "b_mod", "g_pre", "g_post", "w_in", "conv_w", "conv_b", "w_rg_a", "b_rg_a", "w_rg_x", "b_rg_x", "rg_lam",
        "q_norm", "w_uq", "kv_norm", "w_ukv", "lam_q1", "lam_k1", "lam_q2", "lam_k2", "diff_norm",
        "w_br_rnn", "w_br_mla", "w_br_diff", "w_out")}
    shared.update({"cosd": cos, "sind": sin, "rdT": rdT, "ident": ident})
    x_prompt = f32(inp["x_prompt"]); x_sample = f32(inp["x_sample"])
    c = f32(inp["c"]); c_ctx = f32(inp["c_ctx"])
    in_maps = []
    for i in range(n):
        m = dict(shared)
        m["xs"] = x_sample[i]
        m["xp"] = x_prompt[NPB * i:NPB * (i + 1)].reshape(TP, D)
        m["ckv_c"] = f32(inp["cache_mla_ckv"][i])
        m["kr_c"] = f32(inp["cache_mla_krope"][i])
        m["dk_c"] = f32(inp["cache_diff_k"][i]).reshape(DEPTH, PAST, 512)
        m["dv_c"] = f32(inp["cache_diff_v"][i]).reshape(DEPTH, PAST, 512)
        m["st_c"] = f32(inp["state_rnn"][i])
        m["cvec"] = np.stack([c[i], c_ctx], axis=0)
        in_maps.append(m)
    return in_maps


def kernel(**inp):
    n = 8
    if "nc" not in _CACHE:
        _CACHE["nc"] = build_program()[0]
    nc = _CACHE["nc"]
    in_maps = make_in_maps(inp, n)
    res = run_bass_kernel_spmd(nc, in_maps, core_ids=list(range(n)))
    R = res.results
    y_prompt = np.concatenate([R[i]["yp"].reshape(NPB, SP, D) for i in range(n)], axis=0)
    y_sample = np.stack([R[i]["ys"] for i in range(n)], axis=0)
    new_ckv = np.concatenate([R[i]["o_ckv"] for i in range(n)], axis=0)
    new_kr = np.concatenate([R[i]["o_kr"] for i in range(n)], axis=0)
    new_dk = np.concatenate([R[i]["o_dk"].reshape(NPB, DEPTH, SP, 4, 128) for i in range(n)], axis=0)
    new_dv = np.concatenate([R[i]["o_dv"].reshape(NPB, DEPTH, SP, 4, 128) for i in range(n)], axis=0)
    new_st = np.concatenate([R[i]["o_st"] for i in range(n)], axis=0)
    return (y_prompt.astype(np.float32), y_sample.astype(np.float32), new_ckv.astype(np.float32), new_kr.astype(np.float32),
            new_dk.astype(np.float32), new_dv.astype(np.float32), new_st.astype(np.float32))
```

```python
import math
from contextlib import ExitStack
import numpy as np
import concourse.bass as bass
import concourse.mybir as mybir
from concourse.bass_utils import run_bass_kernel_spmd

F32 = mybir.dt.float32
BF16 = mybir.dt.bfloat16
AF = mybir.ActivationFunctionType
ALU = mybir.AluOpType
AX = mybir.AxisListType

D = 1024
DEPTH = 2
TS = 2048
NPB = 4
SP = 256
TP = NPB * SP
PAST = 512
EPS = 1e-6
IN_COLS = 7328
C_RX, C_RG, C_CQ, C_CKV, C_KR, C_MG, C_DQ, C_DK, C_DV, C_DG, C_MGATE = (
    0, 512, 1024, 1408, 1664, 1696, 2208, 2720, 3232, 3744, 4256)
MLA_SCALE = 96 ** -0.5
DIFF_SCALE = 64 ** -0.5
THETA = 10000.0
STRICT = True


class Buf:
    __slots__ = ("name", "w", "r", "local", "excl")

    def __init__(self, name, local=False, excl=False):
        self.name = name
        self.w = None
        self.r = {}
        self.local = local
        self.excl = excl


class Prog:
    def __init__(self, nc):
        self.nc = nc
        self.eng = {"pe": nc.tensor, "act": nc.scalar, "dve": nc.vector, "pool": nc.gpsimd, "sp": nc.sync}
        self.sem = {}
        self.cnt = {e: 0 for e in self.eng}
        self.seen = {e: {} for e in self.eng}
        for e in self.eng:
            self.sem["e:" + e] = nc.alloc_semaphore("sem_" + e)
        self.dq = {}
        for q, k in (("sp", 24), ("pool", 24), ("act", 6)):
            keys = []
            for j in range(k):
                key = "d:%s%d" % (q, j)
                self.sem[key] = nc.alloc_semaphore("dsem_%s%d" % (q, j))
                keys.append(key)
            self.dq[q] = {"i": 0, "keys": keys}
        self.phase_tok = None
        self.n_wait = 0
        self.log = {e: [] for e in self.eng}

    def _wait(self, e, deps):
        for k, v in deps.items():
            if self.seen[e].get(k, 0) >= v:
                continue
            self.eng[e].wait_ge(self.sem[k], v)
            self.seen[e][k] = v
            self.n_wait += 1
            self.log[e].append(("wait", k, v))

    def _deps(self, e, reads, writes, is_dma):
        deps = {}
        own = "e:" + e

        def add(tok, raw=False):
            if tok is None:
                return
            k, v = tok
            if (not is_dma) and k == own and (e == "pe" or not (raw or STRICT)):
                return
            if deps.get(k, 0) < v:
                deps[k] = v

        loc = False
        for b in reads:
            add(b.w, raw=True)
            loc = loc or b.local
        for b in writes:
            add(b.w)
            for k, v in b.r.items():
                add((k, v))
            loc = loc or b.local
        if loc:
            add(self.phase_tok)
        return deps

    def _record(self, tok, reads, writes):
        k, v = tok
        for b in reads:
            if b.r.get(k, 0) < v:
                b.r[k] = v
        for b in writes:
            b.w = tok
            b.r = {}

    def op(self, e, fn, reads=(), writes=()):
        ex = [b for b in reads if b.excl]
        if ex:
            reads = [b for b in reads if not b.excl]
            writes = list(writes) + ex
        self._wait(e, self._deps(e, reads, writes, False))
        ins = fn(self.eng[e])
        self.cnt[e] += 1
        ins.then_inc(self.sem["e:" + e], 1)
        self.log[e].append(("inc", "e:" + e, 1))
        self._record(("e:" + e, self.cnt[e]), reads, writes)

    def dma(self, q, out, in_, reads=(), writes=(), slow=False):
        st = self.dq[q]
        i = st["i"]
        kk = len(st["keys"])
        key = st["keys"][i % kk]
        deps = self._deps(q, reads, writes, True)
        if i >= kk:
            deps[key] = max(deps.get(key, 0), 16 * (i // kk))
        self._wait(q, deps)
        if slow:
            ins = self.eng[q].dma_start(out=out, in_=in_, allow_slow_non_contiguous=True)
        else:
            ins = self.eng[q].dma_start(out=out, in_=in_)
        ins.then_inc(self.sem[key], 16)
        self.log[q].append(("inc", key, 16))
        st["i"] = i + 1
        self._record((key, 16 * (i // kk + 1)), reads, writes)

    def _all_tokens(self):
        deps = {}
        for e in self.eng:
            if self.cnt[e] > 0:
                deps["e:" + e] = self.cnt[e]
        for q, st in self.dq.items():
            kk = len(st["keys"])
            for j, key in enumerate(st["keys"]):
                uses = (st["i"] - j + kk - 1) // kk if st["i"] > j else 0
                if uses > 0:
                    deps[key] = 16 * uses
        return deps

    def barrier(self, scratch):
        deps = self._all_tokens()
        self._wait("dve", deps)
        ins = self.eng["dve"].memset(scratch, 0.0)
        self.cnt["dve"] += 1
        ins.then_inc(self.sem["e:dve"], 1)
        self.log["dve"].append(("inc", "e:dve", 1))
        self.phase_tok = ("e:dve", self.cnt["dve"])

    def finish(self):
        deps = self._all_tokens()
        self._wait("sp", deps)


def rope_consts():
    t = np.arange(TS)
    prow = (t // 64).astype(np.float64)
    pcol = (t % 64).astype(np.float64)
    inv = THETA ** (-np.arange(16, dtype=np.float64) / 16.0)
    cos = np.zeros((128, TS), np.float64)
    sin = np.zeros((128, TS), np.float64)
    for p in range(128):
        q = p % 64
        pos = prow if q < 32 else pcol
        f = inv[q % 16]
        cos[p] = np.cos(pos * f)
        sin[p] = np.sin(pos * f)
    R = np.zeros((128, 128), np.float32)
    for i in range(128):
        if i % 32 < 16:
            R[i, i + 16] = -1.0
        else:
            R[i, i - 16] = 1.0
    return cos.astype(np.float32), sin.astype(np.float32), np.ascontiguousarray(R.T), np.eye(128, dtype=np.float32)


def build_program(stop=None):
    nc = bass.Bass("TRN2", target_bir_lowering=False)
    P = Prog(nc)

    def din(name, shape):
        return nc.dram_tensor(name, list(shape), F32, kind="ExternalInput").ap()

    def dout(name, shape):
        return nc.dram_tensor(name, list(shape), F32, kind="ExternalOutput").ap()

    xs = din("xs", [TS, D])
    xp = din("xp", [TP, D])
    ckv_c = din("ckv_c", [DEPTH, PAST, 256])
    kr_c = din("kr_c", [DEPTH, PAST, 32])
    dk_c = din("dk_c", [DEPTH, PAST, 512])
    dv_c = din("dv_c", [DEPTH, PAST, 512])
    st_c = din("st_c", [DEPTH, 2, 512])
    cvec = din("cvec", [2, D])
    w_mod = din("w_mod", [DEPTH, D, 3 * D])
    b_mod = din("b_mod", [DEPTH, 3 * D])
    g_pre = din("g_pre", [DEPTH, D])
    g_post = din("g_post", [DEPTH, D])
    w_in = din("w_in", [DEPTH, D, IN_COLS])
    conv_w = din("conv_w", [DEPTH, 4, 512])
    conv_b = din("conv_b", [DEPTH, 512])
    w_rg_a = din("w_rg_a", [DEPTH, 2, 8, 64, 64])
    b_rg_a = din("b_rg_a", [DEPTH, 2, 512])
    w_rg_x = din("w_rg_x", [DEPTH, 2, 8, 64, 64])
    b_rg_x = din("b_rg_x", [DEPTH, 2, 512])
    rg_lam = din("rg_lam", [DEPTH, 2, 512])
    q_norm = din("q_norm", [DEPTH, 384])
    w_uq = din("w_uq", [DEPTH, 384, 768])
    kv_norm = din("kv_norm", [DEPTH, 256])
    w_ukv = din("w_ukv", [DEPTH, 256, 1024])
    lam_q1 = din("lam_q1", [DEPTH, 64])
    lam_k1 = din("lam_k1", [DEPTH, 64])
    lam_q2 = din("lam_q2", [DEPTH, 64])
    lam_k2 = din("lam_k2", [DEPTH, 64])
    diff_norm = din("diff_norm", [DEPTH, 128])
    w_br = [din("w_br_rnn", [DEPTH, 512, D]), din("w_br_mla", [DEPTH, 512, D]), din("w_br_diff", [DEPTH, 512, D])]
    w_out = din("w_out", [DEPTH, D, D])
    cosd_d = din("cosd", [128, TS])
    sind_d = din("sind", [128, TS])
    rdT_d = din("rdT", [128, 128])
    ident_d = din("ident", [128, 128])

    ys = dout("ys", [TS, D])
    yp = dout("yp", [TP, D])
    o_ckv = dout("o_ckv", [NPB, DEPTH, SP, 256])
    o_kr = dout("o_kr", [NPB, DEPTH, SP, 32])
    o_dk = dout("o_dk", [NPB, DEPTH, SP, 512])
    o_dv = dout("o_dv", [NPB, DEPTH, SP, 512])
    o_st = dout("o_st", [NPB, DEPTH, 2, 512])
    x1s = nc.dram_tensor("x1s", [TS, D], F32).ap()
    x1p = nc.dram_tensor("x1p", [TP, D], F32).ap()

    uid = {"n": 0}

    def un(name):
        uid["n"] += 1
        return "%s_%d" % (name, uid["n"])

    def sb(name, shape, dt=F32):
        return nc.alloc_sbuf_tensor(un("sb_" + name), list(shape), dt)

    ident = sb("ident", [128, 128]); b_ident = Buf("ident")
    rdT = sb("rdT", [128, 128]); b_rdT = Buf("rdT")
    cosd = sb("cosd_sb", [128, TS]); sind = sb("sind_sb", [128, TS]); b_tab = Buf("tab")
    ones_b = sb("ones_b", [128, 128], BF16); ones_f = sb("ones_f", [128, 128]); b_ones = Buf("ones")
    scratch1 = sb("scratch1", [128, 8])
    ccol = sb("ccol", [128, 8, 2]); scf = sb("scf", [128, 8, 2]); scb = sb("scb", [128, 8, 2], BF16)
    screp = sb("screp", [128, 2, 8, 128], BF16); b_sc = Buf("sc")
    hT = sb("hT", [128, 8, TS], BF16)
    b_hT = [Buf("hT%d" % g) for g in range(TS // 512)]
    NW = 3
    wbuf = [sb("wbuf%d" % i, [128, 8, 512], BF16) for i in range(NW)]
    b_wbuf = [Buf("wbuf%d" % i) for i in range(NW)]
    wstate = {"i": 0}
    bm = sb("bm", [128, 24]); gpre = sb("gpre", [128, 8]); modc = sb("modc", [128, 24, 2]); a1 = sb("a1", [128, 8, 2])
    tmp8 = sb("tmp8", [128, 8]); modraw = sb("modraw", [128, 48])
    gg = [sb("gg%d" % u, [128, D]) for u in range(2)]
    cw = sb("cw", [128, 4, 4]); cb = sb("cb", [128, 4]); bga = sb("bga", [128, 2, 4]); bgx = sb("bgx", [128, 2, 4])
    lam = sb("lam", [128, 2, 4]); sl = sb("sl", [128, 2, 4]); sl2 = sb("sl2", [128, 2, 4]); h0c = sb("h0c", [128, 2, 4])
    lt = [sb("lt%d" % i, [128, 2, 4]) for i in range(4)]
    bd = sb("bd", [128, 2, 2, 4, 128], BF16)
    qn = sb("qn", [128, 3]); kvn = sb("kvn", [128, 2]); kvn_bc = sb("kvn_bc", [128, 256])
    dn = sb("dn", [128, 1]); neglam = sb("neglam", [128, 1]); lamw = sb("lamw", [128, 4, 64]); lame = sb("lame", [128, 2])
    b_par = Buf("params")
    pall = nc.alloc_psum_tensor("pall", [128, 8 * 512], F32)
    ps = [pall[:, i * 512:(i + 1) * 512] for i in range(8)]
    b_ps = [Buf("ps%d" % i, excl=True) for i in range(8)]

    wcache = {}
    pending_st = []

    def flush_stores(keep):
        while len(pending_st) > keep:
            sc, dst_ap, dst_buf, b_sc = pending_st.pop(0)
            P.dma("sp", sc, dst_ap, reads=[dst_buf], writes=[b_sc])

    _orig_barrier = P.barrier

    def _barrier(scratch):
        flush_stores(0)
        _orig_barrier(scratch)
    P.barrier = _barrier

    def cast_load(dst_ap, dst_buf, src_ap, key):
        if key is None:
            P.dma("pool", dst_ap, src_ap, writes=[dst_buf])
            return
        if key in wcache:
            sc, b_sc = wcache[key]
            P.dma("pool", dst_ap, sc, reads=[b_sc], writes=[dst_buf])
            return
        P.dma("pool", dst_ap, src_ap, writes=[dst_buf])
        sc = nc.dram_tensor(un("wsc"), list(dst_ap.shape), BF16).ap()
        b_sc = Buf("wsc")
        wcache[key] = (sc, b_sc)
        pending_st.append((sc, dst_ap, dst_buf, b_sc))
        flush_stores(2)

    def load_w(src_ap, ncols, key):
        i = wstate["i"] % NW
        wstate["i"] += 1
        cast_load(wbuf[i][:, :, 0:ncols], b_wbuf[i], src_ap.rearrange("(c p) n -> p c n", p=128), key)
        return wbuf[i], b_wbuf[i]

    dbg = {}

    def dbg_dump(name, tile_ap, shape, reads):
        return

    def ring_slot():
        i = wstate["i"] % NW
        wstate["i"] += 1
        return wbuf[i], b_wbuf[i]

    def mm(ps_ap, pairs, reads, writes, start=True, stop=True):
        def fn(pe):
            n = len(pairs)
            ins = None
            for i, (lh, rh) in enumerate(pairs):
                ins = pe.matmul(ps_ap, lhsT=lh, rhs=rh, start=(start and i == 0), stop=(stop and i == n - 1))
            return ins
        P.op("pe", fn, reads=reads, writes=writes)

    def act(out, in_, func, reads, writes, **kw):
        P.op("act", lambda e: e.activation(out=out, in_=in_, func=func, **kw), reads=reads, writes=writes)

    def dve(fn, reads, writes):
        P.op("dve", fn, reads=reads, writes=writes)

    P.dma("sp", ident[:], ident_d, writes=[b_ident])
    P.dma("sp", rdT[:], rdT_d, writes=[b_rdT])
    P.dma("sp", cosd[:], cosd_d, writes=[b_tab])
    P.dma("sp", sind[:], sind_d, writes=[b_tab])
    dve(lambda e: e.memset(ones_b[:], 1.0), [], [b_ones])
    dve(lambda e: e.memset(ones_f[:], 1.0), [], [b_ones])
    for u in range(2):
        P.dma("sp", ccol[:, :, u], cvec[u].rearrange("(c p) -> p c", p=128), writes=[b_sc], slow=True)
    act(scf[:], ccol[:], AF.Silu, [b_sc], [b_sc])
    act(scb[:], scf[:], AF.Identity, [b_sc], [b_sc])
    for u in range(2):
        for k in range(8):
            act(screp[:, u, k, :], ones_f[:], AF.Identity, [b_sc, b_ones], [b_sc], scale=scf[:, k, u:u + 1])

    units = [
        dict(name="s", u=0, T=TS, nseq=1, Ts=TS, ctx=PAST, rope=True, x=[xs, x1s, ys]),
        dict(name="p", u=1, T=TP, nseq=NPB, Ts=SP, ctx=0, rope=False, x=[xp, x1p, yp]),
    ]
    b_x1 = {("s", t): Buf("x1s%d" % t) for t in range(TS // 128)}
    b_x1.update({("p", t): Buf("x1p%d" % t) for t in range(TP // 128)})

    def layer_params(l):
        lam_init = 0.8 - 0.6 * math.exp(-0.3 * l)
        P.barrier(scratch1[:, 0:1])
        rd = [b_par]
        ldb = []

        def nb():
            ldb.append(Buf("pload", local=True))
            return ldb[-1]
        P.dma("sp", bm[:], b_mod[l].rearrange("(c p) -> p c", p=128), writes=[nb()], slow=True)
        P.dma("sp", gpre[:], g_pre[l].rearrange("(c p) -> p c", p=128), writes=[nb()], slow=True)
        for k in range(4):
            P.dma("sp", cw[:, :, k], conv_w[l, k].rearrange("(c p) -> p c", p=128), writes=[nb()], slow=True)
        P.dma("sp", cb[:], conv_b[l].rearrange("(c p) -> p c", p=128), writes=[nb()], slow=True)
        for d in range(2):
            P.dma("sp", bga[:, d, :], b_rg_a[l, d].rearrange("(c p) -> p c", p=128), writes=[nb()], slow=True)
        for d in range(2):
            P.dma("sp", bgx[:, d, :], b_rg_x[l, d].rearrange("(c p) -> p c", p=128), writes=[nb()], slow=True)
        for d in range(2):
            P.dma("sp", lam[:, d, :], rg_lam[l, d].rearrange("(c p) -> p c", p=128), writes=[nb()], slow=True)
        for d in range(2):
            P.dma("sp", h0c[:, d, :], st_c[l, d].rearrange("(c p) -> p c", p=128), writes=[nb()], slow=True)
        P.dma("sp", qn[:], q_norm[l].rearrange("(c p) -> p c", p=128), writes=[nb()], slow=True)
        P.dma("sp", kvn[:], kv_norm[l].rearrange("(c p) -> p c", p=128), writes=[nb()], slow=True)
        P.dma("sp", kvn_bc[:], kv_norm[l:l + 1, :].partition_broadcast(128), writes=[nb()])
        P.dma("sp", dn[:], diff_norm[l].rearrange("(c p) -> p c", p=128), writes=[nb()], slow=True)
        for i, t in enumerate((lam_q1, lam_k1, lam_q2, lam_k2)):
            P.dma("sp", lamw[:, i, :], t[l:l + 1, :].partition_broadcast(128), writes=[nb()])
        b_bd = Buf("bd0", local=True)
        dve(lambda e: e.memset(bd[:], 0.0), [], [b_bd])
        for gi, wg in enumerate((w_rg_a, w_rg_x)):
            for d in range(2):
                for hh in range(2):
                    src = wg[l, d].rearrange("(c two) k j -> two k c j", two=2)[hh]
                    P.dma("pool", bd[hh * 64:(hh + 1) * 64, d, gi, :, hh * 64:(hh + 1) * 64], src, reads=[b_bd], writes=[nb()])
        dve(lambda e: e.memset(tmp8[:, 0:1], 0.0), ldb + [b_bd], [b_par])
        act(lt[0][:], lam[:], AF.Exp, rd, [b_par], scale=-1.0)
        dve(lambda e: e.tensor_scalar(out=lt[1][:], in0=lt[0][:], scalar1=2.0, scalar2=None, op0=ALU.add), rd, [b_par])
        dve(lambda e: e.reciprocal(out=lt[1][:], in_=lt[1][:]), rd, [b_par])
        dve(lambda e: e.tensor_tensor(out=lt[0][:], in0=lt[0][:], in1=lt[1][:], op=ALU.mult), rd, [b_par])
        dve(lambda e: e.tensor_tensor(out=lt[1][:], in0=lt[0][:], in1=lt[0][:], op=ALU.mult), rd, [b_par])
        dve(lambda e: e.tensor_scalar(out=lt[2][:], in0=lt[1][:], scalar1=1.0 / 11.0, scalar2=1.0 / 9.0, op0=ALU.mult, op1=ALU.add), rd, [b_par])
        for cst in (1.0 / 7.0, 1.0 / 5.0, 1.0 / 3.0, 1.0):
            dve(lambda e: e.tensor_tensor(out=lt[2][:], in0=lt[2][:], in1=lt[1][:], op=ALU.mult), rd, [b_par])
            dve(lambda e, cst=cst: e.tensor_scalar(out=lt[2][:], in0=lt[2][:], scalar1=cst, scalar2=None, op0=ALU.add), rd, [b_par])
        dve(lambda e: e.tensor_tensor(out=lt[2][:], in0=lt[2][:], in1=lt[0][:], op=ALU.mult), rd, [b_par])
        dve(lambda e: e.tensor_scalar(out=sl[:], in0=lt[2][:], scalar1=-16.0, scalar2=None, op0=ALU.mult), rd, [b_par])
        dve(lambda e: e.tensor_scalar(out=sl2[:], in0=lt[2][:], scalar1=-32.0, scalar2=None, op0=ALU.mult), rd, [b_par])
        dve(lambda e: e.tensor_tensor(out=lamw[:, 0, :], in0=lamw[:, 0, :], in1=lamw[:, 1, :], op=ALU.mult), rd, [b_par])
        dve(lambda e: e.tensor_tensor(out=lamw[:, 2, :], in0=lamw[:, 2, :], in1=lamw[:, 3, :], op=ALU.mult), rd, [b_par])
        dve(lambda e: e.reduce_sum(out=lame[:, 0:1], in_=lamw[:, 0, :], axis=AX.X), rd, [b_par])
        dve(lambda e: e.reduce_sum(out=lame[:, 1:2], in_=lamw[:, 2, :], axis=AX.X), rd, [b_par])
        act(lame[:], lame[:], AF.Exp, rd, [b_par])
        dve(lambda e: e.tensor_tensor(out=neglam[:], in0=lame[:, 1:2], in1=lame[:, 0:1], op=ALU.subtract), rd, [b_par])
        dve(lambda e: e.tensor_scalar(out=neglam[:], in0=neglam[:], scalar1=-lam_init, scalar2=None, op0=ALU.add), rd, [b_par])
        dve(lambda e: e.tensor_scalar(out=dn[:], in0=dn[:], scalar1=1.0 - lam_init, scalar2=None, op0=ALU.mult), rd, [b_par])

        with ExitStack() as es:
            bmg = es.enter_context(nc.sbuf_tensor(un("bmg"), [128, D], F32)); gpb = es.enter_context(nc.sbuf_tensor(un("gpb"), [128, D], F32))
            b_l = Buf("modl", local=True)
            P.dma("sp", bmg[:], b_mod[l:l + 1, 2 * D:3 * D].partition_broadcast(128), writes=[b_l])
            P.dma("sp", gpb[:], g_post[l:l + 1, :].partition_broadcast(128), writes=[b_l])
            psm = ps[0][:].rearrange("p (c u) -> p c u", u=2)
            for pc in range(6):
                wt, bw = load_w(w_mod[l][:, pc * 512:(pc + 1) * 512], 512, None)
                for cc in range(4):
                    col = pc * 4 + cc
                    mm(psm[:, col, :], [(wt[:, k, cc * 128:(cc + 1) * 128], scb[:, k, :]) for k in range(8)],
                       [bw, b_sc], [b_ps[0]])
                if pc >= 4:
                    half = pc - 4
                    for u in range(2):
                        bank = 1 + u * 2 + half
                        mm(ps[bank][:], [(screp[:, u, k, :], wt[:, k, :]) for k in range(8)], [bw, b_sc], [b_ps[bank]])
            dve(lambda e: e.tensor_copy(out=modraw[:], in_=ps[0][:, 0:48]), [b_ps[0]], [b_par])
            dbg_dump("modraw", modraw[:], [128, 48], [b_par])
            mrv = modraw[:].rearrange("p (c u) -> p c u", u=2)
            for u in range(2):
                dve(lambda e, u=u: e.tensor_tensor(out=modc[:, :, u], in0=mrv[:, :, u], in1=bm[:], op=ALU.add), [b_par], [b_par])
                dve(lambda e, u=u: e.tensor_scalar(out=tmp8[:], in0=modc[:, 8:16, u], scalar1=1.0, scalar2=None, op0=ALU.add), rd, [b_par])
                dve(lambda e, u=u: e.tensor_tensor(out=a1[:, :, u], in0=tmp8[:], in1=gpre[:], op=ALU.mult), rd, [b_par])
                for half in range(2):
                    bank = 1 + u * 2 + half
                    hs = slice(half * 512, (half + 1) * 512)
                    dve(lambda e, u=u, bank=bank, hs=hs: e.tensor_tensor(out=gg[u][:, hs], in0=ps[bank][:], in1=bmg[:, hs], op=ALU.add), [b_ps[bank], b_l], [b_par])
                    dve(lambda e, u=u, hs=hs: e.tensor_tensor(out=gg[u][:, hs], in0=gg[u][:, hs], in1=gpb[:, hs], op=ALU.mult), [b_l, b_par], [b_par])
            P.barrier(scratch1[:, 0:1])
        if l == 0:
            dbg_dump("modc", modc[:], [128, 24, 2], [b_par])
            dbg_dump("a1", a1[:], [128, 8, 2], [b_par])
            dbg_dump("gg1", gg[1][:], [128, D], [b_par])
            dbg_dump("scf", scf[:], [128, 8, 2], [b_sc])
            dbg_dump("sl", sl[:], [128, 2, 4], [b_par])
            dbg_dump("neglam", neglam[:], [128, 1], [b_par])

    def phase_h(l, U):
        u = U["u"]
        xsrc = U["x"][l]
        with ExitStack() as es:
            xt = [es.enter_context(nc.sbuf_tensor(un("xt%d" % i), [128, 4, D], F32)) for i in range(2)]
            b_xt = [Buf("xt%d" % i, local=True) for i in range(2)]
            junk = es.enter_context(nc.sbuf_tensor(un("junk"), [128, D], F32)); b_junk = Buf("junk", local=True)
            ssq = [es.enter_context(nc.sbuf_tensor(un("ssq%d" % i), [128, 4], F32)) for i in range(2)]
            b_ss = [Buf("ssq%d" % i, local=True) for i in range(2)]
            for g in range(U["T"] // 512):
                i = g % 2
                rdx = [b_x1[(U["name"], 4 * g + j)] for j in range(4)] if l > 0 else []
                P.dma("sp", xt[i][:], xsrc[g * 512:(g + 1) * 512, :].rearrange("(j p) d -> p j d", p=128), reads=rdx, writes=[b_xt[i]])
                for j in range(4):
                    act(junk[:], xt[i][:, j, :], AF.Square, [b_xt[i]], [b_junk, b_ss[i]], accum_out=ssq[i][:, j:j + 1])
                act(ssq[i][:], ssq[i][:], AF.Sqrt, [b_ss[i]], [b_ss[i]], scale=1.0 / D, bias=EPS)
                dve(lambda e, i=i: e.reciprocal(out=ssq[i][:], in_=ssq[i][:]), [b_ss[i]], [b_ss[i]])
                for j in range(4):
                    dve(lambda e, i=i, j=j: e.tensor_scalar(out=xt[i][:, j, :], in0=xt[i][:, j, :], scalar1=ssq[i][:, j:j + 1], scalar2=None, op0=ALU.mult),
                        [b_ss[i], b_xt[i]], [b_xt[i]])
                for c in range(8):
                    bank = c % 4

                    def tr(pe, i=i, c=c, bank=bank):
                        ins = None
                        for j in range(4):
                            ins = pe.transpose(ps[bank][:, j * 128:(j + 1) * 128], xt[i][:, j, c * 128:(c + 1) * 128], ident[:])
                        return ins
                    P.op("pe", tr, reads=[b_xt[i], b_ident], writes=[b_ps[bank]])
                    dst = hT[:, c, g * 512:(g + 1) * 512]
                    if c % 2 == 0:
                        act(dst, ps[bank][:], AF.Identity, [b_ps[bank], b_par], [b_hT[g]], scale=a1[:, c, u:u + 1], bias=modc[:, c, u:u + 1])
                    else:
                        dve(lambda e, dst=dst, bank=bank, c=c: e.tensor_scalar(out=dst, in0=ps[bank][:], scalar1=a1[:, c, u:u + 1], scalar2=modc[:, c, u:u + 1], op0=ALU.mult, op1=ALU.add),
                            [b_ps[bank], b_par], [b_hT[g]])
            P.barrier(scratch1[:, 0:1])
            if l == 0 and U["name"] == "p":
                dbg_dump("hT", hT[:, :, 0:1024], [128, 8, 1024], hTr if False else b_hT[:2])

    def phase_rnn(l, U, yr, b_yr):
        T, nseq, Ts = U["T"], U["nseq"], U["Ts"]
        G = T // 512
        hTr = b_hT[:G]
        with ExitStack() as es:
            def loc(name, shape, dt=F32):
                return es.enter_context(nc.sbuf_tensor(un(name), list(shape), dt)), Buf(name, local=True)
            rx, b_rx = loc("rx", [128, T]); xc, b_xc = loc("xc", [128, T]); xcb, b_xcb = loc("xcb", [128, T], BF16)
            A, b_A = loc("rA", [128, T]); I, b_I = loc("rI", [128, T]); S, b_S = loc("rS", [128, T])
            A1, b_A1 = loc("rA1", [128, T]); I1, b_I1 = loc("rI1", [128, T])
            srg0, b_srg0 = loc("srg0", [128, 512]); srg1, b_srg1 = loc("srg1", [128, 512]); stc, b_stc0 = loc("stc", [128, 4, NPB, 2])
            b_stcs = [b_stc0] + [Buf("stc%d" % i, local=True) for i in range(1, 4)]
            wrx, bwrx = load_w(w_in[l][:, C_RX:C_RX + 512], 512, ("in", l, C_RX))
            wrg, bwrg = load_w(w_in[l][:, C_RG:C_RG + 512], 512, ("in", l, C_RG))
            for c in range(4):
                cs = slice(c * 128, (c + 1) * 128)
                for g in range(G):
                    gs = slice(g * 512, (g + 1) * 512)
                    bank = g % 4
                    mm(ps[bank][:], [(wrx[:, k, cs], hT[:, k, gs]) for k in range(8)], [bwrx, hTr[g]], [b_ps[bank]])
                    act(rx[:, gs], ps[bank][:], AF.Identity, [b_ps[bank]], [b_rx])
                dve(lambda e, c=c: e.tensor_scalar(out=xc[:], in0=rx[:], scalar1=cw[:, c, 2:3], scalar2=cb[:, c:c + 1], op0=ALU.mult, op1=ALU.add), [b_rx, b_par], [b_xc])
                for s in range(nseq):
                    t0 = s * Ts
                    for k, off in ((0, -2), (1, -1), (3, 1)):
                        if off < 0:
                            o_sl = slice(t0 - off, t0 + Ts); i_sl = slice(t0, t0 + Ts + off)
                        else:
                            o_sl = slice(t0, t0 + Ts - off); i_sl = slice(t0 + off, t0 + Ts)
                        dve(lambda e, c=c, k=k, o_sl=o_sl, i_sl=i_sl: e.scalar_tensor_tensor(out=xc[:, o_sl], in0=rx[:, i_sl], scalar=cw[:, c, k:k + 1], in1=xc[:, o_sl], op0=ALU.mult, op1=ALU.add),
                            [b_rx, b_par, b_xc], [b_xc])
                act(xcb[:], xc[:], AF.Identity, [b_xc], [b_xcb])
                Ad = [(A, b_A), (A1, b_A1)]
                Id = [(I, b_I), (I1, b_I1)]
                Td = [(rx, b_rx), (S, b_S)]
                for d in range(2):
                    A_, b_A_ = Ad[d]; I_, b_I_ = Id[d]; T_, b_T_ = Td[d]
                    for gi, (dst, b_dst, bias_t) in enumerate(((A_, b_A_, bga), (I_, b_I_, bgx))):
                        for g in range(G):
                            gs = slice(g * 512, (g + 1) * 512)
                            bank = 4 + (g % 4)
                            mm(ps[bank][:], [(bd[:, d, gi, c, :], xcb[:, gs])], [b_par, b_xcb], [b_ps[bank]])
                            act(dst[:, gs], ps[bank][:], AF.Sigmoid, [b_ps[bank], b_par], [b_dst], bias=bias_t[:, d, c:c + 1])
                    act(T_[:], A_[:], AF.Exp, [b_A_, b_par], [b_T_], scale=sl2[:, d, c:c + 1])
                    act(A_[:], A_[:], AF.Exp, [b_A_, b_par], [b_A_], scale=sl[:, d, c:c + 1])
                    act(T_[:], T_[:], AF.Sqrt, [b_T_], [b_T_], scale=-1.0, bias=1.0)
                for d in range(2):
                    A_, b_A_ = Ad[d]; I_, b_I_ = Id[d]; T_, b_T_ = Td[d]
                    dve(lambda e, I_=I_, T_=T_: e.tensor_tensor(out=I_[:], in0=I_[:], in1=T_[:], op=ALU.mult), [b_I_, b_T_], [b_I_])
                    dve(lambda e, I_=I_: e.tensor_tensor(out=I_[:], in0=I_[:], in1=xc[:], op=ALU.mult), [b_I_, b_xc], [b_I_])
                    for s in range(nseq):
                        ss_ = slice(s * Ts, (s + 1) * Ts)
                        init = h0c[:, d, c:c + 1] if U["ctx"] else 0.0
                        if d == 0:
                            dve(lambda e, ss_=ss_, init=init, T_=T_, A_=A_, I_=I_: e.tensor_tensor_scan(out=T_[:, ss_], data0=A_[:, ss_], data1=I_[:, ss_], initial=init, op0=ALU.mult, op1=ALU.add),
                                [b_A_, b_I_, b_par], [b_T_])
                        else:
                            rs_ = slice((s + 1) * Ts - 1, s * Ts - 1 if s > 0 else None, -1)
                            dve(lambda e, rs_=rs_, init=init, T_=T_, A_=A_, I_=I_: e.tensor_tensor_scan(out=T_[:, rs_], data0=A_[:, rs_], data1=I_[:, rs_], initial=init, op0=ALU.mult, op1=ALU.add),
                                [b_A_, b_I_, b_par], [b_T_])
                if not U["ctx"]:
                    for s in range(nseq):
                        dve(lambda e, s=s, c=c: e.tensor_copy(out=stc[:, c, s, 0:1], in_=rx[:, (s + 1) * Ts - 1:(s + 1) * Ts]), [b_rx], [b_stcs[c]])
                        dve(lambda e, s=s, c=c: e.tensor_copy(out=stc[:, c, s, 1:2], in_=S[:, s * Ts:s * Ts + 1]), [b_S], [b_stcs[c]])
                if not U["ctx"]:
                    for s in range(nseq):
                        P.dma("sp", o_st[s, l, :, c * 128:(c + 1) * 128].rearrange("d p -> p d"), stc[:, c, s, :], reads=[b_stcs[c]], slow=True)
                dve(lambda e: e.tensor_tensor(out=rx[:], in0=rx[:], in1=S[:], op=ALU.add), [b_rx, b_S], [b_rx])
                for g in range(G):
                    gs = slice(g * 512, (g + 1) * 512)
                    bank = g % 4
                    mm(ps[bank][:], [(wrg[:, k, cs], hT[:, k, gs]) for k in range(8)], [bwrg, hTr[g]], [b_ps[bank]])
                    srg, b_srg = (srg0, b_srg0) if g % 2 == 0 else (srg1, b_srg1)
                    act(srg[:], ps[bank][:], AF.Silu, [b_ps[bank]], [b_srg])
                    dve(lambda e, gs=gs, c=c, srg=srg: e.tensor_tensor(out=yr[:, c, gs], in0=rx[:, gs], in1=srg[:], op=ALU.mult), [b_rx, b_srg], [b_yr])
            P.barrier(scratch1[:, 0:1])

    def rope_evac(dst, psrc, bank_r, g, rows, reads, writes, xr, b_xr, t1, b_t1):
        gs = slice(g * 512, (g + 1) * 512)
        n = rows.stop - rows.start
        act(xr[rows, :], psrc, AF.Identity, reads, [b_xr])
        mm(ps[bank_r][rows, :], [(rdT[rows, rows], xr[rows, :])], [b_xr, b_rdT], [b_ps[bank_r]])
        dve(lambda e: e.tensor_tensor(out=t1[rows, :], in0=ps[bank_r][rows, :], in1=sind[rows, gs], op=ALU.mult), [b_ps[bank_r], b_tab], [b_t1])
        dve(lambda e: e.tensor_tensor(out=xr[rows, :], in0=xr[rows, :], in1=cosd[rows, gs], op=ALU.mult), [b_xr, b_tab], [b_xr])
        dve(lambda e: e.tensor_tensor(out=dst, in0=xr[rows, :], in1=t1[rows, :], op=ALU.add), [b_xr, b_t1], writes)

    def phase_mla(l, U, ym, b_ym):
        T, nseq, Ts, ctx = U["T"], U["nseq"], U["Ts"], U["ctx"]
        G = T // 512
        Tk = Ts + ctx
        KT = Tk // 128
        TkA = nseq * Tk
        hTr = b_hT[:G]
        with ExitStack() as es:
            def loc(name, shape, dt=F32):
                return es.enter_context(nc.sbuf_tensor(un(name), list(shape), dt)), Buf(name, local=True)
            cqn, b_cqn = loc("cqn", [128, 3, T], BF16)
            ckvn, b_ckvn = loc("ckvn", [128, 2, TkA], BF16)
            Kh = []; b_Kh = []
            for i in range(2):
                t, b = loc("Kh%d" % i, [128, TkA], BF16); Kh.append(t); b_Kh.append(b)
            Qh = []; b_Qh = []
            for i in range(2):
                t, b = loc("Qh%d" % i, [128, T], BF16); Qh.append(t); b_Qh.append(b)
            Vt = []; b_Vt = []
            for i in range(2):
                t, b = loc("Vt%d" % i, [128, nseq * KT, 2, 128], BF16); Vt.append(t); b_Vt.append(b)
            PT2a, b_PT2a = loc("PT2a", [128, 1024], BF16); PT2b, b_PT2b = loc("PT2b", [128, 1024], BF16)
            xr, b_xr = loc("xr", [128, 512]); t1, b_t1 = loc("t1", [128, 512])
            xr2, b_xr2 = loc("xr2", [128, 512]); t12, b_t12 = loc("t12", [128, 512])
            sq, b_sq = loc("sq", [128, 3, 512], BF16)
            rs, b_rs = loc("rs", [128, 512])
            rc, b_rc = loc("rc", [128, 512]); ot, b_ot = loc("ot", [128, 512])
            wq2, b_wq2 = loc("wq2", [128, 3, 8, 128], BF16)
            wkv2, b_wkv2 = loc("wkv2", [128, 2, 8, 128], BF16)
            wkr2, b_wkr2 = loc("wkr2", [128, 8, 64], BF16)
            ost, b_ost = loc("ost", [128, 288]); ssv, b_ssv = loc("ssv", [128, 2])
            es2 = es.enter_context(ExitStack())

            def loc2(name, shape, dt=F32):
                return es2.enter_context(nc.sbuf_tensor(un(name), list(shape), dt)), Buf(name, local=True)
            wq_slot, b_wq_raw = ring_slot()
            wq_raw = wq_slot[:].rearrange("p c n -> p (c n)")[:, 0:2304].rearrange("p (c n) -> p c n", n=768)
            wkv_slot, b_wkv_raw = ring_slot()
            wkv_raw = wkv_slot[:].rearrange("p c n -> p (c n)")[:, 0:2048].rearrange("p (c n) -> p c n", n=1024)
            cst, b_cst = loc2("cst", [128, 4, 256]); krst, b_krst = loc2("krst", [128, 4, 64])
            krT = Kh

            cast_load(wq_raw, b_wq_raw, w_uq[l].rearrange("(c p) n -> p c n", p=128), ("uq", l))
            cast_load(wkv_raw, b_wkv_raw, w_ukv[l].rearrange("(c p) n -> p c n", p=128), ("ukv", l))
            wq_v = wq_raw.rearrange("p c (h e) -> p c h e", e=96)
            dve(lambda e: e.memset(wq2[:], 0.0), [], [b_wq2])
            for c in range(3):
                dve(lambda e, c=c: e.tensor_copy(out=wq2[:, c, :, 0:64:2], in_=wq_v[:, c, :, 64:96]), [b_wq_raw], [b_wq2])
                dve(lambda e, c=c: e.tensor_copy(out=wq2[:, c, :, 64:128], in_=wq_v[:, c, :, 0:64]), [b_wq_raw], [b_wq2])
            wkv_v = wkv_raw.rearrange("p c (h e) -> p c h e", e=128)
            for c in range(2):
                dve(lambda e, c=c: e.tensor_copy(out=wkv2[:, c, :, 0:64], in_=wkv_v[:, c, :, 64:128]), [b_wkv_raw], [b_wkv2])
                dve(lambda e, c=c: e.tensor_copy(out=wkv2[:, c, :, 64:128], in_=wkv_v[:, c, :, 0:64]), [b_wkv_raw], [b_wkv2])
            wcq, bwcq = load_w(w_in[l][:, C_CQ:C_CQ + 384], 384, ("in", l, C_CQ))
            wck, bwck = load_w(w_in[l][:, C_CKV:C_CKV + 288], 288, ("in", l, C_CKV))
            dve(lambda e: e.memset(wkr2[:], 0.0), [], [b_wkr2])
            dve(lambda e: e.tensor_copy(out=wkr2[:, :, 0:64:2], in_=wck[:, :, 256:288]), [bwck], [b_wkr2])
            wmg, bwmg = load_w(w_in[l][:, C_MG:C_MG + 512], 512, ("in", l, C_MG))
            for i in range(2):
                dve(lambda e, i=i: e.memset(Vt[i][:], 1.0), [], [b_Vt[i]])

            def koff(s, t):
                return s * Tk + t
            if ctx:
                P.dma("sp", cst[:], ckv_c[l].rearrange("(j p) d -> p j d", p=128), writes=[b_cst])
                dve(lambda e: e.memset(krst[:], 0.0), [], [b_krst])
                krraw, b_krraw = loc2("krraw", [128, 4, 32])
                P.dma("sp", krraw[:], kr_c[l].rearrange("(j p) d -> p j d", p=128), writes=[b_krraw])
                dve(lambda e: e.tensor_copy(out=krst[:, :, 0:64:2], in_=krraw[:]), [b_krraw, b_krst], [b_krst])
                for j2 in range(2):
                    def tr(pe, j2=j2):
                        ins = None
                        for j in range(4):
                            ins = pe.transpose(ps[j2][:, j * 128:(j + 1) * 128], cst[:, j, j2 * 128:(j2 + 1) * 128], ident[:])
                        return ins
                    P.op("pe", tr, reads=[b_cst, b_ident], writes=[b_ps[j2]])
                    act(ckvn[:, j2, 0:512], ps[j2][:], AF.Identity, [b_ps[j2]], [b_ckvn])

                def tr2(pe):
                    ins = None
                    for j in range(4):
                        ins = pe.transpose(ps[2][0:64, j * 128:(j + 1) * 128], krst[:, j, :], ident[:])
                    return ins
                P.op("pe", tr2, reads=[b_krst, b_ident], writes=[b_ps[2]])
                for i in range(2):
                    act(Kh[i][0:64, 0:512], ps[2][0:64, :], AF.Identity, [b_ps[2]], [b_Kh[i]])

            for g in range(G):
                gs = slice(g * 512, (g + 1) * 512)
                for j in range(3):
                    mm(ps[j][:], [(wcq[:, k, j * 128:(j + 1) * 128], hT[:, k, gs]) for k in range(8)], [bwcq, hTr[g]], [b_ps[j]])
                    act(sq[:, j, :], ps[j][:], AF.Square, [b_ps[j]], [b_sq])
                mm(ps[3][:], [(ones_b[:], sq[:, j, :]) for j in range(3)], [b_ones, b_sq], [b_ps[3]])
                act(rs[:], ps[3][:], AF.Ln, [b_ps[3]], [b_rs], scale=1.0 / 384.0, bias=EPS)
                act(rs[:], rs[:], AF.Exp, [b_rs], [b_rs], scale=-0.5)
                for j in range(3):
                    dve(lambda e, j=j, gs=gs: e.scalar_tensor_tensor(out=cqn[:, j, gs], in0=ps[j][:], scalar=qn[:, j:j + 1], in1=rs[:], op0=ALU.mult, op1=ALU.mult),
                        [b_ps[j], b_par, b_rs], [b_cqn])
                for j in range(2):
                    mm(ps[4 + j][:], [(wck[:, k, j * 128:(j + 1) * 128], hT[:, k, gs]) for k in range(8)], [bwck, hTr[g]], [b_ps[4 + j]])
                    act(sq[:, j, :], ps[4 + j][:], AF.Square, [b_ps[4 + j]], [b_sq])
                mm(ps[6][:], [(ones_b[:], sq[:, j, :]) for j in range(2)], [b_ones, b_sq], [b_ps[6]])
                act(rs[:], ps[6][:], AF.Ln, [b_ps[6]], [b_rs], scale=1.0 / 256.0, bias=EPS)
                act(rs[:], rs[:], AF.Exp, [b_rs], [b_rs], scale=-0.5)
                if nseq == 1:
                    kdst = [(slice(ctx + g * 512, ctx + (g + 1) * 512), slice(0, 512))]
                else:
                    kdst = [(slice((2 * g + h2) * Tk, (2 * g + h2 + 1) * Tk), slice(h2 * 256, (h2 + 1) * 256)) for h2 in range(2)]
                for j in range(2):
                    for (kd_, sd_) in kdst:
                        dve(lambda e, j=j, kd_=kd_, sd_=sd_: e.scalar_tensor_tensor(out=ckvn[:, j, kd_], in0=ps[4 + j][:, sd_], scalar=kvn[:, j:j + 1], in1=rs[:, sd_], op0=ALU.mult, op1=ALU.mult),
                            [b_ps[4 + j], b_par, b_rs], [b_ckvn])
                mm(ps[7][0:64, :], [(wkr2[:, k, :], hT[:, k, gs]) for k in range(8)], [b_wkr2, hTr[g]], [b_ps[7]])
                for (kd_, sd_) in kdst:
                    if U["rope"]:
                        rope_evac(Kh[0][0:64, kd_], ps[7][0:64, :], 3, g, slice(0, 64), [b_ps[7]], [b_Kh[0]], xr, b_xr, t1, b_t1)
                        act(Kh[1][0:64, kd_], Kh[0][0:64, kd_], AF.Identity, [b_Kh[0]], [b_Kh[1]])
                    else:
                        for i in range(2):
                            act(Kh[i][0:64, kd_], ps[7][0:64, sd_], AF.Identity, [b_ps[7]], [b_Kh[i]])
                for j in range(4):
                    bank = j % 3
                    mm(ps[bank][:], [(wmg[:, k, j * 128:(j + 1) * 128], hT[:, k, gs]) for k in range(8)], [bwmg, hTr[g]], [b_ps[bank]])
                    act(ym[:, j, gs], ps[bank][:], AF.Silu, [b_ps[bank]], [b_ym])

            if not ctx:
                for tt in range(T // 128):
                    ts_ = slice(tt * 128, (tt + 1) * 128)
                    bank = 4 + tt % 2
                    s, r0 = divmod(tt * 128, Ts)
                    mm(ps[bank][:, 0:288], [(hT[:, k, ts_], wck[:, k, 0:288]) for k in range(8)], [bwck, hTr[tt // 4]], [b_ps[bank]])
                    act(ost[:, 0:256], ps[bank][:, 0:256], AF.Square, [b_ps[bank]], [b_ost, b_ssv], accum_out=ssv[:, 0:1])
                    act(ost[:, 256:288], ps[bank][:, 256:288], AF.Identity, [b_ps[bank]], [b_ost])
                    act(ssv[:, 0:1], ssv[:, 0:1], AF.Sqrt, [b_ssv], [b_ssv], scale=1.0 / 256.0, bias=EPS)
                    dve(lambda e: e.reciprocal(out=ssv[:, 0:1], in_=ssv[:, 0:1]), [b_ssv], [b_ssv])
                    dve(lambda e, bank=bank: e.scalar_tensor_tensor(out=ost[:, 0:256], in0=ps[bank][:, 0:256], scalar=ssv[:, 0:1], in1=kvn_bc[:], op0=ALU.mult, op1=ALU.mult),
                        [b_ps[bank], b_ssv, b_par], [b_ost])
                    P.dma("sp", o_ckv[s, l, r0:r0 + 128, :], ost[:, 0:256], reads=[b_ost])
                    P.dma("sp", o_kr[s, l, r0:r0 + 128, :], ost[:, 256:288], reads=[b_ost])

            NQ = 512 if nseq == 1 else Ts
            nq_groups = T // NQ
            PT = [PT2a, PT2b]
            b_PT = [b_PT2a, b_PT2b]

            def prep(h):
                hi = h % 2
                if hi == 0:
                    vi = (h // 2) % 2
                    for kt in range(nseq * KT):
                        bank = 6 + kt % 2
                        ks = slice(kt * 128, (kt + 1) * 128)
                        pv = ps[bank][:, 0:128].rearrange("p (a e) -> p a e", e=64)
                        mm(pv, [(ckvn[:, r, ks], wkv2[:, r, h:h + 2, 0:64]) for r in range(2)], [b_ckvn, b_wkv2], [b_ps[bank]])
                        dve(lambda e, vi=vi, kt=kt, pv=pv: e.tensor_copy(out=Vt[vi][:, kt, 0, 0:64], in_=pv[:, 0, :]), [b_ps[bank]], [b_Vt[vi]])
                        dve(lambda e, vi=vi, kt=kt, pv=pv: e.tensor_copy(out=Vt[vi][:, kt, 1, 64:128], in_=pv[:, 1, :]), [b_ps[bank]], [b_Vt[vi]])
                        yield
                for kg in range(TkA // 512):
                    ks = slice(kg * 512, (kg + 1) * 512)
                    bank = 6 + kg % 2
                    mm(ps[bank][:], [(wkv2[:, r, h, :], ckvn[:, r, ks]) for r in range(2)], [b_wkv2, b_ckvn], [b_ps[bank]])
                    dve(lambda e, ks=ks, bank=bank, hi=hi: e.tensor_copy(out=Kh[hi][64:128, ks], in_=ps[bank][64:128, :]), [b_ps[bank]], [b_Kh[hi]])
                    yield
                for g in range(G):
                    gs = slice(g * 512, (g + 1) * 512)
                    bank = 6 + g % 2
                    mm(ps[bank][:], [(wq2[:, c, h, :], cqn[:, c, gs]) for c in range(3)], [b_wq2, b_cqn], [b_ps[bank]])
                    dve(lambda e, gs=gs, bank=bank, hi=hi: e.tensor_copy(out=Qh[hi][64:128, gs], in_=ps[bank][64:128, :]), [b_ps[bank]], [b_Qh[hi]])
                    if U["rope"]:
                        rows = slice(0, 64)
                        xq, b_xq = (xr, b_xr) if g % 2 == 0 else (xr2, b_xr2)
                        tq, b_tq = (t1, b_t1) if g % 2 == 0 else (t12, b_t12)
                        dve(lambda e, xq=xq, bank=bank: e.tensor_copy(out=xq[rows, :], in_=ps[bank][0:64, :]), [b_ps[bank]], [b_xq])
                        yield
                        rb = 7 - g % 2
                        mm(ps[rb][rows, :], [(rdT[rows, rows], xq[rows, :])], [b_xq, b_rdT], [b_ps[rb]])
                        yield
                        dve(lambda e, tq=tq, rb=rb, gs=gs: e.tensor_tensor(out=tq[rows, :], in0=ps[rb][rows, :], in1=sind[rows, gs], op=ALU.mult), [b_ps[rb], b_tab], [b_tq])
                        dve(lambda e, xq=xq, gs=gs: e.tensor_tensor(out=xq[rows, :], in0=xq[rows, :], in1=cosd[rows, gs], op=ALU.mult), [b_xq, b_tab], [b_xq])
                        dve(lambda e, xq=xq, tq=tq, gs=gs, hi=hi: e.tensor_tensor(out=Qh[hi][0:64, gs], in0=xq[rows, :], in1=tq[rows, :], op=ALU.add), [b_xq, b_tq], [b_Qh[hi]])
                    else:
                        dve(lambda e, gs=gs, bank=bank, hi=hi: e.tensor_copy(out=Qh[hi][0:64, gs], in_=ps[bank][0:64, :]), [b_ps[bank]], [b_Qh[hi]])
                    yield

            def n_prep_units(h):
                return (nseq * KT if h % 2 == 0 else 0) + TkA // 512 + G * (3 if U["rope"] else 1)

            def attention(h, gen=None, n_units=0):
                hi = h % 2
                vt = Vt[(h // 2) % 2]
                b_vt = b_Vt[(h // 2) % 2]
                steps = []
                for qg in range(nq_groups):
                    if nseq == 1:
                        for kp in range(KT // 2):
                            steps.append((qg, 0, (2 * kp, 2 * kp + 1), kp == 0, kp == KT // 2 - 1))
                    else:
                        steps.append((qg, qg, (0, 1), True, True))

                def qk(i):
                    qg, s, kts, first, last = steps[i]
                    pair = (i % 2) * 2
                    qs = slice(qg * NQ, (qg + 1) * NQ)
                    for t, kt in enumerate(kts):
                        kk_ = s * KT + kt
                        ks = slice(kk_ * 128, (kk_ + 1) * 128)
                        if NQ == 512:
                            mm(ps[pair + t][:], [(Kh[hi][:, ks], Qh[hi][:, qs])], [b_Kh[hi], b_Qh[hi]], [b_ps[pair + t]])
                        else:
                            mm(ps[pair][:, t * NQ:(t + 1) * NQ], [(Kh[hi][:, ks], Qh[hi][:, qs])], [b_Kh[hi], b_Qh[hi]], [b_ps[pair]])

                def expv(i):
                    qg, s, kts, first, last = steps[i]
                    pair = (i % 2) * 2
                    pt = PT[i % 2]
                    b_pt = b_PT[i % 2]
                    ob = 4 + qg % 2
                    qs = slice(qg * NQ, (qg + 1) * NQ)
                    if NQ == 512:
                        act(pt[:, 0:1024], pall[:, pair * 512:(pair + 2) * 512], AF.Exp, [b_ps[pair], b_ps[pair + 1]], [b_pt], scale=MLA_SCALE)
                    else:
                        act(pt[:, 0:512], ps[pair][:], AF.Exp, [b_ps[pair]], [b_pt], scale=MLA_SCALE)
                    for t, kt in enumerate(kts):
                        kk_ = s * KT + kt
                        mm(ps[ob][:, 0:NQ], [(vt[:, kk_, hi, :], pt[:, t * NQ:(t + 1) * NQ])], [b_vt, b_pt], [b_ps[ob]],
                           start=(first and t == 0), stop=(last and t == len(kts) - 1))
                    if last:
                        dr = slice(0, 64) if hi == 0 else slice(64, 128)
                        sr = slice(64, 128) if hi == 0 else slice(0, 64)
                        act(rc[dr, 0:NQ], ps[ob][sr, 0:NQ], AF.Ln, [b_ps[ob]], [b_rc])
                        act(rc[dr, 0:NQ], rc[dr, 0:NQ], AF.Exp, [b_rc], [b_rc], scale=-1.0)
                        dve(lambda e: e.tensor_tensor(out=ot[dr, 0:NQ], in0=ps[ob][dr, 0:NQ], in1=rc[dr, 0:NQ], op=ALU.mult), [b_ps[ob], b_rc], [b_ot])
                        dve(lambda e: e.tensor_tensor(out=ym[dr, h // 2, qs], in0=ot[dr, 0:NQ], in1=ym[dr, h // 2, qs], op=ALU.mult), [b_ot, b_ym], [b_ym])

                per_step = -(-n_units // max(1, len(steps) - 1)) if gen is not None else 0
                qk(0)
                for i in range(len(steps)):
                    if i + 1 < len(steps):
                        qk(i + 1)
                    if gen is not None:
                        for _ in range(per_step):
                            next(gen, None)
                    expv(i)
                if gen is not None:
                    for _ in gen:
                        pass

            for _ in prep(0):
                pass
            for h in range(8):
                if h + 1 < 8:
                    attention(h, prep(h + 1), n_prep_units(h + 1))
                else:
                    attention(h)
            P.barrier(scratch1[:, 0:1])

    def phase_diff(l, U, yd, b_yd):
        T, nseq, Ts, ctx = U["T"], U["nseq"], U["Ts"], U["ctx"]
        G = T // 512
        Tk = Ts + ctx
        KT = Tk // 128
        TkA = nseq * Tk
        hTr = b_hT[:G]
        with ExitStack() as es:
            def loc(name, shape, dt=F32):
                return es.enter_context(nc.sbuf_tensor(un(name), list(shape), dt)), Buf(name, local=True)
            qd, b_qd = loc("qd", [128, 4, T], BF16)
            kd, b_kd = loc("kd", [128, 4, TkA], BF16)
            vd, b_vd = loc("vd", [128, nseq * KT, 512], BF16)
            PT2a, b_PT2a = loc("dPT2a", [128, 1024], BF16); PT2b, b_PT2b = loc("dPT2b", [128, 1024], BF16)
            xr, b_xr = loc("dxr", [128, 512]); t1, b_t1 = loc("dt1", [128, 512])
            r1, b_r1 = loc("r1", [128, 512]); r2, b_r2 = loc("r2", [128, 512])
            od, b_od = loc("od", [128, 512]); ou, b_ou = loc("ou", [128, 512])
            sqd, b_sqd = loc("sqd", [128, 512], BF16)
            ost = []; b_ost = []
            for i in range(2):
                t, b = loc("dost%d" % i, [128, 512]); ost.append(t); b_ost.append(b)

            def koff_list(g):
                if nseq == 1:
                    return [(slice(ctx + g * 512, ctx + (g + 1) * 512), slice(0, 512))]
                return [(slice((2 * g + h2) * Tk, (2 * g + h2 + 1) * Tk), slice(h2 * 256, (h2 + 1) * 256)) for h2 in range(2)]

            if ctx:
                cst = [xr, t1, r1, r2]
                b_cst = [b_xr, b_t1, b_r1, b_r2]
                for j in range(4):
                    P.dma("sp", cst[j][:], dk_c[l][j * 128:(j + 1) * 128, :], writes=[b_cst[j]])
                for j2 in range(4):
                    def tr(pe, j2=j2):
                        ins = None
                        for j in range(4):
                            ins = pe.transpose(ps[j2][:, j * 128:(j + 1) * 128], cst[j][:, j2 * 128:(j2 + 1) * 128], ident[:])
                        return ins
                    P.op("pe", tr, reads=b_cst + [b_ident], writes=[b_ps[j2]])
                    act(kd[:, j2, 0:512], ps[j2][:], AF.Identity, [b_ps[j2]], [b_kd])
                P.dma("pool", vd[:, 0:4, :], dv_c[l].rearrange("(j p) d -> p j d", p=128), writes=[b_vd])

            for (col0, dst, b_dst, is_k) in ((C_DQ, qd, b_qd, False), (C_DK, kd, b_kd, True)):
                wt, bw = load_w(w_in[l][:, col0:col0 + 512], 512, ("in", l, col0))
                for g in range(G):
                    gs = slice(g * 512, (g + 1) * 512)
                    for j in range(4):
                        bank = j % 3
                        mm(ps[bank][:], [(wt[:, k, j * 128:(j + 1) * 128], hT[:, k, gs]) for k in range(8)], [bw, hTr[g]], [b_ps[bank]])
                        dsts = koff_list(g) if is_k else [(gs, slice(0, 512))]
                        if U["rope"]:
                            if j % 2 == 0:
                                rope_evac(dst[:, j, dsts[0][0]], ps[bank][:], 3 + (j % 2), g, slice(0, 128), [b_ps[bank]], [b_dst], xr, b_xr, t1, b_t1)
                            else:
                                rope_evac(dst[:, j, dsts[0][0]], ps[bank][:], 3 + (j % 2), g, slice(0, 128), [b_ps[bank]], [b_dst], r1, b_r1, r2, b_r2)
                        else:
                            for (kd_, sd_) in dsts:
                                act(dst[:, j, kd_], ps[bank][:, sd_], AF.Identity, [b_ps[bank]], [b_dst])
                if is_k and not ctx:
                    for tt in range(T // 128):
                        ts_ = slice(tt * 128, (tt + 1) * 128)
                        bank = 5 + tt % 2
                        s, r0 = divmod(tt * 128, Ts)
                        mm(ps[bank][:], [(hT[:, k, ts_], wt[:, k, :]) for k in range(8)], [bw, hTr[tt // 4]], [b_ps[bank]])
                        act(ost[tt % 2][:], ps[bank][:], AF.Identity, [b_ps[bank]], [b_ost[tt % 2]])
                        P.dma("sp", o_dk[s, l, r0:r0 + 128, :], ost[tt % 2][:], reads=[b_ost[tt % 2]])
            wt, bw = load_w(w_in[l][:, C_DV:C_DV + 512], 512, ("in", l, C_DV))
            for tt in range(T // 128):
                ts_ = slice(tt * 128, (tt + 1) * 128)
                bank = 5 + tt % 2
                s, r0 = divmod(tt * 128, Ts)
                kt = s * KT + (ctx + r0) // 128
                mm(ps[bank][:], [(hT[:, k, ts_], wt[:, k, :]) for k in range(8)], [bw, hTr[tt // 4]], [b_ps[bank]])
                if ctx:
                    dve(lambda e, kt=kt, bank=bank: e.tensor_copy(out=vd[:, kt, :], in_=ps[bank][:]), [b_ps[bank]], [b_vd])
                else:
                    act(ost[tt % 2][:], ps[bank][:], AF.Identity, [b_ps[bank]], [b_ost[tt % 2]])
                    dve(lambda e, kt=kt, tt=tt: e.tensor_copy(out=vd[:, kt, :], in_=ost[tt % 2][:]), [b_ost[tt % 2]], [b_vd])
                    P.dma("sp", o_dv[s, l, r0:r0 + 128, :], ost[tt % 2][:], reads=[b_ost[tt % 2]])
            wt, bw = load_w(w_in[l][:, C_DG:C_DG + 512], 512, ("in", l, C_DG))
            for g in range(G):
                gs = slice(g * 512, (g + 1) * 512)
                for j in range(4):
                    bank = j % 3
                    mm(ps[bank][:], [(wt[:, k, j * 128:(j + 1) * 128], hT[:, k, gs]) for k in range(8)], [bw, hTr[g]], [b_ps[bank]])
                    act(yd[:, j, gs], ps[bank][:], AF.Silu, [b_ps[bank]], [b_yd])

            NQ = 512 if nseq == 1 else Ts
            PT = [PT2a, PT2b]
            b_PT = [b_PT2a, b_PT2b]
            for j in range(4):
                steps = []
                for qg in range(T // NQ):
                    s = 0 if nseq == 1 else qg
                    for kt in range(KT):
                        steps.append((qg, s, kt))

                def qk(i, j=j, steps=steps):
                    qg, s, kt = steps[i]
                    pair = (i % 2) * 2
                    qs = slice(qg * NQ, (qg + 1) * NQ)
                    kk_ = s * KT + kt
                    ks = slice(kk_ * 128, (kk_ + 1) * 128)
                    for cp in range(2):
                        rows = slice(cp * 64, (cp + 1) * 64)
                        if NQ == 512:
                            mm(ps[pair + cp][:], [(kd[rows, j, ks], qd[rows, j, qs])], [b_kd, b_qd], [b_ps[pair + cp]])
                        else:
                            mm(ps[pair + cp][:, 0:NQ], [(kd[rows, j, ks], qd[rows, j, qs])], [b_kd, b_qd], [b_ps[pair + cp]])

                def expv(i, j=j, steps=steps):
                    qg, s, kt = steps[i]
                    pair = (i % 2) * 2
                    pt = PT[i % 2]
                    b_pt = b_PT[i % 2]
                    qs = slice(qg * NQ, (qg + 1) * NQ)
                    kk_ = s * KT + kt
                    if NQ == 512:
                        act(pt[:, 0:1024], pall[:, pair * 512:(pair + 2) * 512], AF.Exp, [b_ps[pair], b_ps[pair + 1]], [b_pt], scale=DIFF_SCALE)
                    else:
                        act(pt[:, 0:2 * NQ].rearrange("p (b c) -> p b c", c=NQ),
                            pall[:, pair * 512:(pair + 2) * 512].rearrange("p (b c) -> p b c", c=512)[:, :, 0:NQ],
                            AF.Exp, [b_ps[pair], b_ps[pair + 1]], [b_pt], scale=DIFF_SCALE)
                    for cp in range(2):
                        mm(ps[4 + cp][:, 0:NQ], [(vd[:, kk_, j * 128:(j + 1) * 128], pt[:, cp * NQ:(cp + 1) * NQ])], [b_vd, b_pt], [b_ps[4 + cp]],
                           start=(kt == 0), stop=(kt == KT - 1))
                        mm(ps[6 + cp][:, 0:NQ], [(ones_b[:], pt[:, cp * NQ:(cp + 1) * NQ])], [b_ones, b_pt], [b_ps[6 + cp]],
                           start=(kt == 0), stop=(kt == KT - 1))
                    if kt == KT - 1:
                        act(r1[:, 0:NQ], ps[6][:, 0:NQ], AF.Ln, [b_ps[6]], [b_r1])
                        act(r2[:, 0:NQ], ps[7][:, 0:NQ], AF.Ln, [b_ps[7]], [b_r2])
                        act(r1[:, 0:NQ], r1[:, 0:NQ], AF.Exp, [b_r1], [b_r1], scale=-1.0)
                        act(r2[:, 0:NQ], r2[:, 0:NQ], AF.Exp, [b_r2], [b_r2], scale=-1.0)
                        dve(lambda e: e.tensor_tensor(out=ou[:, 0:NQ], in0=ps[4][:, 0:NQ], in1=r1[:, 0:NQ], op=ALU.mult), [b_ps[4], b_r1], [b_ou])
                        dve(lambda e: e.scalar_tensor_tensor(out=od[:, 0:NQ], in0=ps[5][:, 0:NQ], scalar=neglam[:, 0:1], in1=r2[:, 0:NQ], op0=ALU.mult, op1=ALU.mult),
                            [b_ps[5], b_r2, b_par], [b_od])
                        dve(lambda e: e.tensor_tensor(out=od[:, 0:NQ], in0=od[:, 0:NQ], in1=ou[:, 0:NQ], op=ALU.add), [b_od, b_ou], [b_od])
                        dve(lambda e: e.tensor_tensor(out=sqd[:, 0:NQ], in0=od[:, 0:NQ], in1=od[:, 0:NQ], op=ALU.mult), [b_od], [b_sqd])

                        def part2(nb, j=j, qs=qs):
                            mm(ps[nb][:, 0:NQ], [(ones_b[:], sqd[:, 0:NQ])], [b_ones, b_sqd], [b_ps[nb]])
                            act(ou[:, 0:NQ], ps[nb][:, 0:NQ], AF.Ln, [b_ps[nb]], [b_ou], scale=1.0 / 128.0, bias=EPS)
                            act(ou[:, 0:NQ], ou[:, 0:NQ], AF.Exp, [b_ou], [b_ou], scale=-0.5)
                            dve(lambda e: e.scalar_tensor_tensor(out=od[:, 0:NQ], in0=od[:, 0:NQ], scalar=dn[:, 0:1], in1=ou[:, 0:NQ], op0=ALU.mult, op1=ALU.mult),
                                [b_od, b_ou, b_par], [b_od])
                            dve(lambda e: e.tensor_tensor(out=yd[:, j, qs], in0=od[:, 0:NQ], in1=yd[:, j, qs], op=ALU.mult), [b_od, b_yd], [b_yd])
                        pend.append((i + min(2, KT - 1), part2))

                pend = []
                qk(0)
                for i in range(len(steps)):
                    while pend and pend[0][0] <= i:
                        pend.pop(0)[1](((i + 1) % 2) * 2)
                    if i + 1 < len(steps):
                        qk(i + 1)
                    expv(i)
                while pend:
                    pend.pop(0)[1](0)
            P.barrier(scratch1[:, 0:1])

    def phase_merge_out(l, U, ybr, b_ybr):
        T = U["T"]
        G = T // 512
        u = U["u"]
        hTr = b_hT[:G]
        xsrc = U["x"][l]
        xdst = U["x"][l + 1]
        with ExitStack() as es:
            def loc(name, shape, dt=F32):
                return es.enter_context(nc.sbuf_tensor(un(name), list(shape), dt)), Buf(name, local=True)
            mg, b_mg = loc("merged", [128, 8, T], BF16)
            esA = ExitStack()

            def locA(name, shape, dt=F32):
                return esA.enter_context(nc.sbuf_tensor(un(name), list(shape), dt)), Buf(name, local=True)
            loc_save = loc
            loc = locA
            wbr2 = []; b_wbr2 = []
            for i in range(2):
                t, b = loc("wbr2_%d" % i, [128, 3, 4, 256], BF16); wbr2.append(t); b_wbr2.append(b)
            gsb = []; b_gsb = []
            for i in range(4):
                t, b = loc("gsb%d" % i, [128, 512]); gsb.append(t); b_gsb.append(b)
            mt = []; b_mt = []
            for i in range(4):
                t, b = loc("mt%d" % i, [128, 512]); mt.append(t); b_mt.append(b)
            it = 0
            hslot = {"n": 0, "cur": None}

            def half_slot():
                if hslot["n"] % 2 == 0:
                    hslot["cur"] = ring_slot()
                t, b = hslot["cur"]
                v = t[:].rearrange("p c n -> p (c n)")[:, (hslot["n"] % 2) * 2048:(hslot["n"] % 2 + 1) * 2048].rearrange("p (c n) -> p c n", n=256)
                hslot["n"] += 1
                return v, b

            def load_group(f2):
                wi = f2 % 2
                wm = []
                for b in range(3):
                    c0 = C_MGATE + b * D + f2 * 256
                    v, bb = half_slot()
                    cast_load(v, bb, w_in[l][:, c0:c0 + 256].rearrange("(c p) n -> p c n", p=128), ("mg2", l, b, f2))
                    wm.append((v, bb))
                    cast_load(wbr2[wi][:, b, :, :], b_wbr2[wi], w_br[b][l][:, f2 * 256:(f2 + 1) * 256].rearrange("(c p) n -> p c n", p=128), ("br2", l, b, f2))
                return wm

            nxt = load_group(0)
            for f2 in range(4):
                wm2 = nxt
                wi = f2 % 2
                if f2 + 1 < 4:
                    nxt = load_group(f2 + 1)
                for g in range(G):
                    gs = slice(g * 512, (g + 1) * 512)
                    for fi in range(2):
                        f = f2 * 2 + fi
                        fs = slice(fi * 128, (fi + 1) * 128)
                        ma = (f * G + g) % 2
                        for b in range(3):
                            r = it % 4
                            it += 1
                            bA, bB = 2 * r, 2 * r + 1
                            mm(ps[bA][:], [(wm2[b][0][:, k, fs], hT[:, k, gs]) for k in range(8)], [wm2[b][1], hTr[g]], [b_ps[bA]])
                            act(gsb[r][:], ps[bA][:], AF.Sigmoid, [b_ps[bA]], [b_gsb[r]])
                            mm(ps[bB][:], [(wbr2[wi][:, b, k, fs], ybr[b][:, k, gs]) for k in range(4)], [b_wbr2[wi], b_ybr[b]], [b_ps[bB]])
                            if b == 0:
                                dve(lambda e, r=r, bB=bB, ma=ma: e.tensor_tensor(out=mt[ma][:], in0=ps[bB][:], in1=gsb[r][:], op=ALU.mult), [b_ps[bB], b_gsb[r]], [b_mt[ma]])
                            else:
                                dve(lambda e, r=r, bB=bB, ma=ma: e.tensor_tensor(out=mt[2 + ma][:], in0=ps[bB][:], in1=gsb[r][:], op=ALU.mult), [b_ps[bB], b_gsb[r]], [b_mt[2 + ma]])
                                if b == 1:
                                    dve(lambda e, ma=ma: e.tensor_tensor(out=mt[ma][:], in0=mt[ma][:], in1=mt[2 + ma][:], op=ALU.add), [b_mt[ma], b_mt[2 + ma]], [b_mt[ma]])
                                else:
                                    dve(lambda e, ma=ma, f=f, gs=gs: e.tensor_tensor(out=mg[:, f, gs], in0=mt[ma][:], in1=mt[2 + ma][:], op=ALU.add), [b_mt[ma], b_mt[2 + ma]], [b_mg])
            P.barrier(scratch1[:, 0:1])
            esA.close()
            loc = loc_save
            wo = []
            for half in range(2):
                wo.append(load_w(w_out[l][:, half * 512:(half + 1) * 512], 512, ("out", l, half)))
            xt = []; b_xt = []; ot = []; b_ot = []
            for i in range(2):
                t, b = loc("oxt%d" % i, [128, D]); xt.append(t); b_xt.append(b)
                t, b = loc("oot%d" % i, [128, D]); ot.append(t); b_ot.append(b)
            junk, b_junk = loc("ojunk", [128, 512]); ss2, b_ss2 = loc("oss2", [128, 4])
            for tt in range(T // 128):
                i = tt % 2
                ts_ = slice(tt * 128, (tt + 1) * 128)
                rdx = [b_x1[(U["name"], tt)]] if l > 0 else []
                P.dma("sp", xt[i][:], xsrc[ts_, :], reads=rdx, writes=[b_xt[i]])
                for half in range(2):
                    bank = 6 + half
                    mm(ps[bank][:], [(mg[:, k, ts_], wo[half][0][:, k, :]) for k in range(8)], [wo[half][1], b_mg], [b_ps[bank]])
                    act(junk[:], ps[bank][:], AF.Square, [b_ps[bank]], [b_junk, b_ss2], accum_out=ss2[:, half:half + 1])
                dve(lambda e: e.tensor_tensor(out=ss2[:, 2:3], in0=ss2[:, 0:1], in1=ss2[:, 1:2], op=ALU.add), [b_ss2], [b_ss2])
                act(ss2[:, 3:4], ss2[:, 2:3], AF.Sqrt, [b_ss2], [b_ss2], scale=1.0 / D, bias=EPS)
                dve(lambda e: e.reciprocal(out=ss2[:, 3:4], in_=ss2[:, 3:4]), [b_ss2], [b_ss2])
                for half in range(2):
                    bank = 6 + half
                    hs = slice(half * 512, (half + 1) * 512)
                    dve(lambda e, i=i, bank=bank, hs=hs: e.scalar_tensor_tensor(out=ot[i][:, hs], in0=ps[bank][:], scalar=ss2[:, 3:4], in1=gg[u][:, hs], op0=ALU.mult, op1=ALU.mult),
                        [b_ps[bank], b_ss2, b_par], [b_ot[i]])
                dve(lambda e, i=i: e.tensor_tensor(out=ot[i][:], in0=ot[i][:], in1=xt[i][:], op=ALU.add), [b_ot[i], b_xt[i]], [b_ot[i]])
                wr = [b_x1[(U["name"], tt)]] if l == 0 else []
                P.dma("sp", xdst[ts_, :], ot[i][:], reads=[b_ot[i]], writes=wr)
            P.barrier(scratch1[:, 0:1])

    class _Stop(Exception):
        pass

    def chk(l, U, ph):
        if stop is not None and tuple(stop[:3]) == (l, U["name"] if U else None, ph):
            raise _Stop()

    try:
        for l in range(DEPTH):
            layer_params(l)
            chk(l, None, "params")
            for U in units:
                T = U["T"]
                if stop is not None and len(stop) > 3 and U["name"] not in stop[3]:
                    continue
                phase_h(l, U)
                chk(l, U, "h")
                with ExitStack() as es:
                    ym = es.enter_context(nc.sbuf_tensor(un("ym"), [128, 4, T], BF16)); b_ym = Buf("ym", local=True)
                    phase_mla(l, U, ym, b_ym)
                    chk(l, U, "mla")
                    yd = es.enter_context(nc.sbuf_tensor(un("yd"), [128, 4, T], BF16)); b_yd = Buf("yd", local=True)
                    phase_diff(l, U, yd, b_yd)
                    chk(l, U, "diff")
                    yr = es.enter_context(nc.sbuf_tensor(un("yr"), [128, 4, T], BF16)); b_yr = Buf("yr", local=True)
                    phase_rnn(l, U, yr, b_yr)
                    chk(l, U, "rnn")
                    phase_merge_out(l, U, [yr, ym, yd], [b_yr, b_ym, b_yd])
                    chk(l, U, "merge")
    except _Stop:
        pass
    P.finish()
    return nc, P


_CACHE = {}


def make_in_maps(inp, n=8):
    f32 = lambda a: np.ascontiguousarray(np.asarray(a, dtype=np.float32))
    cos, sin, rdT, ident = rope_consts()
    shared = {k: f32(inp[k]) for k in (
        "w_mod", "b_mod", "g_pre", "g_post", "w_in", "conv_w", "conv_b", "w_rg_a", "b_rg_a", "w_rg_x", "b_rg_x", "rg_lam",
        "q_norm", "w_uq", "kv_norm", "w_ukv", "lam_q1", "lam_k1", "lam_q2", "lam_k2", "diff_norm",
        "w_br_rnn", "w_br_mla", "w_br_diff", "w_out")}
    shared.update({"cosd": cos, "sind": sin, "rdT": rdT, "ident": ident})
    x_prompt = f32(inp["x_prompt"]); x_sample = f32(inp["x_sample"])
    c = f32(inp["c"]); c_ctx = f32(inp["c_ctx"])
    in_maps = []
    for i in range(n):
        m = dict(shared)
        m["xs"] = x_sample[i]
        m["xp"] = x_prompt[NPB * i:NPB * (i + 1)].reshape(TP, D)
        m["ckv_c"] = f32(inp["cache_mla_ckv"][i])
        m["kr_c"] = f32(inp["cache_mla_krope"][i])
        m["dk_c"] = f32(inp["cache_diff_k"][i]).reshape(DEPTH, PAST, 512)
        m["dv_c"] = f32(inp["cache_diff_v"][i]).reshape(DEPTH, PAST, 512)
        m["st_c"] = f32(inp["state_rnn"][i])
        m["cvec"] = np.stack([c[i], c_ctx], axis=0)
        in_maps.append(m)
    return in_maps


def kernel(**inp):
    n = 8
    if "nc" not in _CACHE:
        _CACHE["nc"] = build_program()[0]
    nc = _CACHE["nc"]
    in_maps = make_in_maps(inp, n)
    res = run_bass_kernel_spmd(nc, in_maps, core_ids=list(range(n)))
    R = res.results
    y_prompt = np.concatenate([R[i]["yp"].reshape(NPB, SP, D) for i in range(n)], axis=0)
    y_sample = np.stack([R[i]["ys"] for i in range(n)], axis=0)
    new_ckv = np.concatenate([R[i]["o_ckv"] for i in range(n)], axis=0)
    new_kr = np.concatenate([R[i]["o_kr"] for i in range(n)], axis=0)
    new_dk = np.concatenate([R[i]["o_dk"].reshape(NPB, DEPTH, SP, 4, 128) for i in range(n)], axis=0)
    new_dv = np.concatenate([R[i]["o_dv"].reshape(NPB, DEPTH, SP, 4, 128) for i in range(n)], axis=0)
    new_st = np.concatenate([R[i]["o_st"] for i in range(n)], axis=0)
    return (y_prompt.astype(np.float32), y_sample.astype(np.float32), new_ckv.astype(np.float32), new_kr.astype(np.float32),
            new_dk.astype(np.float32), new_dv.astype(np.float32), new_st.astype(np.float32))
```

```python
import math
from contextlib import ExitStack
import numpy as np
import concourse.bass as bass
import concourse.mybir as mybir
from concourse.bass_utils import run_bass_kernel_spmd

F32 = mybir.dt.float32
BF16 = mybir.dt.bfloat16
AF = mybir.ActivationFunctionType
ALU = mybir.AluOpType
AX = mybir.AxisListType

D = 1024
DEPTH = 2
TS = 2048
NPB = 4
SP = 256
TP = NPB * SP
PAST = 512
EPS = 1e-6
IN_COLS = 7328
C_RX, C_RG, C_CQ, C_CKV, C_KR, C_MG, C_DQ, C_DK, C_DV, C_DG, C_MGATE = (
    0, 512, 1024, 1408, 1664, 1696, 2208, 2720, 3232, 3744, 4256)
MLA_SCALE = 96 ** -0.5
DIFF_SCALE = 64 ** -0.5
THETA = 10000.0
STRICT = False


class Buf:
    __slots__ = ("name", "w", "r", "local", "excl")

    def __init__(self, name, local=False, excl=False):
        self.name = name
        self.w = None
        self.r = {}
        self.local = local
        self.excl = excl


class Prog:
    def __init__(self, nc):
        self.nc = nc
        self.eng = {"pe": nc.tensor, "act": nc.scalar, "dve": nc.vector, "pool": nc.gpsimd, "sp": nc.sync}
        self.sem = {}
        self.cnt = {e: 0 for e in self.eng}
        self.seen = {e: {} for e in self.eng}
        for e in self.eng:
            self.sem["e:" + e] = nc.alloc_semaphore("sem_" + e)
        self.dq = {}
        for q, k in (("sp", 24), ("pool", 24), ("act", 6)):
            keys = []
            for j in range(k):
                key = "d:%s%d" % (q, j)
                self.sem[key] = nc.alloc_semaphore("dsem_%s%d" % (q, j))
                keys.append(key)
            self.dq[q] = {"i": 0, "keys": keys}
        self.phase_tok = None
        self.n_wait = 0
        self.log = {e: [] for e in self.eng}

    def _wait(self, e, deps):
        for k, v in deps.items():
            if self.seen[e].get(k, 0) >= v:
                continue
            self.eng[e].wait_ge(self.sem[k], v)
            self.seen[e][k] = v
            self.n_wait += 1
            self.log[e].append(("wait", k, v))

    def _deps(self, e, reads, writes, is_dma):
        deps = {}
        own = "e:" + e

        def add(tok, raw=False):
            if tok is None:
                return
            k, v = tok
            if (not is_dma) and k == own and (e == "pe" or not (raw or STRICT)):
                return
            if deps.get(k, 0) < v:
                deps[k] = v

        loc = False
        for b in reads:
            add(b.w, raw=True)
            loc = loc or b.local
        for b in writes:
            add(b.w)
            for k, v in b.r.items():
                add((k, v))
            loc = loc or b.local
        if loc:
            add(self.phase_tok)
        return deps

    def _record(self, tok, reads, writes):
        k, v = tok
        for b in reads:
            if b.r.get(k, 0) < v:
                b.r[k] = v
        for b in writes:
            b.w = tok
            b.r = {}

    def op(self, e, fn, reads=(), writes=()):
        ex = [b for b in reads if b.excl]
        if ex:
            reads = [b for b in reads if not b.excl]
            writes = list(writes) + ex
        self._wait(e, self._deps(e, reads, writes, False))
        ins = fn(self.eng[e])
        self.cnt[e] += 1
        ins.then_inc(self.sem["e:" + e], 1)
        self.log[e].append(("inc", "e:" + e, 1))
        self._record(("e:" + e, self.cnt[e]), reads, writes)

    def dma(self, q, out, in_, reads=(), writes=(), slow=False):
        st = self.dq[q]
        i = st["i"]
        kk = len(st["keys"])
        key = st["keys"][i % kk]
        deps = self._deps(q, reads, writes, True)
        if i >= kk:
            deps[key] = max(deps.get(key, 0), 16 * (i // kk))
        self._wait(q, deps)
        if slow:
            ins = self.eng[q].dma_start(out=out, in_=in_, allow_slow_non_contiguous=True)
        else:
            ins = self.eng[q].dma_start(out=out, in_=in_)
        ins.then_inc(self.sem[key], 16)
        self.log[q].append(("inc", key, 16))
        st["i"] = i + 1
        self._record((key, 16 * (i // kk + 1)), reads, writes)

    def _all_tokens(self):
        deps = {}
        for e in self.eng:
            if self.cnt[e] > 0:
                deps["e:" + e] = self.cnt[e]
        for q, st in self.dq.items():
            kk = len(st["keys"])
            for j, key in enumerate(st["keys"]):
                uses = (st["i"] - j + kk - 1) // kk if st["i"] > j else 0
                if uses > 0:
                    deps[key] = 16 * uses
        return deps

    def barrier(self, scratch):
        deps = self._all_tokens()
        self._wait("dve", deps)
        ins = self.eng["dve"].memset(scratch, 0.0)
        self.cnt["dve"] += 1
        ins.then_inc(self.sem["e:dve"], 1)
        self.log["dve"].append(("inc", "e:dve", 1))
        self.phase_tok = ("e:dve", self.cnt["dve"])

    def finish(self):
        deps = self._all_tokens()
        self._wait("sp", deps)


def rope_consts():
    t = np.arange(TS)
    prow = (t // 64).astype(np.float64)
    pcol = (t % 64).astype(np.float64)
    inv = THETA ** (-np.arange(16, dtype=np.float64) / 16.0)
    cos = np.zeros((128, TS), np.float64)
    sin = np.zeros((128, TS), np.float64)
    for p in range(128):
        q = p % 64
        pos = prow if q < 32 else pcol
        f = inv[q % 16]
        cos[p] = np.cos(pos * f)
        sin[p] = np.sin(pos * f)
    R = np.zeros((128, 128), np.float32)
    for i in range(128):
        if i % 32 < 16:
            R[i, i + 16] = -1.0
        else:
            R[i, i - 16] = 1.0
    return cos.astype(np.float32), sin.astype(np.float32), np.ascontiguousarray(R.T), np.eye(128, dtype=np.float32)


def build_program(stop=None):
    nc = bass.Bass("TRN2", target_bir_lowering=False)
    P = Prog(nc)

    def din(name, shape):
        return nc.dram_tensor(name, list(shape), F32, kind="ExternalInput").ap()

    def dout(name, shape):
        return nc.dram_tensor(name, list(shape), F32, kind="ExternalOutput").ap()

    xs = din("xs", [TS, D])
    xp = din("xp", [TP, D])
    ckv_c = din("ckv_c", [DEPTH, PAST, 256])
    kr_c = din("kr_c", [DEPTH, PAST, 32])
    dk_c = din("dk_c", [DEPTH, PAST, 512])
    dv_c = din("dv_c", [DEPTH, PAST, 512])
    st_c = din("st_c", [DEPTH, 2, 512])
    cvec = din("cvec", [2, D])
    w_mod = din("w_mod", [DEPTH, D, 3 * D])
    b_mod = din("b_mod", [DEPTH, 3 * D])
    g_pre = din("g_pre", [DEPTH, D])
    g_post = din("g_post", [DEPTH, D])
    w_in = din("w_in", [DEPTH, D, IN_COLS])
    conv_w = din("conv_w", [DEPTH, 4, 512])
    conv_b = din("conv_b", [DEPTH, 512])
    w_rg_a = din("w_rg_a", [DEPTH, 2, 8, 64, 64])
    b_rg_a = din("b_rg_a", [DEPTH, 2, 512])
    w_rg_x = din("w_rg_x", [DEPTH, 2, 8, 64, 64])
    b_rg_x = din("b_rg_x", [DEPTH, 2, 512])
    rg_lam = din("rg_lam", [DEPTH, 2, 512])
    q_norm = din("q_norm", [DEPTH, 384])
    w_uq = din("w_uq", [DEPTH, 384, 768])
    kv_norm = din("kv_norm", [DEPTH, 256])
    w_ukv = din("w_ukv", [DEPTH, 256, 1024])
    lam_q1 = din("lam_q1", [DEPTH, 64])
    lam_k1 = din("lam_k1", [DEPTH, 64])
    lam_q2 = din("lam_q2", [DEPTH, 64])
    lam_k2 = din("lam_k2", [DEPTH, 64])
    diff_norm = din("diff_norm", [DEPTH, 128])
    w_br = [din("w_br_rnn", [DEPTH, 512, D]), din("w_br_mla", [DEPTH, 512, D]), din("w_br_diff", [DEPTH, 512, D])]
    w_out = din("w_out", [DEPTH, D, D])
    cosd_d = din("cosd", [128, TS])
    sind_d = din("sind", [128, TS])
    rdT_d = din("rdT", [128, 128])
    ident_d = din("ident", [128, 128])

    ys = dout("ys", [TS, D])
    yp = dout("yp", [TP, D])
    o_ckv = dout("o_ckv", [NPB, DEPTH, SP, 256])
    o_kr = dout("o_kr", [NPB, DEPTH, SP, 32])
    o_dk = dout("o_dk", [NPB, DEPTH, SP, 512])
    o_dv = dout("o_dv", [NPB, DEPTH, SP, 512])
    o_st = dout("o_st", [NPB, DEPTH, 2, 512])
    x1s = nc.dram_tensor("x1s", [TS, D], F32).ap()
    x1p = nc.dram_tensor("x1p", [TP, D], F32).ap()

    uid = {"n": 0}

    def un(name):
        uid["n"] += 1
        return "%s_%d" % (name, uid["n"])

    def sb(name, shape, dt=F32):
        return nc.alloc_sbuf_tensor(un("sb_" + name), list(shape), dt)

    ident = sb("ident", [128, 128]); b_ident = Buf("ident")
    rdT = sb("rdT", [128, 128]); b_rdT = Buf("rdT")
    cosd = sb("cosd_sb", [128, TS]); sind = sb("sind_sb", [128, TS]); b_tab = Buf("tab")
    ones_b = sb("ones_b", [128, 128], BF16); ones_f = sb("ones_f", [128, 128]); b_ones = Buf("ones")
    scratch1 = sb("scratch1", [128, 8])
    ccol = sb("ccol", [128, 8, 2]); scf = sb("scf", [128, 8, 2]); scb = sb("scb", [128, 8, 2], BF16)
    screp = sb("screp", [128, 2, 8, 128], BF16); b_sc = Buf("sc")
    hT = sb("hT", [128, 8, TS], BF16)
    b_hT = [Buf("hT%d" % g) for g in range(TS // 512)]
    NW = 3
    wbuf = [sb("wbuf%d" % i, [128, 8, 512], BF16) for i in range(NW)]
    b_wbuf = [Buf("wbuf%d" % i) for i in range(NW)]
    wstate = {"i": 0}
    bm = sb("bm", [128, 24]); gpre = sb("gpre", [128, 8]); modc = sb("modc", [128, 24, 2]); a1 = sb("a1", [128, 8, 2])
    tmp8 = sb("tmp8", [128, 8]); modraw = sb("modraw", [128, 48])
    gg = [sb("gg%d" % u, [128, D]) for u in range(2)]
    cw = sb("cw", [128, 4, 4]); cb = sb("cb", [128, 4]); bga = sb("bga", [128, 2, 4]); bgx = sb("bgx", [128, 2, 4])
    lam = sb("lam", [128, 2, 4]); sl = sb("sl", [128, 2, 4]); sl2 = sb("sl2", [128, 2, 4]); h0c = sb("h0c", [128, 2, 4])
    lt = [sb("lt%d" % i, [128, 2, 4]) for i in range(4)]
    bd = sb("bd", [128, 2, 2, 4, 128], BF16)
    qn = sb("qn", [128, 3]); kvn = sb("kvn", [128, 2]); kvn_bc = sb("kvn_bc", [128, 256])
    dn = sb("dn", [128, 1]); neglam = sb("neglam", [128, 1]); lamw = sb("lamw", [128, 4, 64]); lame = sb("lame", [128, 2])
    b_par = Buf("params")
    pall = nc.alloc_psum_tensor("pall", [128, 8 * 512], F32)
    ps = [pall[:, i * 512:(i + 1) * 512] for i in range(8)]
    b_ps = [Buf("ps%d" % i, excl=True) for i in range(8)]

    wcache = {}

    def cast_load(dst_ap, dst_buf, src_ap, key):
        if key is None:
            P.dma("pool", dst_ap, src_ap, writes=[dst_buf])
            return
        if key in wcache:
            sc, b_sc = wcache[key]
            P.dma("pool", dst_ap, sc, reads=[b_sc], writes=[dst_buf])
            return
        P.dma("pool", dst_ap, src_ap, writes=[dst_buf])
        sc = nc.dram_tensor(un("wsc"), list(dst_ap.shape), BF16).ap()
        b_sc = Buf("wsc")
        P.dma("sp", sc, dst_ap, reads=[dst_buf], writes=[b_sc])
        wcache[key] = (sc, b_sc)

    def load_w(src_ap, ncols, key):
        i = wstate["i"] % NW
        wstate["i"] += 1
        cast_load(wbuf[i][:, :, 0:ncols], b_wbuf[i], src_ap.rearrange("(c p) n -> p c n", p=128), key)
        return wbuf[i], b_wbuf[i]

    dbg = {}

    def dbg_dump(name, tile_ap, shape, reads):
        return

    def ring_slot():
        i = wstate["i"] % NW
        wstate["i"] += 1
        return wbuf[i], b_wbuf[i]

    def mm(ps_ap, pairs, reads, writes, start=True, stop=True):
        def fn(pe):
            n = len(pairs)
            ins = None
            for i, (lh, rh) in enumerate(pairs):
                ins = pe.matmul(ps_ap, lhsT=lh, rhs=rh, start=(start and i == 0), stop=(stop and i == n - 1))
            return ins
        P.op("pe", fn, reads=reads, writes=writes)

    def act(out, in_, func, reads, writes, **kw):
        P.op("act", lambda e: e.activation(out=out, in_=in_, func=func, **kw), reads=reads, writes=writes)

    def dve(fn, reads, writes):
        P.op("dve", fn, reads=reads, writes=writes)

    P.dma("sp", ident[:], ident_d, writes=[b_ident])
    P.dma("sp", rdT[:], rdT_d, writes=[b_rdT])
    P.dma("sp", cosd[:], cosd_d, writes=[b_tab])
    P.dma("sp", sind[:], sind_d, writes=[b_tab])
    dve(lambda e: e.memset(ones_b[:], 1.0), [], [b_ones])
    dve(lambda e: e.memset(ones_f[:], 1.0), [], [b_ones])
    for u in range(2):
        P.dma("sp", ccol[:, :, u], cvec[u].rearrange("(c p) -> p c", p=128), writes=[b_sc], slow=True)
    act(scf[:], ccol[:], AF.Silu, [b_sc], [b_sc])
    act(scb[:], scf[:], AF.Identity, [b_sc], [b_sc])
    for u in range(2):
        for k in range(8):
            act(screp[:, u, k, :], ones_f[:], AF.Identity, [b_sc, b_ones], [b_sc], scale=scf[:, k, u:u + 1])

    units = [
        dict(name="s", u=0, T=TS, nseq=1, Ts=TS, ctx=PAST, rope=True, x=[xs, x1s, ys]),
        dict(name="p", u=1, T=TP, nseq=NPB, Ts=SP, ctx=0, rope=False, x=[xp, x1p, yp]),
    ]
    b_x1 = {("s", t): Buf("x1s%d" % t) for t in range(TS // 128)}
    b_x1.update({("p", t): Buf("x1p%d" % t) for t in range(TP // 128)})

    def layer_params(l):
        lam_init = 0.8 - 0.6 * math.exp(-0.3 * l)
        P.barrier(scratch1[:, 0:1])
        rd = [b_par]
        ldb = []

        def nb():
            ldb.append(Buf("pload", local=True))
            return ldb[-1]
        P.dma("sp", bm[:], b_mod[l].rearrange("(c p) -> p c", p=128), writes=[nb()], slow=True)
        P.dma("sp", gpre[:], g_pre[l].rearrange("(c p) -> p c", p=128), writes=[nb()], slow=True)
        for k in range(4):
            P.dma("sp", cw[:, :, k], conv_w[l, k].rearrange("(c p) -> p c", p=128), writes=[nb()], slow=True)
        P.dma("sp", cb[:], conv_b[l].rearrange("(c p) -> p c", p=128), writes=[nb()], slow=True)
        for d in range(2):
            P.dma("sp", bga[:, d, :], b_rg_a[l, d].rearrange("(c p) -> p c", p=128), writes=[nb()], slow=True)
        for d in range(2):
            P.dma("sp", bgx[:, d, :], b_rg_x[l, d].rearrange("(c p) -> p c", p=128), writes=[nb()], slow=True)
        for d in range(2):
            P.dma("sp", lam[:, d, :], rg_lam[l, d].rearrange("(c p) -> p c", p=128), writes=[nb()], slow=True)
        for d in range(2):
            P.dma("sp", h0c[:, d, :], st_c[l, d].rearrange("(c p) -> p c", p=128), writes=[nb()], slow=True)
        P.dma("sp", qn[:], q_norm[l].rearrange("(c p) -> p c", p=128), writes=[nb()], slow=True)
        P.dma("sp", kvn[:], kv_norm[l].rearrange("(c p) -> p c", p=128), writes=[nb()], slow=True)
        P.dma("sp", kvn_bc[:], kv_norm[l:l + 1, :].partition_broadcast(128), writes=[nb()])
        P.dma("sp", dn[:], diff_norm[l].rearrange("(c p) -> p c", p=128), writes=[nb()], slow=True)
        for i, t in enumerate((lam_q1, lam_k1, lam_q2, lam_k2)):
            P.dma("sp", lamw[:, i, :], t[l:l + 1, :].partition_broadcast(128), writes=[nb()])
        b_bd = Buf("bd0", local=True)
        dve(lambda e: e.memset(bd[:], 0.0), [], [b_bd])
        for gi, wg in enumerate((w_rg_a, w_rg_x)):
            for d in range(2):
                for hh in range(2):
                    src = wg[l, d].rearrange("(c two) k j -> two k c j", two=2)[hh]
                    P.dma("pool", bd[hh * 64:(hh + 1) * 64, d, gi, :, hh * 64:(hh + 1) * 64], src, reads=[b_bd], writes=[nb()])
        dve(lambda e: e.memset(tmp8[:, 0:1], 0.0), ldb + [b_bd], [b_par])
        act(lt[0][:], lam[:], AF.Exp, rd, [b_par], scale=-1.0)
        dve(lambda e: e.tensor_scalar(out=lt[1][:], in0=lt[0][:], scalar1=2.0, scalar2=None, op0=ALU.add), rd, [b_par])
        dve(lambda e: e.reciprocal(out=lt[1][:], in_=lt[1][:]), rd, [b_par])
        dve(lambda e: e.tensor_tensor(out=lt[0][:], in0=lt[0][:], in1=lt[1][:], op=ALU.mult), rd, [b_par])
        dve(lambda e: e.tensor_tensor(out=lt[1][:], in0=lt[0][:], in1=lt[0][:], op=ALU.mult), rd, [b_par])
        dve(lambda e: e.tensor_scalar(out=lt[2][:], in0=lt[1][:], scalar1=1.0 / 11.0, scalar2=1.0 / 9.0, op0=ALU.mult, op1=ALU.add), rd, [b_par])
        for cst in (1.0 / 7.0, 1.0 / 5.0, 1.0 / 3.0, 1.0):
            dve(lambda e: e.tensor_tensor(out=lt[2][:], in0=lt[2][:], in1=lt[1][:], op=ALU.mult), rd, [b_par])
            dve(lambda e, cst=cst: e.tensor_scalar(out=lt[2][:], in0=lt[2][:], scalar1=cst, scalar2=None, op0=ALU.add), rd, [b_par])
        dve(lambda e: e.tensor_tensor(out=lt[2][:], in0=lt[2][:], in1=lt[0][:], op=ALU.mult), rd, [b_par])
        dve(lambda e: e.tensor_scalar(out=sl[:], in0=lt[2][:], scalar1=-16.0, scalar2=None, op0=ALU.mult), rd, [b_par])
        dve(lambda e: e.tensor_scalar(out=sl2[:], in0=lt[2][:], scalar1=-32.0, scalar2=None, op0=ALU.mult), rd, [b_par])
        dve(lambda e: e.tensor_tensor(out=lamw[:, 0, :], in0=lamw[:, 0, :], in1=lamw[:, 1, :], op=ALU.mult), rd, [b_par])
        dve(lambda e: e.tensor_tensor(out=lamw[:, 2, :], in0=lamw[:, 2, :], in1=lamw[:, 3, :], op=ALU.mult), rd, [b_par])
        dve(lambda e: e.reduce_sum(out=lame[:, 0:1], in_=lamw[:, 0, :], axis=AX.X), rd, [b_par])
        dve(lambda e: e.reduce_sum(out=lame[:, 1:2], in_=lamw[:, 2, :], axis=AX.X), rd, [b_par])
        act(lame[:], lame[:], AF.Exp, rd, [b_par])
        dve(lambda e: e.tensor_tensor(out=neglam[:], in0=lame[:, 1:2], in1=lame[:, 0:1], op=ALU.subtract), rd, [b_par])
        dve(lambda e: e.tensor_scalar(out=neglam[:], in0=neglam[:], scalar1=-lam_init, scalar2=None, op0=ALU.add), rd, [b_par])
        dve(lambda e: e.tensor_scalar(out=dn[:], in0=dn[:], scalar1=1.0 - lam_init, scalar2=None, op0=ALU.mult), rd, [b_par])

        with ExitStack() as es:
            bmg = es.enter_context(nc.sbuf_tensor(un("bmg"), [128, D], F32)); gpb = es.enter_context(nc.sbuf_tensor(un("gpb"), [128, D], F32))
            b_l = Buf("modl", local=True)
            P.dma("sp", bmg[:], b_mod[l:l + 1, 2 * D:3 * D].partition_broadcast(128), writes=[b_l])
            P.dma("sp", gpb[:], g_post[l:l + 1, :].partition_broadcast(128), writes=[b_l])
            psm = ps[0][:].rearrange("p (c u) -> p c u", u=2)
            for pc in range(6):
                wt, bw = load_w(w_mod[l][:, pc * 512:(pc + 1) * 512], 512, None)
                for cc in range(4):
                    col = pc * 4 + cc
                    mm(psm[:, col, :], [(wt[:, k, cc * 128:(cc + 1) * 128], scb[:, k, :]) for k in range(8)],
                       [bw, b_sc], [b_ps[0]])
                if pc >= 4:
                    half = pc - 4
                    for u in range(2):
                        bank = 1 + u * 2 + half
                        mm(ps[bank][:], [(screp[:, u, k, :], wt[:, k, :]) for k in range(8)], [bw, b_sc], [b_ps[bank]])
            dve(lambda e: e.tensor_copy(out=modraw[:], in_=ps[0][:, 0:48]), [b_ps[0]], [b_par])
            dbg_dump("modraw", modraw[:], [128, 48], [b_par])
            mrv = modraw[:].rearrange("p (c u) -> p c u", u=2)
            for u in range(2):
                dve(lambda e, u=u: e.tensor_tensor(out=modc[:, :, u], in0=mrv[:, :, u], in1=bm[:], op=ALU.add), [b_par], [b_par])
                dve(lambda e, u=u: e.tensor_scalar(out=tmp8[:], in0=modc[:, 8:16, u], scalar1=1.0, scalar2=None, op0=ALU.add), rd, [b_par])
                dve(lambda e, u=u: e.tensor_tensor(out=a1[:, :, u], in0=tmp8[:], in1=gpre[:], op=ALU.mult), rd, [b_par])
                for half in range(2):
                    bank = 1 + u * 2 + half
                    hs = slice(half * 512, (half + 1) * 512)
                    dve(lambda e, u=u, bank=bank, hs=hs: e.tensor_tensor(out=gg[u][:, hs], in0=ps[bank][:], in1=bmg[:, hs], op=ALU.add), [b_ps[bank], b_l], [b_par])
                    dve(lambda e, u=u, hs=hs: e.tensor_tensor(out=gg[u][:, hs], in0=gg[u][:, hs], in1=gpb[:, hs], op=ALU.mult), [b_l, b_par], [b_par])
            P.barrier(scratch1[:, 0:1])
        if l == 0:
            dbg_dump("modc", modc[:], [128, 24, 2], [b_par])
            dbg_dump("a1", a1[:], [128, 8, 2], [b_par])
            dbg_dump("gg1", gg[1][:], [128, D], [b_par])
            dbg_dump("scf", scf[:], [128, 8, 2], [b_sc])
            dbg_dump("sl", sl[:], [128, 2, 4], [b_par])
            dbg_dump("neglam", neglam[:], [128, 1], [b_par])

    def phase_h(l, U):
        u = U["u"]
        xsrc = U["x"][l]
        with ExitStack() as es:
            xt = [es.enter_context(nc.sbuf_tensor(un("xt%d" % i), [128, 4, D], F32)) for i in range(2)]
            b_xt = [Buf("xt%d" % i, local=True) for i in range(2)]
            junk = es.enter_context(nc.sbuf_tensor(un("junk"), [128, D], F32)); b_junk = Buf("junk", local=True)
            ssq = [es.enter_context(nc.sbuf_tensor(un("ssq%d" % i), [128, 4], F32)) for i in range(2)]
            b_ss = [Buf("ssq%d" % i, local=True) for i in range(2)]
            for g in range(U["T"] // 512):
                i = g % 2
                rdx = [b_x1[(U["name"], 4 * g + j)] for j in range(4)] if l > 0 else []
                P.dma("sp", xt[i][:], xsrc[g * 512:(g + 1) * 512, :].rearrange("(j p) d -> p j d", p=128), reads=rdx, writes=[b_xt[i]])
                for j in range(4):
                    act(junk[:], xt[i][:, j, :], AF.Square, [b_xt[i]], [b_junk, b_ss[i]], accum_out=ssq[i][:, j:j + 1])
                act(ssq[i][:], ssq[i][:], AF.Sqrt, [b_ss[i]], [b_ss[i]], scale=1.0 / D, bias=EPS)
                dve(lambda e, i=i: e.reciprocal(out=ssq[i][:], in_=ssq[i][:]), [b_ss[i]], [b_ss[i]])
                for j in range(4):
                    dve(lambda e, i=i, j=j: e.tensor_scalar(out=xt[i][:, j, :], in0=xt[i][:, j, :], scalar1=ssq[i][:, j:j + 1], scalar2=None, op0=ALU.mult),
                        [b_ss[i], b_xt[i]], [b_xt[i]])
                for c in range(8):
                    bank = c % 4

                    def tr(pe, i=i, c=c, bank=bank):
                        ins = None
                        for j in range(4):
                            ins = pe.transpose(ps[bank][:, j * 128:(j + 1) * 128], xt[i][:, j, c * 128:(c + 1) * 128], ident[:])
                        return ins
                    P.op("pe", tr, reads=[b_xt[i], b_ident], writes=[b_ps[bank]])
                    dst = hT[:, c, g * 512:(g + 1) * 512]
                    if c % 2 == 0:
                        act(dst, ps[bank][:], AF.Identity, [b_ps[bank], b_par], [b_hT[g]], scale=a1[:, c, u:u + 1], bias=modc[:, c, u:u + 1])
                    else:
                        dve(lambda e, dst=dst, bank=bank, c=c: e.tensor_scalar(out=dst, in0=ps[bank][:], scalar1=a1[:, c, u:u + 1], scalar2=modc[:, c, u:u + 1], op0=ALU.mult, op1=ALU.add),
                            [b_ps[bank], b_par], [b_hT[g]])
            P.barrier(scratch1[:, 0:1])
            if l == 0 and U["name"] == "p":
                dbg_dump("hT", hT[:, :, 0:1024], [128, 8, 1024], hTr if False else b_hT[:2])

    def phase_rnn(l, U, yr, b_yr):
        T, nseq, Ts = U["T"], U["nseq"], U["Ts"]
        G = T // 512
        hTr = b_hT[:G]
        with ExitStack() as es:
            def loc(name, shape, dt=F32):
                return es.enter_context(nc.sbuf_tensor(un(name), list(shape), dt)), Buf(name, local=True)
            rx, b_rx = loc("rx", [128, T]); xc, b_xc = loc("xc", [128, T]); xcb, b_xcb = loc("xcb", [128, T], BF16)
            A, b_A = loc("rA", [128, T]); I, b_I = loc("rI", [128, T]); S, b_S = loc("rS", [128, T])
            A1, b_A1 = loc("rA1", [128, T]); I1, b_I1 = loc("rI1", [128, T])
            srg0, b_srg0 = loc("srg0", [128, 512]); srg1, b_srg1 = loc("srg1", [128, 512]); stc, b_stc0 = loc("stc", [128, 4, NPB, 2])
            b_stcs = [b_stc0] + [Buf("stc%d" % i, local=True) for i in range(1, 4)]
            wrx, bwrx = load_w(w_in[l][:, C_RX:C_RX + 512], 512, ("in", l, C_RX))
            wrg, bwrg = load_w(w_in[l][:, C_RG:C_RG + 512], 512, ("in", l, C_RG))
            for c in range(4):
                cs = slice(c * 128, (c + 1) * 128)
                for g in range(G):
                    gs = slice(g * 512, (g + 1) * 512)
                    bank = g % 4
                    mm(ps[bank][:], [(wrx[:, k, cs], hT[:, k, gs]) for k in range(8)], [bwrx, hTr[g]], [b_ps[bank]])
                    act(rx[:, gs], ps[bank][:], AF.Identity, [b_ps[bank]], [b_rx])
                dve(lambda e, c=c: e.tensor_scalar(out=xc[:], in0=rx[:], scalar1=cw[:, c, 2:3], scalar2=cb[:, c:c + 1], op0=ALU.mult, op1=ALU.add), [b_rx, b_par], [b_xc])
                for s in range(nseq):
                    t0 = s * Ts
                    for k, off in ((0, -2), (1, -1), (3, 1)):
                        if off < 0:
                            o_sl = slice(t0 - off, t0 + Ts); i_sl = slice(t0, t0 + Ts + off)
                        else:
                            o_sl = slice(t0, t0 + Ts - off); i_sl = slice(t0 + off, t0 + Ts)
                        dve(lambda e, c=c, k=k, o_sl=o_sl, i_sl=i_sl: e.scalar_tensor_tensor(out=xc[:, o_sl], in0=rx[:, i_sl], scalar=cw[:, c, k:k + 1], in1=xc[:, o_sl], op0=ALU.mult, op1=ALU.add),
                            [b_rx, b_par, b_xc], [b_xc])
                act(xcb[:], xc[:], AF.Identity, [b_xc], [b_xcb])
                Ad = [(A, b_A), (A1, b_A1)]
                Id = [(I, b_I), (I1, b_I1)]
                Td = [(rx, b_rx), (S, b_S)]
                for d in range(2):
                    A_, b_A_ = Ad[d]; I_, b_I_ = Id[d]; T_, b_T_ = Td[d]
                    for gi, (dst, b_dst, bias_t) in enumerate(((A_, b_A_, bga), (I_, b_I_, bgx))):
                        for g in range(G):
                            gs = slice(g * 512, (g + 1) * 512)
                            bank = 4 + (g % 4)
                            mm(ps[bank][:], [(bd[:, d, gi, c, :], xcb[:, gs])], [b_par, b_xcb], [b_ps[bank]])
                            act(dst[:, gs], ps[bank][:], AF.Sigmoid, [b_ps[bank], b_par], [b_dst], bias=bias_t[:, d, c:c + 1])
                    act(T_[:], A_[:], AF.Exp, [b_A_, b_par], [b_T_], scale=sl2[:, d, c:c + 1])
                    act(A_[:], A_[:], AF.Exp, [b_A_, b_par], [b_A_], scale=sl[:, d, c:c + 1])
                    act(T_[:], T_[:], AF.Sqrt, [b_T_], [b_T_], scale=-1.0, bias=1.0)
                for d in range(2):
                    A_, b_A_ = Ad[d]; I_, b_I_ = Id[d]; T_, b_T_ = Td[d]
                    dve(lambda e, I_=I_, T_=T_: e.tensor_tensor(out=I_[:], in0=I_[:], in1=T_[:], op=ALU.mult), [b_I_, b_T_], [b_I_])
                    dve(lambda e, I_=I_: e.tensor_tensor(out=I_[:], in0=I_[:], in1=xc[:], op=ALU.mult), [b_I_, b_xc], [b_I_])
                    for s in range(nseq):
                        ss_ = slice(s * Ts, (s + 1) * Ts)
                        init = h0c[:, d, c:c + 1] if U["ctx"] else 0.0
                        if d == 0:
                            dve(lambda e, ss_=ss_, init=init, T_=T_, A_=A_, I_=I_: e.tensor_tensor_scan(out=T_[:, ss_], data0=A_[:, ss_], data1=I_[:, ss_], initial=init, op0=ALU.mult, op1=ALU.add),
                                [b_A_, b_I_, b_par], [b_T_])
                        else:
                            rs_ = slice((s + 1) * Ts - 1, s * Ts - 1 if s > 0 else None, -1)
                            dve(lambda e, rs_=rs_, init=init, T_=T_, A_=A_, I_=I_: e.tensor_tensor_scan(out=T_[:, rs_], data0=A_[:, rs_], data1=I_[:, rs_], initial=init, op0=ALU.mult, op1=ALU.add),
                                [b_A_, b_I_, b_par], [b_T_])
                if not U["ctx"]:
                    for s in range(nseq):
                        dve(lambda e, s=s, c=c: e.tensor_copy(out=stc[:, c, s, 0:1], in_=rx[:, (s + 1) * Ts - 1:(s + 1) * Ts]), [b_rx], [b_stcs[c]])
                        dve(lambda e, s=s, c=c: e.tensor_copy(out=stc[:, c, s, 1:2], in_=S[:, s * Ts:s * Ts + 1]), [b_S], [b_stcs[c]])
                if not U["ctx"]:
                    for s in range(nseq):
                        P.dma("sp", o_st[s, l, :, c * 128:(c + 1) * 128].rearrange("d p -> p d"), stc[:, c, s, :], reads=[b_stcs[c]], slow=True)
                dve(lambda e: e.tensor_tensor(out=rx[:], in0=rx[:], in1=S[:], op=ALU.add), [b_rx, b_S], [b_rx])
                for g in range(G):
                    gs = slice(g * 512, (g + 1) * 512)
                    bank = g % 4
                    mm(ps[bank][:], [(wrg[:, k, cs], hT[:, k, gs]) for k in range(8)], [bwrg, hTr[g]], [b_ps[bank]])
                    srg, b_srg = (srg0, b_srg0) if g % 2 == 0 else (srg1, b_srg1)
                    act(srg[:], ps[bank][:], AF.Silu, [b_ps[bank]], [b_srg])
                    dve(lambda e, gs=gs, c=c, srg=srg: e.tensor_tensor(out=yr[:, c, gs], in0=rx[:, gs], in1=srg[:], op=ALU.mult), [b_rx, b_srg], [b_yr])
            P.barrier(scratch1[:, 0:1])

    def rope_evac(dst, psrc, bank_r, g, rows, reads, writes, xr, b_xr, t1, b_t1):
        gs = slice(g * 512, (g + 1) * 512)
        n = rows.stop - rows.start
        act(xr[rows, :], psrc, AF.Identity, reads, [b_xr])
        mm(ps[bank_r][rows, :], [(rdT[rows, rows], xr[rows, :])], [b_xr, b_rdT], [b_ps[bank_r]])
        dve(lambda e: e.tensor_tensor(out=t1[rows, :], in0=ps[bank_r][rows, :], in1=sind[rows, gs], op=ALU.mult), [b_ps[bank_r], b_tab], [b_t1])
        dve(lambda e: e.tensor_tensor(out=xr[rows, :], in0=xr[rows, :], in1=cosd[rows, gs], op=ALU.mult), [b_xr, b_tab], [b_xr])
        dve(lambda e: e.tensor_tensor(out=dst, in0=xr[rows, :], in1=t1[rows, :], op=ALU.add), [b_xr, b_t1], writes)

    def phase_mla(l, U, ym, b_ym):
        T, nseq, Ts, ctx = U["T"], U["nseq"], U["Ts"], U["ctx"]
        G = T // 512
        Tk = Ts + ctx
        KT = Tk // 128
        TkA = nseq * Tk
        hTr = b_hT[:G]
        with ExitStack() as es:
            def loc(name, shape, dt=F32):
                return es.enter_context(nc.sbuf_tensor(un(name), list(shape), dt)), Buf(name, local=True)
            cqn, b_cqn = loc("cqn", [128, 3, T], BF16)
            ckvn, b_ckvn = loc("ckvn", [128, 2, TkA], BF16)
            Kh = []; b_Kh = []
            for i in range(2):
                t, b = loc("Kh%d" % i, [128, TkA], BF16); Kh.append(t); b_Kh.append(b)
            Qh = []; b_Qh = []
            for i in range(2):
                t, b = loc("Qh%d" % i, [128, T], BF16); Qh.append(t); b_Qh.append(b)
            Vt = []; b_Vt = []
            for i in range(2):
                t, b = loc("Vt%d" % i, [128, nseq * KT, 2, 128], BF16); Vt.append(t); b_Vt.append(b)
            PT2a, b_PT2a = loc("PT2a", [128, 1024], BF16); PT2b, b_PT2b = loc("PT2b", [128, 1024], BF16)
            xr, b_xr = loc("xr", [128, 512]); t1, b_t1 = loc("t1", [128, 512])
            xr2, b_xr2 = loc("xr2", [128, 512]); t12, b_t12 = loc("t12", [128, 512])
            sq, b_sq = loc("sq", [128, 3, 512], BF16)
            rs, b_rs = loc("rs", [128, 512])
            rc, b_rc = loc("rc", [128, 512]); ot, b_ot = loc("ot", [128, 512])
            wq2, b_wq2 = loc("wq2", [128, 3, 8, 128], BF16)
            wkv2, b_wkv2 = loc("wkv2", [128, 2, 8, 128], BF16)
            wkr2, b_wkr2 = loc("wkr2", [128, 8, 64], BF16)
            ost, b_ost = loc("ost", [128, 288]); ssv, b_ssv = loc("ssv", [128, 2])
            es2 = es.enter_context(ExitStack())

            def loc2(name, shape, dt=F32):
                return es2.enter_context(nc.sbuf_tensor(un(name), list(shape), dt)), Buf(name, local=True)
            wq_slot, b_wq_raw = ring_slot()
            wq_raw = wq_slot[:].rearrange("p c n -> p (c n)")[:, 0:2304].rearrange("p (c n) -> p c n", n=768)
            wkv_slot, b_wkv_raw = ring_slot()
            wkv_raw = wkv_slot[:].rearrange("p c n -> p (c n)")[:, 0:2048].rearrange("p (c n) -> p c n", n=1024)
            cst, b_cst = loc2("cst", [128, 4, 256]); krst, b_krst = loc2("krst", [128, 4, 64])
            krT = Kh

            cast_load(wq_raw, b_wq_raw, w_uq[l].rearrange("(c p) n -> p c n", p=128), ("uq", l))
            cast_load(wkv_raw, b_wkv_raw, w_ukv[l].rearrange("(c p) n -> p c n", p=128), ("ukv", l))
            wq_v = wq_raw.rearrange("p c (h e) -> p c h e", e=96)
            dve(lambda e: e.memset(wq2[:], 0.0), [], [b_wq2])
            for c in range(3):
                dve(lambda e, c=c: e.tensor_copy(out=wq2[:, c, :, 0:64:2], in_=wq_v[:, c, :, 64:96]), [b_wq_raw], [b_wq2])
                dve(lambda e, c=c: e.tensor_copy(out=wq2[:, c, :, 64:128], in_=wq_v[:, c, :, 0:64]), [b_wq_raw], [b_wq2])
            wkv_v = wkv_raw.rearrange("p c (h e) -> p c h e", e=128)
            for c in range(2):
                dve(lambda e, c=c: e.tensor_copy(out=wkv2[:, c, :, 0:64], in_=wkv_v[:, c, :, 64:128]), [b_wkv_raw], [b_wkv2])
                dve(lambda e, c=c: e.tensor_copy(out=wkv2[:, c, :, 64:128], in_=wkv_v[:, c, :, 0:64]), [b_wkv_raw], [b_wkv2])
            wcq, bwcq = load_w(w_in[l][:, C_CQ:C_CQ + 384], 384, ("in", l, C_CQ))
            wck, bwck = load_w(w_in[l][:, C_CKV:C_CKV + 288], 288, ("in", l, C_CKV))
            dve(lambda e: e.memset(wkr2[:], 0.0), [], [b_wkr2])
            dve(lambda e: e.tensor_copy(out=wkr2[:, :, 0:64:2], in_=wck[:, :, 256:288]), [bwck], [b_wkr2])
            wmg, bwmg = load_w(w_in[l][:, C_MG:C_MG + 512], 512, ("in", l, C_MG))
            for i in range(2):
                dve(lambda e, i=i: e.memset(Vt[i][:], 1.0), [], [b_Vt[i]])

            def koff(s, t):
                return s * Tk + t
            if ctx:
                P.dma("sp", cst[:], ckv_c[l].rearrange("(j p) d -> p j d", p=128), writes=[b_cst])
                dve(lambda e: e.memset(krst[:], 0.0), [], [b_krst])
                krraw, b_krraw = loc2("krraw", [128, 4, 32])
                P.dma("sp", krraw[:], kr_c[l].rearrange("(j p) d -> p j d", p=128), writes=[b_krraw])
                dve(lambda e: e.tensor_copy(out=krst[:, :, 0:64:2], in_=krraw[:]), [b_krraw, b_krst], [b_krst])
                for j2 in range(2):
                    def tr(pe, j2=j2):
                        ins = None
                        for j in range(4):
                            ins = pe.transpose(ps[j2][:, j * 128:(j + 1) * 128], cst[:, j, j2 * 128:(j2 + 1) * 128], ident[:])
                        return ins
                    P.op("pe", tr, reads=[b_cst, b_ident], writes=[b_ps[j2]])
                    act(ckvn[:, j2, 0:512], ps[j2][:], AF.Identity, [b_ps[j2]], [b_ckvn])

                def tr2(pe):
                    ins = None
                    for j in range(4):
                        ins = pe.transpose(ps[2][0:64, j * 128:(j + 1) * 128], krst[:, j, :], ident[:])
                    return ins
                P.op("pe", tr2, reads=[b_krst, b_ident], writes=[b_ps[2]])
                for i in range(2):
                    act(Kh[i][0:64, 0:512], ps[2][0:64, :], AF.Identity, [b_ps[2]], [b_Kh[i]])

            for g in range(G):
                gs = slice(g * 512, (g + 1) * 512)
                for j in range(3):
                    mm(ps[j][:], [(wcq[:, k, j * 128:(j + 1) * 128], hT[:, k, gs]) for k in range(8)], [bwcq, hTr[g]], [b_ps[j]])
                    act(sq[:, j, :], ps[j][:], AF.Square, [b_ps[j]], [b_sq])
                mm(ps[3][:], [(ones_b[:], sq[:, j, :]) for j in range(3)], [b_ones, b_sq], [b_ps[3]])
                act(rs[:], ps[3][:], AF.Ln, [b_ps[3]], [b_rs], scale=1.0 / 384.0, bias=EPS)
                act(rs[:], rs[:], AF.Exp, [b_rs], [b_rs], scale=-0.5)
                for j in range(3):
                    dve(lambda e, j=j, gs=gs: e.scalar_tensor_tensor(out=cqn[:, j, gs], in0=ps[j][:], scalar=qn[:, j:j + 1], in1=rs[:], op0=ALU.mult, op1=ALU.mult),
                        [b_ps[j], b_par, b_rs], [b_cqn])
                for j in range(2):
                    mm(ps[4 + j][:], [(wck[:, k, j * 128:(j + 1) * 128], hT[:, k, gs]) for k in range(8)], [bwck, hTr[g]], [b_ps[4 + j]])
                    act(sq[:, j, :], ps[4 + j][:], AF.Square, [b_ps[4 + j]], [b_sq])
                mm(ps[6][:], [(ones_b[:], sq[:, j, :]) for j in range(2)], [b_ones, b_sq], [b_ps[6]])
                act(rs[:], ps[6][:], AF.Ln, [b_ps[6]], [b_rs], scale=1.0 / 256.0, bias=EPS)
                act(rs[:], rs[:], AF.Exp, [b_rs], [b_rs], scale=-0.5)
                if nseq == 1:
                    kdst = [(slice(ctx + g * 512, ctx + (g + 1) * 512), slice(0, 512))]
                else:
                    kdst = [(slice((2 * g + h2) * Tk, (2 * g + h2 + 1) * Tk), slice(h2 * 256, (h2 + 1) * 256)) for h2 in range(2)]
                for j in range(2):
                    for (kd_, sd_) in kdst:
                        dve(lambda e, j=j, kd_=kd_, sd_=sd_: e.scalar_tensor_tensor(out=ckvn[:, j, kd_], in0=ps[4 + j][:, sd_], scalar=kvn[:, j:j + 1], in1=rs[:, sd_], op0=ALU.mult, op1=ALU.mult),
                            [b_ps[4 + j], b_par, b_rs], [b_ckvn])
                mm(ps[7][0:64, :], [(wkr2[:, k, :], hT[:, k, gs]) for k in range(8)], [b_wkr2, hTr[g]], [b_ps[7]])
                for (kd_, sd_) in kdst:
                    if U["rope"]:
                        rope_evac(Kh[0][0:64, kd_], ps[7][0:64, :], 3, g, slice(0, 64), [b_ps[7]], [b_Kh[0]], xr, b_xr, t1, b_t1)
                        act(Kh[1][0:64, kd_], Kh[0][0:64, kd_], AF.Identity, [b_Kh[0]], [b_Kh[1]])
                    else:
                        for i in range(2):
                            act(Kh[i][0:64, kd_], ps[7][0:64, sd_], AF.Identity, [b_ps[7]], [b_Kh[i]])
                for j in range(4):
                    bank = j % 3
                    mm(ps[bank][:], [(wmg[:, k, j * 128:(j + 1) * 128], hT[:, k, gs]) for k in range(8)], [bwmg, hTr[g]], [b_ps[bank]])
                    act(ym[:, j, gs], ps[bank][:], AF.Silu, [b_ps[bank]], [b_ym])

            if not ctx:
                for tt in range(T // 128):
                    ts_ = slice(tt * 128, (tt + 1) * 128)
                    bank = 4 + tt % 2
                    s, r0 = divmod(tt * 128, Ts)
                    mm(ps[bank][:, 0:288], [(hT[:, k, ts_], wck[:, k, 0:288]) for k in range(8)], [bwck, hTr[tt // 4]], [b_ps[bank]])
                    act(ost[:, 0:256], ps[bank][:, 0:256], AF.Square, [b_ps[bank]], [b_ost, b_ssv], accum_out=ssv[:, 0:1])
                    act(ost[:, 256:288], ps[bank][:, 256:288], AF.Identity, [b_ps[bank]], [b_ost])
                    act(ssv[:, 0:1], ssv[:, 0:1], AF.Sqrt, [b_ssv], [b_ssv], scale=1.0 / 256.0, bias=EPS)
                    dve(lambda e: e.reciprocal(out=ssv[:, 0:1], in_=ssv[:, 0:1]), [b_ssv], [b_ssv])
                    dve(lambda e, bank=bank: e.scalar_tensor_tensor(out=ost[:, 0:256], in0=ps[bank][:, 0:256], scalar=ssv[:, 0:1], in1=kvn_bc[:], op0=ALU.mult, op1=ALU.mult),
                        [b_ps[bank], b_ssv, b_par], [b_ost])
                    P.dma("sp", o_ckv[s, l, r0:r0 + 128, :], ost[:, 0:256], reads=[b_ost])
                    P.dma("sp", o_kr[s, l, r0:r0 + 128, :], ost[:, 256:288], reads=[b_ost])

            NQ = 512 if nseq == 1 else Ts
            nq_groups = T // NQ
            PT = [PT2a, PT2b]
            b_PT = [b_PT2a, b_PT2b]

            def prep(h):
                hi = h % 2
                if hi == 0:
                    vi = (h // 2) % 2
                    for kt in range(nseq * KT):
                        bank = 6 + kt % 2
                        ks = slice(kt * 128, (kt + 1) * 128)
                        pv = ps[bank][:, 0:128].rearrange("p (a e) -> p a e", e=64)
                        mm(pv, [(ckvn[:, r, ks], wkv2[:, r, h:h + 2, 0:64]) for r in range(2)], [b_ckvn, b_wkv2], [b_ps[bank]])
                        dve(lambda e, vi=vi, kt=kt, pv=pv: e.tensor_copy(out=Vt[vi][:, kt, 0, 0:64], in_=pv[:, 0, :]), [b_ps[bank]], [b_Vt[vi]])
                        dve(lambda e, vi=vi, kt=kt, pv=pv: e.tensor_copy(out=Vt[vi][:, kt, 1, 64:128], in_=pv[:, 1, :]), [b_ps[bank]], [b_Vt[vi]])
                        yield
                for kg in range(TkA // 512):
                    ks = slice(kg * 512, (kg + 1) * 512)
                    bank = 6 + kg % 2
                    mm(ps[bank][:], [(wkv2[:, r, h, :], ckvn[:, r, ks]) for r in range(2)], [b_wkv2, b_ckvn], [b_ps[bank]])
                    dve(lambda e, ks=ks, bank=bank, hi=hi: e.tensor_copy(out=Kh[hi][64:128, ks], in_=ps[bank][64:128, :]), [b_ps[bank]], [b_Kh[hi]])
                    yield
                for g in range(G):
                    gs = slice(g * 512, (g + 1) * 512)
                    bank = 6 + g % 2
                    mm(ps[bank][:], [(wq2[:, c, h, :], cqn[:, c, gs]) for c in range(3)], [b_wq2, b_cqn], [b_ps[bank]])
                    dve(lambda e, gs=gs, bank=bank, hi=hi: e.tensor_copy(out=Qh[hi][64:128, gs], in_=ps[bank][64:128, :]), [b_ps[bank]], [b_Qh[hi]])
                    if U["rope"]:
                        rows = slice(0, 64)
                        xq, b_xq = (xr, b_xr) if g % 2 == 0 else (xr2, b_xr2)
                        tq, b_tq = (t1, b_t1) if g % 2 == 0 else (t12, b_t12)
                        dve(lambda e, xq=xq, bank=bank: e.tensor_copy(out=xq[rows, :], in_=ps[bank][0:64, :]), [b_ps[bank]], [b_xq])
                        yield
                        rb = 7 - g % 2
                        mm(ps[rb][rows, :], [(rdT[rows, rows], xq[rows, :])], [b_xq, b_rdT], [b_ps[rb]])
                        yield
                        dve(lambda e, tq=tq, rb=rb, gs=gs: e.tensor_tensor(out=tq[rows, :], in0=ps[rb][rows, :], in1=sind[rows, gs], op=ALU.mult), [b_ps[rb], b_tab], [b_tq])
                        dve(lambda e, xq=xq, gs=gs: e.tensor_tensor(out=xq[rows, :], in0=xq[rows, :], in1=cosd[rows, gs], op=ALU.mult), [b_xq, b_tab], [b_xq])
                        dve(lambda e, xq=xq, tq=tq, gs=gs, hi=hi: e.tensor_tensor(out=Qh[hi][0:64, gs], in0=xq[rows, :], in1=tq[rows, :], op=ALU.add), [b_xq, b_tq], [b_Qh[hi]])
                    else:
                        dve(lambda e, gs=gs, bank=bank, hi=hi: e.tensor_copy(out=Qh[hi][0:64, gs], in_=ps[bank][0:64, :]), [b_ps[bank]], [b_Qh[hi]])
                    yield

            def n_prep_units(h):
                return (nseq * KT if h % 2 == 0 else 0) + TkA // 512 + G * (3 if U["rope"] else 1)

            def attention(h, gen=None, n_units=0):
                hi = h % 2
                vt = Vt[(h // 2) % 2]
                b_vt = b_Vt[(h // 2) % 2]
                steps = []
                for qg in range(nq_groups):
                    if nseq == 1:
                        for kp in range(KT // 2):
                            steps.append((qg, 0, (2 * kp, 2 * kp + 1), kp == 0, kp == KT // 2 - 1))
                    else:
                        steps.append((qg, qg, (0, 1), True, True))

                def qk(i):
                    qg, s, kts, first, last = steps[i]
                    pair = (i % 2) * 2
                    qs = slice(qg * NQ, (qg + 1) * NQ)
                    for t, kt in enumerate(kts):
                        kk_ = s * KT + kt
                        ks = slice(kk_ * 128, (kk_ + 1) * 128)
                        if NQ == 512:
                            mm(ps[pair + t][:], [(Kh[hi][:, ks], Qh[hi][:, qs])], [b_Kh[hi], b_Qh[hi]], [b_ps[pair + t]])
                        else:
                            mm(ps[pair][:, t * NQ:(t + 1) * NQ], [(Kh[hi][:, ks], Qh[hi][:, qs])], [b_Kh[hi], b_Qh[hi]], [b_ps[pair]])

                def expv(i):
                    qg, s, kts, first, last = steps[i]
                    pair = (i % 2) * 2
                    pt = PT[i % 2]
                    b_pt = b_PT[i % 2]
                    ob = 4 + qg % 2
                    qs = slice(qg * NQ, (qg + 1) * NQ)
                    if NQ == 512:
                        act(pt[:, 0:1024], pall[:, pair * 512:(pair + 2) * 512], AF.Exp, [b_ps[pair], b_ps[pair + 1]], [b_pt], scale=MLA_SCALE)
                    else:
                        act(pt[:, 0:512], ps[pair][:], AF.Exp, [b_ps[pair]], [b_pt], scale=MLA_SCALE)
                    for t, kt in enumerate(kts):
                        kk_ = s * KT + kt
                        mm(ps[ob][:, 0:NQ], [(vt[:, kk_, hi, :], pt[:, t * NQ:(t + 1) * NQ])], [b_vt, b_pt], [b_ps[ob]],
                           start=(first and t == 0), stop=(last and t == len(kts) - 1))
                    if last:
                        dr = slice(0, 64) if hi == 0 else slice(64, 128)
                        sr = slice(64, 128) if hi == 0 else slice(0, 64)
                        act(rc[dr, 0:NQ], ps[ob][sr, 0:NQ], AF.Ln, [b_ps[ob]], [b_rc])
                        act(rc[dr, 0:NQ], rc[dr, 0:NQ], AF.Exp, [b_rc], [b_rc], scale=-1.0)
                        dve(lambda e: e.tensor_tensor(out=ot[dr, 0:NQ], in0=ps[ob][dr, 0:NQ], in1=rc[dr, 0:NQ], op=ALU.mult), [b_ps[ob], b_rc], [b_ot])
                        dve(lambda e: e.tensor_tensor(out=ym[dr, h // 2, qs], in0=ot[dr, 0:NQ], in1=ym[dr, h // 2, qs], op=ALU.mult), [b_ot, b_ym], [b_ym])

                per_step = -(-n_units // max(1, len(steps) - 1)) if gen is not None else 0
                qk(0)
                for i in range(len(steps)):
                    if i + 1 < len(steps):
                        qk(i + 1)
                    if gen is not None:
                        for _ in range(per_step):
                            next(gen, None)
                    expv(i)
                if gen is not None:
                    for _ in gen:
                        pass

            for _ in prep(0):
                pass
            for h in range(8):
                if h + 1 < 8:
                    attention(h, prep(h + 1), n_prep_units(h + 1))
                else:
                    attention(h)
            P.barrier(scratch1[:, 0:1])

    def phase_diff(l, U, yd, b_yd):
        T, nseq, Ts, ctx = U["T"], U["nseq"], U["Ts"], U["ctx"]
        G = T // 512
        Tk = Ts + ctx
        KT = Tk // 128
        TkA = nseq * Tk
        hTr = b_hT[:G]
        with ExitStack() as es:
            def loc(name, shape, dt=F32):
                return es.enter_context(nc.sbuf_tensor(un(name), list(shape), dt)), Buf(name, local=True)
            qd, b_qd = loc("qd", [128, 4, T], BF16)
            kd, b_kd = loc("kd", [128, 4, TkA], BF16)
            vd, b_vd = loc("vd", [128, nseq * KT, 512], BF16)
            PT2a, b_PT2a = loc("dPT2a", [128, 1024], BF16); PT2b, b_PT2b = loc("dPT2b", [128, 1024], BF16)
            xr, b_xr = loc("dxr", [128, 512]); t1, b_t1 = loc("dt1", [128, 512])
            r1, b_r1 = loc("r1", [128, 512]); r2, b_r2 = loc("r2", [128, 512])
            od, b_od = loc("od", [128, 512]); ou, b_ou = loc("ou", [128, 512])
            sqd, b_sqd = loc("sqd", [128, 512], BF16)
            ost = []; b_ost = []
            for i in range(2):
                t, b = loc("dost%d" % i, [128, 512]); ost.append(t); b_ost.append(b)

            def koff_list(g):
                if nseq == 1:
                    return [(slice(ctx + g * 512, ctx + (g + 1) * 512), slice(0, 512))]
                return [(slice((2 * g + h2) * Tk, (2 * g + h2 + 1) * Tk), slice(h2 * 256, (h2 + 1) * 256)) for h2 in range(2)]

            if ctx:
                cst = [xr, t1, r1, r2]
                b_cst = [b_xr, b_t1, b_r1, b_r2]
                for j in range(4):
                    P.dma("sp", cst[j][:], dk_c[l][j * 128:(j + 1) * 128, :], writes=[b_cst[j]])
                for j2 in range(4):
                    def tr(pe, j2=j2):
                        ins = None
                        for j in range(4):
                            ins = pe.transpose(ps[j2][:, j * 128:(j + 1) * 128], cst[j][:, j2 * 128:(j2 + 1) * 128], ident[:])
                        return ins
                    P.op("pe", tr, reads=b_cst + [b_ident], writes=[b_ps[j2]])
                    act(kd[:, j2, 0:512], ps[j2][:], AF.Identity, [b_ps[j2]], [b_kd])
                P.dma("pool", vd[:, 0:4, :], dv_c[l].rearrange("(j p) d -> p j d", p=128), writes=[b_vd])

            for (col0, dst, b_dst, is_k) in ((C_DQ, qd, b_qd, False), (C_DK, kd, b_kd, True)):
                wt, bw = load_w(w_in[l][:, col0:col0 + 512], 512, ("in", l, col0))
                for g in range(G):
                    gs = slice(g * 512, (g + 1) * 512)
                    for j in range(4):
                        bank = j % 3
                        mm(ps[bank][:], [(wt[:, k, j * 128:(j + 1) * 128], hT[:, k, gs]) for k in range(8)], [bw, hTr[g]], [b_ps[bank]])
                        dsts = koff_list(g) if is_k else [(gs, slice(0, 512))]
                        if U["rope"]:
                            if j % 2 == 0:
                                rope_evac(dst[:, j, dsts[0][0]], ps[bank][:], 3 + (j % 2), g, slice(0, 128), [b_ps[bank]], [b_dst], xr, b_xr, t1, b_t1)
                            else:
                                rope_evac(dst[:, j, dsts[0][0]], ps[bank][:], 3 + (j % 2), g, slice(0, 128), [b_ps[bank]], [b_dst], r1, b_r1, r2, b_r2)
                        else:
                            for (kd_, sd_) in dsts:
                                act(dst[:, j, kd_], ps[bank][:, sd_], AF.Identity, [b_ps[bank]], [b_dst])
                if is_k and not ctx:
                    for tt in range(T // 128):
                        ts_ = slice(tt * 128, (tt + 1) * 128)
                        bank = 5 + tt % 2
                        s, r0 = divmod(tt * 128, Ts)
                        mm(ps[bank][:], [(hT[:, k, ts_], wt[:, k, :]) for k in range(8)], [bw, hTr[tt // 4]], [b_ps[bank]])
                        act(ost[tt % 2][:], ps[bank][:], AF.Identity, [b_ps[bank]], [b_ost[tt % 2]])
                        P.dma("sp", o_dk[s, l, r0:r0 + 128, :], ost[tt % 2][:], reads=[b_ost[tt % 2]])
            wt, bw = load_w(w_in[l][:, C_DV:C_DV + 512], 512, ("in", l, C_DV))
            for tt in range(T // 128):
                ts_ = slice(tt * 128, (tt + 1) * 128)
                bank = 5 + tt % 2
                s, r0 = divmod(tt * 128, Ts)
                kt = s * KT + (ctx + r0) // 128
                mm(ps[bank][:], [(hT[:, k, ts_], wt[:, k, :]) for k in range(8)], [bw, hTr[tt // 4]], [b_ps[bank]])
                if ctx:
                    dve(lambda e, kt=kt, bank=bank: e.tensor_copy(out=vd[:, kt, :], in_=ps[bank][:]), [b_ps[bank]], [b_vd])
                else:
                    act(ost[tt % 2][:], ps[bank][:], AF.Identity, [b_ps[bank]], [b_ost[tt % 2]])
                    dve(lambda e, kt=kt, tt=tt: e.tensor_copy(out=vd[:, kt, :], in_=ost[tt % 2][:]), [b_ost[tt % 2]], [b_vd])
                    P.dma("sp", o_dv[s, l, r0:r0 + 128, :], ost[tt % 2][:], reads=[b_ost[tt % 2]])
            wt, bw = load_w(w_in[l][:, C_DG:C_DG + 512], 512, ("in", l, C_DG))
            for g in range(G):
                gs = slice(g * 512, (g + 1) * 512)
                for j in range(4):
                    bank = j % 3
                    mm(ps[bank][:], [(wt[:, k, j * 128:(j + 1) * 128], hT[:, k, gs]) for k in range(8)], [bw, hTr[g]], [b_ps[bank]])
                    act(yd[:, j, gs], ps[bank][:], AF.Silu, [b_ps[bank]], [b_yd])

            NQ = 512 if nseq == 1 else Ts
            PT = [PT2a, PT2b]
            b_PT = [b_PT2a, b_PT2b]
            for j in range(4):
                steps = []
                for qg in range(T // NQ):
                    s = 0 if nseq == 1 else qg
                    for kt in range(KT):
                        steps.append((qg, s, kt))

                def qk(i, j=j, steps=steps):
                    qg, s, kt = steps[i]
                    pair = (i % 2) * 2
                    qs = slice(qg * NQ, (qg + 1) * NQ)
                    kk_ = s * KT + kt
                    ks = slice(kk_ * 128, (kk_ + 1) * 128)
                    for cp in range(2):
                        rows = slice(cp * 64, (cp + 1) * 64)
                        if NQ == 512:
                            mm(ps[pair + cp][:], [(kd[rows, j, ks], qd[rows, j, qs])], [b_kd, b_qd], [b_ps[pair + cp]])
                        else:
                            mm(ps[pair + cp][:, 0:NQ], [(kd[rows, j, ks], qd[rows, j, qs])], [b_kd, b_qd], [b_ps[pair + cp]])

                def expv(i, j=j, steps=steps):
                    qg, s, kt = steps[i]
                    pair = (i % 2) * 2
                    pt = PT[i % 2]
                    b_pt = b_PT[i % 2]
                    qs = slice(qg * NQ, (qg + 1) * NQ)
                    kk_ = s * KT + kt
                    if NQ == 512:
                        act(pt[:, 0:1024], pall[:, pair * 512:(pair + 2) * 512], AF.Exp, [b_ps[pair], b_ps[pair + 1]], [b_pt], scale=DIFF_SCALE)
                    else:
                        act(pt[:, 0:2 * NQ].rearrange("p (b c) -> p b c", c=NQ),
                            pall[:, pair * 512:(pair + 2) * 512].rearrange("p (b c) -> p b c", c=512)[:, :, 0:NQ],
                            AF.Exp, [b_ps[pair], b_ps[pair + 1]], [b_pt], scale=DIFF_SCALE)
                    for cp in range(2):
                        mm(ps[4 + cp][:, 0:NQ], [(vd[:, kk_, j * 128:(j + 1) * 128], pt[:, cp * NQ:(cp + 1) * NQ])], [b_vd, b_pt], [b_ps[4 + cp]],
                           start=(kt == 0), stop=(kt == KT - 1))
                        mm(ps[6 + cp][:, 0:NQ], [(ones_b[:], pt[:, cp * NQ:(cp + 1) * NQ])], [b_ones, b_pt], [b_ps[6 + cp]],
                           start=(kt == 0), stop=(kt == KT - 1))
                    if kt == KT - 1:
                        act(r1[:, 0:NQ], ps[6][:, 0:NQ], AF.Ln, [b_ps[6]], [b_r1])
                        act(r2[:, 0:NQ], ps[7][:, 0:NQ], AF.Ln, [b_ps[7]], [b_r2])
                        act(r1[:, 0:NQ], r1[:, 0:NQ], AF.Exp, [b_r1], [b_r1], scale=-1.0)
                        act(r2[:, 0:NQ], r2[:, 0:NQ], AF.Exp, [b_r2], [b_r2], scale=-1.0)
                        dve(lambda e: e.tensor_tensor(out=ou[:, 0:NQ], in0=ps[4][:, 0:NQ], in1=r1[:, 0:NQ], op=ALU.mult), [b_ps[4], b_r1], [b_ou])
                        dve(lambda e: e.scalar_tensor_tensor(out=od[:, 0:NQ], in0=ps[5][:, 0:NQ], scalar=neglam[:, 0:1], in1=r2[:, 0:NQ], op0=ALU.mult, op1=ALU.mult),
                            [b_ps[5], b_r2, b_par], [b_od])
                        dve(lambda e: e.tensor_tensor(out=od[:, 0:NQ], in0=od[:, 0:NQ], in1=ou[:, 0:NQ], op=ALU.add), [b_od, b_ou], [b_od])
                        dve(lambda e: e.tensor_tensor(out=sqd[:, 0:NQ], in0=od[:, 0:NQ], in1=od[:, 0:NQ], op=ALU.mult), [b_od], [b_sqd])

                        def part2(nb, j=j, qs=qs):
                            mm(ps[nb][:, 0:NQ], [(ones_b[:], sqd[:, 0:NQ])], [b_ones, b_sqd], [b_ps[nb]])
                            act(ou[:, 0:NQ], ps[nb][:, 0:NQ], AF.Ln, [b_ps[nb]], [b_ou], scale=1.0 / 128.0, bias=EPS)
                            act(ou[:, 0:NQ], ou[:, 0:NQ], AF.Exp, [b_ou], [b_ou], scale=-0.5)
                            dve(lambda e: e.scalar_tensor_tensor(out=od[:, 0:NQ], in0=od[:, 0:NQ], scalar=dn[:, 0:1], in1=ou[:, 0:NQ], op0=ALU.mult, op1=ALU.mult),
                                [b_od, b_ou, b_par], [b_od])
                            dve(lambda e: e.tensor_tensor(out=yd[:, j, qs], in0=od[:, 0:NQ], in1=yd[:, j, qs], op=ALU.mult), [b_od, b_yd], [b_yd])
                        pend.append((i + min(2, KT - 1), part2))

                pend = []
                qk(0)
                for i in range(len(steps)):
                    while pend and pend[0][0] <= i:
                        pend.pop(0)[1](((i + 1) % 2) * 2)
                    if i + 1 < len(steps):
                        qk(i + 1)
                    expv(i)
                while pend:
                    pend.pop(0)[1](0)
            P.barrier(scratch1[:, 0:1])

    def phase_merge_out(l, U, ybr, b_ybr):
        T = U["T"]
        G = T // 512
        u = U["u"]
        hTr = b_hT[:G]
        xsrc = U["x"][l]
        xdst = U["x"][l + 1]
        with ExitStack() as es:
            def loc(name, shape, dt=F32):
                return es.enter_context(nc.sbuf_tensor(un(name), list(shape), dt)), Buf(name, local=True)
            mg, b_mg = loc("merged", [128, 8, T], BF16)
            esA = ExitStack()

            def locA(name, shape, dt=F32):
                return esA.enter_context(nc.sbuf_tensor(un(name), list(shape), dt)), Buf(name, local=True)
            loc_save = loc
            loc = locA
            wbr2 = []; b_wbr2 = []
            for i in range(2):
                t, b = loc("wbr2_%d" % i, [128, 3, 4, 256], BF16); wbr2.append(t); b_wbr2.append(b)
            gsb = []; b_gsb = []
            for i in range(4):
                t, b = loc("gsb%d" % i, [128, 512]); gsb.append(t); b_gsb.append(b)
            mt = []; b_mt = []
            for i in range(4):
                t, b = loc("mt%d" % i, [128, 512]); mt.append(t); b_mt.append(b)
            it = 0
            hslot = {"n": 0, "cur": None}

            def half_slot():
                if hslot["n"] % 2 == 0:
                    hslot["cur"] = ring_slot()
                t, b = hslot["cur"]
                v = t[:].rearrange("p c n -> p (c n)")[:, (hslot["n"] % 2) * 2048:(hslot["n"] % 2 + 1) * 2048].rearrange("p (c n) -> p c n", n=256)
                hslot["n"] += 1
                return v, b

            def load_group(f2):
                wi = f2 % 2
                wm = []
                for b in range(3):
                    c0 = C_MGATE + b * D + f2 * 256
                    v, bb = half_slot()
                    cast_load(v, bb, w_in[l][:, c0:c0 + 256].rearrange("(c p) n -> p c n", p=128), ("mg2", l, b, f2))
                    wm.append((v, bb))
                    cast_load(wbr2[wi][:, b, :, :], b_wbr2[wi], w_br[b][l][:, f2 * 256:(f2 + 1) * 256].rearrange("(c p) n -> p c n", p=128), ("br2", l, b, f2))
                return wm

            nxt = load_group(0)
            for f2 in range(4):
                wm2 = nxt
                wi = f2 % 2
                if f2 + 1 < 4:
                    nxt = load_group(f2 + 1)
                for g in range(G):
                    gs = slice(g * 512, (g + 1) * 512)
                    for fi in range(2):
                        f = f2 * 2 + fi
                        fs = slice(fi * 128, (fi + 1) * 128)
                        ma = (f * G + g) % 2
                        for b in range(3):
                            r = it % 4
                            it += 1
                            bA, bB = 2 * r, 2 * r + 1
                            mm(ps[bA][:], [(wm2[b][0][:, k, fs], hT[:, k, gs]) for k in range(8)], [wm2[b][1], hTr[g]], [b_ps[bA]])
                            act(gsb[r][:], ps[bA][:], AF.Sigmoid, [b_ps[bA]], [b_gsb[r]])
                            mm(ps[bB][:], [(wbr2[wi][:, b, k, fs], ybr[b][:, k, gs]) for k in range(4)], [b_wbr2[wi], b_ybr[b]], [b_ps[bB]])
                            if b == 0:
                                dve(lambda e, r=r, bB=bB, ma=ma: e.tensor_tensor(out=mt[ma][:], in0=ps[bB][:], in1=gsb[r][:], op=ALU.mult), [b_ps[bB], b_gsb[r]], [b_mt[ma]])
                            else:
                                dve(lambda e, r=r, bB=bB, ma=ma: e.tensor_tensor(out=mt[2 + ma][:], in0=ps[bB][:], in1=gsb[r][:], op=ALU.mult), [b_ps[bB], b_gsb[r]], [b_mt[2 + ma]])
                                if b == 1:
                                    dve(lambda e, ma=ma: e.tensor_tensor(out=mt[ma][:], in0=mt[ma][:], in1=mt[2 + ma][:], op=ALU.add), [b_mt[ma], b_mt[2 + ma]], [b_mt[ma]])
                                else:
                                    dve(lambda e, ma=ma, f=f, gs=gs: e.tensor_tensor(out=mg[:, f, gs], in0=mt[ma][:], in1=mt[2 + ma][:], op=ALU.add), [b_mt[ma], b_mt[2 + ma]], [b_mg])
            P.barrier(scratch1[:, 0:1])
            esA.close()
            loc = loc_save
            wo = []
            for half in range(2):
                wo.append(load_w(w_out[l][:, half * 512:(half + 1) * 512], 512, ("out", l, half)))
            xt = []; b_xt = []; ot = []; b_ot = []
            for i in range(2):
                t, b = loc("oxt%d" % i, [128, D]); xt.append(t); b_xt.append(b)
                t, b = loc("oot%d" % i, [128, D]); ot.append(t); b_ot.append(b)
            junk, b_junk = loc("ojunk", [128, 512]); ss2, b_ss2 = loc("oss2", [128, 4])
            for tt in range(T // 128):
                i = tt % 2
                ts_ = slice(tt * 128, (tt + 1) * 128)
                rdx = [b_x1[(U["name"], tt)]] if l > 0 else []
                P.dma("sp", xt[i][:], xsrc[ts_, :], reads=rdx, writes=[b_xt[i]])
                for half in range(2):
                    bank = 6 + half
                    mm(ps[bank][:], [(mg[:, k, ts_], wo[half][0][:, k, :]) for k in range(8)], [wo[half][1], b_mg], [b_ps[bank]])
                    act(junk[:], ps[bank][:], AF.Square, [b_ps[bank]], [b_junk, b_ss2], accum_out=ss2[:, half:half + 1])
                dve(lambda e: e.tensor_tensor(out=ss2[:, 2:3], in0=ss2[:, 0:1], in1=ss2[:, 1:2], op=ALU.add), [b_ss2], [b_ss2])
                act(ss2[:, 3:4], ss2[:, 2:3], AF.Sqrt, [b_ss2], [b_ss2], scale=1.0 / D, bias=EPS)
                dve(lambda e: e.reciprocal(out=ss2[:, 3:4], in_=ss2[:, 3:4]), [b_ss2], [b_ss2])
                for half in range(2):
                    bank = 6 + half
                    hs = slice(half * 512, (half + 1) * 512)
                    dve(lambda e, i=i, bank=bank, hs=hs: e.scalar_tensor_tensor(out=ot[i][:, hs], in0=ps[bank][:], scalar=ss2[:, 3:4], in1=gg[u][:, hs], op0=ALU.mult, op1=ALU.mult),
                        [b_ps[bank], b_ss2, b_par], [b_ot[i]])
                dve(lambda e, i=i: e.tensor_tensor(out=ot[i][:], in0=ot[i][:], in1=xt[i][:], op=ALU.add), [b_ot[i], b_xt[i]], [b_ot[i]])
                wr = [b_x1[(U["name"], tt)]] if l == 0 else []
                P.dma("sp", xdst[ts_, :], ot[i][:], reads=[b_ot[i]], writes=wr)
            P.barrier(scratch1[:, 0:1])

    class _Stop(Exception):
        pass

    def chk(l, U, ph):
        if stop is not None and tuple(stop[:3]) == (l, U["name"] if U else None, ph):
            raise _Stop()

    try:
        for l in range(DEPTH):
            layer_params(l)
            chk(l, None, "params")
            for U in units:
                T = U["T"]
                if stop is not None and len(stop) > 3 and U["name"] not in stop[3]:
                    continue
                phase_h(l, U)
                chk(l, U, "h")
                with ExitStack() as es:
                    ym = es.enter_context(nc.sbuf_tensor(un("ym"), [128, 4, T], BF16)); b_ym = Buf("ym", local=True)
                    phase_mla(l, U, ym, b_ym)
                    chk(l, U, "mla")
                    yd = es.enter_context(nc.sbuf_tensor(un("yd"), [128, 4, T], BF16)); b_yd = Buf("yd", local=True)
                    phase_diff(l, U, yd, b_yd)
                    chk(l, U, "diff")
                    yr = es.enter_context(nc.sbuf_tensor(un("yr"), [128, 4, T], BF16)); b_yr = Buf("yr", local=True)
                    phase_rnn(l, U, yr, b_yr)
                    chk(l, U, "rnn")
                    phase_merge_out(l, U, [yr, ym, yd], [b_yr, b_ym, b_yd])
                    chk(l, U, "merge")
    except _Stop:
        pass
    P.finish()
    return nc, P


_CACHE = {}


def make_in_maps(inp, n=8):
    f32 = lambda a: np.ascontiguousarray(np.asarray(a, dtype=np.float32))
    cos, sin, rdT, ident = rope_consts()
    shared = {k: f32(inp[k]) for k in (
        "w_mod", "b_mod", "g_pre", "g_post", "w_in", "conv_w", "conv_b", "w_rg_a", "b_rg_a", "w_rg_x", "b_rg_x", "rg_lam",
        "q_norm", "w_uq", "kv_norm", "w_ukv", "lam_q1", "lam_k1", "lam_q2", "lam_k2", "diff_norm",
        "w_br_rnn", "w_br_mla", "w_br_diff", "w_out")}
    shared.update({"cosd": cos, "sind": sin, "rdT": rdT, "ident": ident})
    x_prompt = f32(inp["x_prompt"]); x_sample = f32(inp["x_sample"])
    c = f32(inp["c"]); c_ctx = f32(inp["c_ctx"])
    in_maps = []
    for i in range(n):
        m = dict(shared)
        m["xs"] = x_sample[i]
        m["xp"] = x_prompt[NPB * i:NPB * (i + 1)].reshape(TP, D)
        m["ckv_c"] = f32(inp["cache_mla_ckv"][i])
        m["kr_c"] = f32(inp["cache_mla_krope"][i])
        m["dk_c"] = f32(inp["cache_diff_k"][i]).reshape(DEPTH, PAST, 512)
        m["dv_c"] = f32(inp["cache_diff_v"][i]).reshape(DEPTH, PAST, 512)
        m["st_c"] = f32(inp["state_rnn"][i])
        m["cvec"] = np.stack([c[i], c_ctx], axis=0)
        in_maps.append(m)
    return in_maps


def kernel(**inp):
    n = 8
    if "nc" not in _CACHE:
        _CACHE["nc"] = build_program()[0]
    nc = _CACHE["nc"]
    in_maps = make_in_maps(inp, n)
    res = run_bass_kernel_spmd(nc, in_maps, core_ids=list(range(n)))
    R = res.results
    y_prompt = np.concatenate([R[i]["yp"].reshape(NPB, SP, D) for i in range(n)], axis=0)
    y_sample = np.stack([R[i]["ys"] for i in range(n)], axis=0)
    new_ckv = np.concatenate([R[i]["o_ckv"] for i in range(n)], axis=0)
    new_kr = np.concatenate([R[i]["o_kr"] for i in range(n)], axis=0)
    new_dk = np.concatenate([R[i]["o_dk"].reshape(NPB, DEPTH, SP, 4, 128) for i in range(n)], axis=0)
    new_dv = np.concatenate([R[i]["o_dv"].reshape(NPB, DEPTH, SP, 4, 128) for i in range(n)], axis=0)
    new_st = np.concatenate([R[i]["o_st"] for i in range(n)], axis=0)
    return (y_prompt.astype(np.float32), y_sample.astype(np.float32), new_ckv.astype(np.float32), new_kr.astype(np.float32),
            new_dk.astype(np.float32), new_dv.astype(np.float32), new_st.astype(np.float32))
```
